# Optimizing a Trainium2 kernel written in Bass

```python
import jax
import jax.numpy as jnp
from jax import lax
import numpy as np

D_MODEL = 1024
BATCH = 1
SEQ = 16384
DEPTH = 2

HEAD_DIM = 64
ROPE_THETA = 10000.0
NORM_EPS = 1e-6
Q_BLOCK = 128
NEG_INF = -1e30
FORCE_SELECT = 1e9

NSA_HEADS = 8
NSA_KV_GROUPS = 2
NSA_HEADS_PER_GROUP = NSA_HEADS // NSA_KV_GROUPS
NSA_CMP_LEN = 32
NSA_CMP_STRIDE = 16
NSA_SEL_LEN = 64
NSA_SEL_TOPK = 16
NSA_WINDOW = 512
NSA_BRANCHES = 3

MOBA_HEADS = 8
MOBA_BLOCK = 256
MOBA_TOPK = 3

MEM_LEN = 256
MEM_HEADS = 4

RET_HEADS = 4
RET_QK_DIM = D_MODEL // RET_HEADS
RET_V_DIM = 2 * D_MODEL // RET_HEADS
RET_CHUNK = 128

NSA_Q_W = NSA_HEADS * HEAD_DIM
NSA_KV_W = NSA_KV_GROUPS * HEAD_DIM
MOBA_W = MOBA_HEADS * HEAD_DIM
MEM_W = MEM_HEADS * HEAD_DIM
RET_QK_W = RET_HEADS * RET_QK_DIM
RET_V_W = RET_HEADS * RET_V_DIM

EVEN_SPLITS = (NSA_Q_W, NSA_KV_W, NSA_KV_W, NSA_KV_W, NSA_KV_W, NSA_KV_W, NSA_KV_W,
               NSA_BRANCHES * NSA_HEADS, NSA_Q_W,
               MOBA_W, MOBA_W, MOBA_W, MOBA_W,
               MEM_W, MEM_W)
ODD_SPLITS = (RET_QK_W, RET_QK_W, RET_V_W, RET_V_W, MEM_W, MEM_W)
EVEN_IN_W = sum(EVEN_SPLITS)
EVEN_OUT_W = NSA_Q_W + MOBA_W + MEM_W
ODD_IN_W = sum(ODD_SPLITS)
ODD_OUT_W = RET_V_W + MEM_W

kernel_name = 'hybrid_nsa_moba_retention_memory'


def _split(h, sizes):
    cuts = np.cumsum(np.array(sizes))[:-1].tolist()
    return jnp.split(h, cuts, axis=-1)


def rms_norm(x, g):
    xf = x.astype(jnp.float32)
    y = xf * lax.rsqrt(jnp.mean(xf * xf, axis=-1, keepdims=True) + NORM_EPS)
    return (y * g.astype(jnp.float32)).astype(x.dtype)


def rotary_tables(positions, inv_freq):
    ang = positions.astype(jnp.float32)[..., None] * inv_freq
    return jnp.cos(ang), jnp.sin(ang)


def apply_rotary(x, cos, sin):
    x1, x2 = jnp.split(x, 2, axis=-1)
    c = cos[:, :, None, :].astype(x.dtype)
    s = sin[:, :, None, :].astype(x.dtype)
    return jnp.concatenate([x1 * c - x2 * s, x1 * s + x2 * c], axis=-1)


def masked_softmax(s, mask):
    s = jnp.where(mask, s.astype(jnp.float32), NEG_INF)
    m = jnp.max(s, axis=-1, keepdims=True)
    p = jnp.where(mask, jnp.exp(s - m), 0.0)
    return p / jnp.maximum(jnp.sum(p, axis=-1, keepdims=True), 1e-30)


def nsa_compress(a, pe, w1, w2):
    T = a.shape[1]
    n_cmp = (T - NSA_CMP_LEN) // NSA_CMP_STRIDE + 1
    idx = np.arange(n_cmp)[:, None] * NSA_CMP_STRIDE + np.arange(NSA_CMP_LEN)[None, :]
    blocks = a[:, idx] + pe[None, None, :, None, :].astype(a.dtype)
    hid = jax.nn.silu(jnp.einsum('bnlgd,lde->bnge', blocks, w1))
    return jnp.einsum('bnge,ef->bngf', hid, w2)


def nsa_attention(q, kc, vc, ks, vs, kw, vw, gates):
    Bsz, T = q.shape[0], q.shape[1]
    G, I, D = NSA_KV_GROUPS, NSA_HEADS_PER_GROUP, HEAD_DIM
    n_cmp = kc.shape[1]
    n_sel = T // NSA_SEL_LEN
    k_sel = min(NSA_SEL_TOPK, n_sel)
    scale = HEAD_DIM ** -0.5
    cmp_start = jnp.arange(n_cmp) * NSA_CMP_STRIDE
    cmp_end = cmp_start + NSA_CMP_LEN - 1
    sel_start = jnp.arange(n_sel) * NSA_SEL_LEN
    overlap = ((cmp_start[:, None] < sel_start[None, :] + NSA_SEL_LEN) &
               (cmp_start[:, None] + NSA_CMP_LEN > sel_start[None, :])).astype(jnp.float32)
    ks_blk = ks.reshape(Bsz, n_sel, NSA_SEL_LEN, G, D).transpose(0, 3, 1, 2, 4)
    vs_blk = vs.reshape(Bsz, n_sel, NSA_SEL_LEN, G, D).transpose(0, 3, 1, 2, 4)
    kw_pad = jnp.pad(kw, ((0, 0), (NSA_WINDOW, 0), (0, 0), (0, 0)))
    vw_pad = jnp.pad(vw, ((0, 0), (NSA_WINDOW, 0), (0, 0), (0, 0)))
    b_idx = jnp.arange(Bsz)[:, None, None, None]
    g_idx = jnp.arange(G)[None, :, None, None]
    sel_blocks = jnp.arange(n_sel)

    def block(bi):
        t0 = bi * Q_BLOCK
        t = t0 + jnp.arange(Q_BLOCK)
        qb = lax.dynamic_slice_in_dim(q, t0, Q_BLOCK, axis=1).reshape(Bsz, Q_BLOCK, G, I, D)
        gb = lax.dynamic_slice_in_dim(gates, t0, Q_BLOCK, axis=1).reshape(Bsz, Q_BLOCK, NSA_BRANCHES, G, I)
        s_c = jnp.einsum('bqgid,bngd->bgiqn', qb, kc) * scale
        p_c = masked_softmax(s_c, cmp_end[None, :] <= t[:, None])
        o_c = jnp.einsum('bgiqn,bngd->bqgid', p_c.astype(vc.dtype), vc)
        imp = jnp.einsum('bgiqn,ns->bgqs', p_c, overlap)
        cur = t // NSA_SEL_LEN
        forced = ((sel_blocks[None, :] == 0) | (sel_blocks[None, :] == cur[:, None]) |
                  (sel_blocks[None, :] == cur[:, None] - 1))
        imp = jnp.where(forced, FORCE_SELECT, imp)
        imp = jnp.where(sel_blocks[None, :] <= cur[:, None], imp, NEG_INF)
        _, sel = lax.top_k(imp, k_sel)
        k_g = ks_blk[b_idx, g_idx, sel]
        v_g = vs_blk[b_idx, g_idx, sel]
        kpos = sel[..., None] * NSA_SEL_LEN + jnp.arange(NSA_SEL_LEN)
        mask_s = (kpos <= t[None, None, :, None, None]).reshape(Bsz, G, 1, Q_BLOCK, k_sel * NSA_SEL_LEN)
        s_s = jnp.einsum('bqgid,bgqkld->bgiqkl', qb, k_g).reshape(Bsz, G, I, Q_BLOCK, k_sel * NSA_SEL_LEN) * scale
        p_s = masked_softmax(s_s, mask_s).reshape(Bsz, G, I, Q_BLOCK, k_sel, NSA_SEL_LEN)
        o_s = jnp.einsum('bgiqkl,bgqkld->bqgid', p_s.astype(v_g.dtype), v_g)
        kwb = lax.dynamic_slice_in_dim(kw_pad, t0, Q_BLOCK + NSA_WINDOW, axis=1)
        vwb = lax.dynamic_slice_in_dim(vw_pad, t0, Q_BLOCK + NSA_WINDOW, axis=1)
        wpos = t0 - NSA_WINDOW + jnp.arange(Q_BLOCK + NSA_WINDOW)
        mask_w = ((wpos[None, :] <= t[:, None]) & (wpos[None, :] > t[:, None] - NSA_WINDOW) &
                  (wpos[None, :] >= 0))
        s_w = jnp.einsum('bqgid,bjgd->bgiqj', qb, kwb) * scale
        p_w = masked_softmax(s_w, mask_w)
        o_w = jnp.einsum('bgiqj,bjgd->bqgid', p_w.astype(vwb.dtype), vwb)
        o = (gb[:, :, 0, :, :, None] * o_c + gb[:, :, 1, :, :, None] * o_s +
             gb[:, :, 2, :, :, None] * o_w)
        return o.reshape(Bsz, Q_BLOCK, NSA_Q_W)

    out = lax.map(block, jnp.arange(T // Q_BLOCK))
    return out.transpose(1, 0, 2, 3).reshape(Bsz, T, NSA_Q_W)


def moba_attention(q, k, v):
    Bsz, T, H, D = q.shape
    n_blk = -(-T // MOBA_BLOCK)
    pad = n_blk * MOBA_BLOCK - T
    k_pad = jnp.pad(k, ((0, 0), (0, pad), (0, 0), (0, 0)))
    v_pad = jnp.pad(v, ((0, 0), (0, pad), (0, 0), (0, 0)))
    k_blk = k_pad.reshape(Bsz, n_blk, MOBA_BLOCK, H, D)
    k_mean = jnp.mean(k_blk.astype(jnp.float32), axis=2).astype(k.dtype)
    k_blk = k_blk.transpose(0, 3, 1, 2, 4)
    v_blk = v_pad.reshape(Bsz, n_blk, MOBA_BLOCK, H, D).transpose(0, 3, 1, 2, 4)
    top = max(1, min(MOBA_TOPK, n_blk - 1))
    scale = HEAD_DIM ** -0.5
    b_idx = jnp.arange(Bsz)[:, None, None, None]
    h_idx = jnp.arange(H)[None, :, None, None]
    blocks = jnp.arange(n_blk)

    def block(bi):
        t0 = bi * Q_BLOCK
        t = t0 + jnp.arange(Q_BLOCK)
        cur = t0 // MOBA_BLOCK
        qb = lax.dynamic_slice_in_dim(q, t0, Q_BLOCK, axis=1)
        gate = jnp.einsum('bqhd,bnhd->bhqn', qb, k_mean).astype(jnp.float32)
        gate = jnp.where(blocks < cur, gate, NEG_INF)
        _, sel = lax.top_k(gate, top)
        k_g = k_blk[b_idx, h_idx, sel]
        v_g = v_blk[b_idx, h_idx, sel]
        s_sel = jnp.einsum('bqhd,bhqksd->bhqks', qb, k_g).reshape(Bsz, H, Q_BLOCK, top * MOBA_BLOCK)
        m_sel = jnp.broadcast_to((sel < cur)[..., None],
                                 (Bsz, H, Q_BLOCK, top, MOBA_BLOCK)).reshape(Bsz, H, Q_BLOCK, top * MOBA_BLOCK)
        k_own = lax.dynamic_slice_in_dim(k_pad, cur * MOBA_BLOCK, MOBA_BLOCK, axis=1)
        v_own = lax.dynamic_slice_in_dim(v_pad, cur * MOBA_BLOCK, MOBA_BLOCK, axis=1)
        s_own = jnp.einsum('bqhd,bshd->bhqs', qb, k_own)
        own_pos = cur * MOBA_BLOCK + jnp.arange(MOBA_BLOCK)
        m_own = jnp.broadcast_to(own_pos[None, :] <= t[:, None], (Bsz, H, Q_BLOCK, MOBA_BLOCK))
        p = masked_softmax(jnp.concatenate([s_sel, s_own], axis=-1) * scale,
                           jnp.concatenate([m_sel, m_own], axis=-1)).astype(v.dtype)
        p_sel = p[..., :top * MOBA_BLOCK].reshape(Bsz, H, Q_BLOCK, top, MOBA_BLOCK)
        p_own = p[..., top * MOBA_BLOCK:]
        o = (jnp.einsum('bhqks,bhqksd->bqhd', p_sel, v_g) +
             jnp.einsum('bhqs,bshd->bqhd', p_own, v_own))
        return o.reshape(Bsz, Q_BLOCK, H * D)

    out = lax.map(block, jnp.arange(T // Q_BLOCK))
    return out.transpose(1, 0, 2, 3).reshape(Bsz, T, H * D)


def memory_attention(q, mem_n, w_mem_kv):
    Bsz, T = q.shape[0], q.shape[1]
    k, v = jnp.split(mem_n @ w_mem_kv, 2, axis=-1)
    k = k.reshape(Bsz, -1, MEM_HEADS, HEAD_DIM)
    v = v.reshape(Bsz, -1, MEM_HEADS, HEAD_DIM)
    s = jnp.einsum('bthd,bmhd->bhtm', q, k).astype(jnp.float32) * (HEAD_DIM ** -0.5)
    p = jax.nn.softmax(s, axis=-1).astype(v.dtype)
    return jnp.einsum('bhtm,bmhd->bthd', p, v).reshape(Bsz, T, MEM_W)


def retention(q, k, v):
    Bsz, T, H, _ = q.shape
    C = RET_CHUNK
    NC = T // C
    log_g = jnp.log(1.0 - 2.0 ** (-5.0 - jnp.arange(H, dtype=jnp.float32)))
    i = jnp.arange(C, dtype=jnp.float32)
    diff = i[:, None] - i[None, :]
    decay = jnp.where(diff >= 0, jnp.exp(jnp.maximum(diff, 0.0)[None] * log_g[:, None, None]), 0.0)
    q_decay = jnp.exp((i + 1.0)[None, :] * log_g[:, None])
    k_decay = jnp.exp((C - 1.0 - i)[None, :] * log_g[:, None])
    chunk_decay = jnp.exp(C * log_g)

    def to_chunks(a):
        return a.astype(jnp.float32).reshape(Bsz, NC, C, H, -1).transpose(1, 0, 3, 2, 4)

    def step(state, inp):
        qc, kc, vc = inp
        inner = jnp.einsum('bhid,bhjd->bhij', qc, kc) * decay
        o = (jnp.einsum('bhij,bhjv->bhiv', inner, vc) +
             jnp.einsum('bhid,bhdv->bhiv', qc, state) * q_decay[None, :, :, None])
        state = (state * chunk_decay[None, :, None, None] +
                 jnp.einsum('bhjd,bhjv->bhdv', kc * k_decay[None, :, :, None], vc))
        return state, o

    state0 = jnp.zeros((Bsz, H, q.shape[-1], v.shape[-1]), jnp.float32)
    _, o = lax.scan(step, state0, (to_chunks(q), to_chunks(k), to_chunks(v)))
    return o.transpose(1, 0, 3, 2, 4).reshape(Bsz, T, H, v.shape[-1])


def head_group_norm(o):
    mu = jnp.mean(o, axis=-1, keepdims=True)
    var = jnp.mean(jnp.square(o - mu), axis=-1, keepdims=True)
    return (o - mu) * lax.rsqrt(var + NORM_EPS)


def even_layer(x, mem_n, cos, sin, norm_g, w_in, gate_b, pe_k, w1_k, w2_k, pe_v, w1_v, w2_v, w_mem_kv, w_out):
    Bsz, T, _ = x.shape
    h = rms_norm(x, norm_g)
    (nq, nkc, nvc, nks, nvs, nkw, nvw, ng, nz, mq, mk, mv, mz, eq, ez) = _split(h @ w_in, EVEN_SPLITS)

    def heads(a):
        return a.reshape(Bsz, T, -1, HEAD_DIM)

    def rot(a):
        return apply_rotary(heads(a), cos, sin)

    kc = nsa_compress(rot(nkc), pe_k, w1_k, w2_k)
    vc = nsa_compress(heads(nvc), pe_v, w1_v, w2_v)
    gates = jax.nn.sigmoid(ng + gate_b).reshape(Bsz, T, NSA_BRANCHES, NSA_HEADS)
    a_out = nsa_attention(rot(nq), kc, vc, rot(nks), heads(nvs), rot(nkw), heads(nvw), gates)
    b_out = moba_attention(rot(mq), rot(mk), heads(mv))
    m_out = memory_attention(heads(eq), mem_n, w_mem_kv)
    y = jnp.concatenate([a_out * jax.nn.silu(nz), b_out * jax.nn.silu(mz),
                         m_out * jax.nn.silu(ez)], axis=-1) @ w_out
    return x + y


def odd_layer(x, mem_n, rcos, rsin, norm_g, w_in, w_mem_kv, w_out):
    Bsz, T, _ = x.shape
    h = rms_norm(x, norm_g)
    rq, rk, rv, rz, eq, ez = _split(h @ w_in, ODD_SPLITS)
    q = apply_rotary(rq.reshape(Bsz, T, RET_HEADS, RET_QK_DIM), rcos, rsin)
    k = apply_rotary(rk.reshape(Bsz, T, RET_HEADS, RET_QK_DIM), rcos, rsin) * (RET_QK_DIM ** -0.5)
    v = rv.reshape(Bsz, T, RET_HEADS, RET_V_DIM)
    r_out = head_group_norm(retention(q, k, v)).reshape(Bsz, T, RET_V_W).astype(x.dtype)
    m_out = memory_attention(eq.reshape(Bsz, T, MEM_HEADS, HEAD_DIM), mem_n, w_mem_kv)
    y = jnp.concatenate([r_out * jax.nn.silu(rz), m_out * jax.nn.silu(ez)], axis=-1) @ w_out
    return x + y


def setup_inputs(seed: int = 0) -> dict:
    key = jax.random.key(seed)
    ks = jax.random.split(key, 24)

    def nrm(k, shape, scale):
        return jax.random.normal(k, shape, jnp.float32) * scale

    def gain(k):
        return 1.0 + nrm(k, (D_MODEL,), 0.02)

    x = nrm(ks[0], (BATCH, SEQ, D_MODEL), 1.0)
    mem = nrm(ks[1], (BATCH, MEM_LEN, D_MODEL), 1.0)
    positions = jnp.broadcast_to(jnp.arange(SEQ, dtype=jnp.int32)[None, :], (BATCH, SEQ))
    cmp_in = NSA_CMP_LEN * HEAD_DIM
    return {
        'x': x,
        'mem': mem,
        'positions': positions,
        'l0_norm_g': gain(ks[2]),
        'l0_w_in': nrm(ks[3], (D_MODEL, EVEN_IN_W), D_MODEL ** -0.5),
        'l0_nsa_gate_b': nrm(ks[4], (NSA_BRANCHES * NSA_HEADS,), 0.1),
        'l0_cmp_pe_k': nrm(ks[5], (NSA_CMP_LEN, HEAD_DIM), 0.1),
        'l0_cmp_w1_k': nrm(ks[6], (NSA_CMP_LEN, HEAD_DIM, HEAD_DIM), cmp_in ** -0.5),
        'l0_cmp_w2_k': nrm(ks[7], (HEAD_DIM, HEAD_DIM), HEAD_DIM ** -0.5),
        'l0_cmp_pe_v': nrm(ks[8], (NSA_CMP_LEN, HEAD_DIM), 0.1),
        'l0_cmp_w1_v': nrm(ks[9], (NSA_CMP_LEN, HEAD_DIM, HEAD_DIM), cmp_in ** -0.5),
        'l0_cmp_w2_v': nrm(ks[10], (HEAD_DIM, HEAD_DIM), HEAD_DIM ** -0.5),
        'l0_w_mem_kv': nrm(ks[11], (D_MODEL, 2 * MEM_W), D_MODEL ** -0.5),
        'l0_w_out': nrm(ks[12], (EVEN_OUT_W, D_MODEL), EVEN_OUT_W ** -0.5),
        'l1_norm_g': gain(ks[13]),
        'l1_w_in': nrm(ks[14], (D_MODEL, ODD_IN_W), D_MODEL ** -0.5),
        'l1_w_mem_kv': nrm(ks[15], (D_MODEL, 2 * MEM_W), D_MODEL ** -0.5),
        'l1_w_out': nrm(ks[16], (ODD_OUT_W, D_MODEL), ODD_OUT_W ** -0.5),
        'mem_norm_g': gain(ks[17]),
        'final_norm_g': gain(ks[18]),
    }


def reference(x, mem, positions, l0_norm_g, l0_w_in, l0_nsa_gate_b, l0_cmp_pe_k, l0_cmp_w1_k, l0_cmp_w2_k,
              l0_cmp_pe_v, l0_cmp_w1_v, l0_cmp_w2_v, l0_w_mem_kv, l0_w_out,
              l1_norm_g, l1_w_in, l1_w_mem_kv, l1_w_out, mem_norm_g, final_norm_g):
    attn_inv = 1.0 / (ROPE_THETA ** (jnp.arange(0, HEAD_DIM, 2, dtype=jnp.float32) / HEAD_DIM))
    ret_inv = 1.0 / (ROPE_THETA ** jnp.linspace(0.0, 1.0, RET_QK_DIM // 2, dtype=jnp.float32))
    cos, sin = rotary_tables(positions, attn_inv)
    rcos, rsin = rotary_tables(positions, ret_inv)
    mem_n = rms_norm(mem, mem_norm_g)
    layer_params = (
        (l0_norm_g, l0_w_in, l0_nsa_gate_b, l0_cmp_pe_k, l0_cmp_w1_k, l0_cmp_w2_k,
         l0_cmp_pe_v, l0_cmp_w1_v, l0_cmp_w2_v, l0_w_mem_kv, l0_w_out),
        (l1_norm_g, l1_w_in, l1_w_mem_kv, l1_w_out),
    )
    for layer in range(DEPTH):
        if layer % 2 == 0:
            x = even_layer(x, mem_n, cos, sin, *layer_params[layer])
        else:
            x = odd_layer(x, mem_n, rcos, rsin, *layer_params[layer])
    return rms_norm(x, final_norm_g)
```

```python
import numpy as np
import ml_dtypes
from contextlib import ExitStack
import concourse.bass as bass
import concourse.mybir as mybir
from concourse.bass_utils import run_bass_kernel_spmd

F32 = mybir.dt.float32
BF16 = mybir.dt.bfloat16
I32 = mybir.dt.int32
AF = mybir.ActivationFunctionType
ALU = mybir.AluOpType
AX = mybir.AxisListType
NPBF = ml_dtypes.bfloat16

SEM_ROT = 20000


class Trk:
    __slots__ = ("w", "r", "ap", "name")

    def __init__(self, ap=None, name=""):
        self.w = None
        self.r = {}
        self.ap = ap
        self.name = name


class Prog:
    def __init__(self, nc, es, n_dma_sems=24):
        self.nc = nc
        self.es = es
        self.es_perm = es
        self.eng = {"pe": nc.tensor, "dve": nc.vector, "act": nc.scalar,
                    "pool": nc.gpsimd, "sp": nc.sync}
        self.sems = {}
        self.cur = {}
        self.seen = {e: {} for e in self.eng}
        self.nsem = 0
        for e in self.eng:
            self._new_eng_sem(e)
        self.dma_pool = {}
        self.dma_idx = {}
        for q in ("sp", "act", "pool"):
            self.dma_pool[q] = []
            for i in range(n_dma_sems if q == "sp" else 8):
                k = ("dma", q, i)
                self.sems[k] = es.enter_context(nc.semaphore(f"d_{q}_{i}"))
                self.dma_pool[q].append([k, 0])
            self.dma_idx[q] = 0
        self.out_events = []
        self.ninst = 0

    def _new_eng_sem(self, e):
        k = ("eng", e, self.nsem)
        self.nsem += 1
        self.sems[k] = self.es_perm.enter_context(self.nc.semaphore(f"s_{e}_{self.nsem}"))
        self.cur[e] = [k, 0]

    def sbuf(self, name, shape, dtype, perm=False):
        t = (self.es_perm if perm else self.es).enter_context(self.nc.sbuf_tensor("sb_" + name, list(shape), dtype))
        return Trk(t, name)

    def psum(self, name, shape, dtype=F32):
        t = self.es.enter_context(self.nc.psum_tensor("ps_" + name, list(shape), dtype))
        return Trk(t, name)

    def trk(self, ap=None, name=""):
        return Trk(ap, name)

    def const_tile(self, val):
        if not hasattr(self, "_cb"):
            self._cb = {}
        if val not in self._cb:
            t = self.sbuf(f"cb{len(self._cb)}", [128, 1], F32, perm=True)
            self.op("pool", lambda e: e.memset(t.ap[:], val), writes=[t])
            self._cb[val] = t
        return self._cb[val]

    def _wait(self, e, ev):
        if ev is None:
            return
        k, v = ev
        if self.seen[e].get(k, 0) >= v:
            return
        self.eng[e].wait_ge(self.sems[k], v)
        self.seen[e][k] = v

    def _deps(self, e, reads, writes, same_eng_key):
        for t in reads:
            if t.w is not None:
                self._wait(e, t.w)
        for t in writes:
            if t.w is not None and t.w[0] != same_eng_key:
                self._wait(e, t.w)
            for k, v in t.r.items():
                if k != same_eng_key:
                    self._wait(e, (k, v))

    def op(self, e, fn, reads=(), writes=()):
        if DBG.get("stopped"):
            return None
        ck = self.cur[e]
        if ck[1] >= SEM_ROT:
            self._new_eng_sem(e)
            ck = self.cur[e]
        self._deps(e, reads, writes, ck[0])
        ins = fn(self.eng[e])
        ck[1] += 1
        ins.then_inc(self.sems[ck[0]], 1)
        ev = (ck[0], ck[1])
        for t in reads:
            t.r[ck[0]] = ck[1]
        for t in writes:
            t.w = ev
            t.r = {}
        self.ninst += 1
        return ev

    def dma(self, q, out, in_, reads=(), writes=(), is_output=False, **kw):
        if DBG.get("stopped"):
            return None
        pool = self.dma_pool[q]
        slot = pool[self.dma_idx[q] % len(pool)]
        self.dma_idx[q] += 1
        k = slot[0]
        if slot[1] > 0:
            self._wait(q, (k, slot[1]))
        self._deps(q, reads, writes, None)
        ins = self.eng[q].dma_start(out=out, in_=in_, **kw)
        slot[1] += 16
        ins.then_inc(self.sems[k], 16)
        ev = (k, slot[1])
        for t in reads:
            t.r[k] = slot[1]
        for t in writes:
            t.w = ev
            t.r = {}
        if is_output:
            self.out_events.append(ev)
        self.ninst += 1
        return ev

    def barrier(self):
        if DBG.get("stopped"):
            return
        for e in self.eng:
            for e2 in self.eng:
                if e2 != e and e2 != "sp":
                    k, v = self.cur[e2]
                    if v > 0:
                        self._wait(e, (k, v))
            for q in self.dma_pool:
                for k, v in self.dma_pool[q]:
                    if v > 0:
                        self._wait(e, (k, v))

    def scope(self):
        prog = self

        class _Scope:
            def __enter__(self_):
                self_.old = prog.es
                self_.st = ExitStack()
                self_.st.__enter__()
                prog.es = self_.st
                return self_

            def __exit__(self_, *a):
                prog.barrier()
                prog.es = self_.old
                return self_.st.__exit__(*a)
        return _Scope()

    def finish(self):
        for q in self.dma_pool:
            for k, v in self.dma_pool[q]:
                if v > 0:
                    self._wait("sp", (k, v))
        for e in self.eng:
            k, v = self.cur[e]
            if v > 0 and e != "sp":
                self._wait("sp", (k, v))


T = 16384
DM = 1024
NCORE = 8
TPC = T // NCORE
NTT = TPC // 128
EPS = 1e-6
TWO_PI = float(2.0 * np.pi)
PI = float(np.pi)


def bcast_mid(ap2d, h):
    p, n = ap2d.shape
    return ap2d.unsqueeze(1).to_broadcast([p, h, n])


def emit_sincos(p, pos_i32, invf_bc, nfreq, ntt, cos_t, sin_t, tmp_t, posf_t):
    C1 = 6.28125
    C2 = float(np.float32(2.0 * np.pi - 6.28125))
    C3 = float(2.0 * np.pi - 6.28125 - np.float64(np.float32(2.0 * np.pi - 6.28125)))
    ki = p.sbuf("sc_ki", [128, ntt, nfreq], I32)
    kf = p.sbuf("sc_kf", [128, ntt, nfreq], F32)
    ang = p.sbuf("sc_ang", [128, ntt, nfreq], F32)
    m = p.sbuf("sc_m", [128, ntt, nfreq], F32)
    p.op("dve", lambda e: e.tensor_copy(out=posf_t.ap[:], in_=pos_i32.ap[:]),
         reads=[pos_i32], writes=[posf_t])
    for i in range(ntt):
        p.op("dve", lambda e: e.tensor_scalar(
            out=ang.ap[:, i, :], in0=invf_bc.ap[:], scalar1=posf_t.ap[:, i:i + 1],
            scalar2=None, op0=ALU.mult), reads=[invf_bc, posf_t], writes=[ang])
    p.op("dve", lambda e: e.tensor_scalar(out=ki.ap[:], in0=ang.ap[:], scalar1=1.0 / TWO_PI,
                                          scalar2=None, op0=ALU.mult), reads=[ang], writes=[ki])
    p.op("dve", lambda e: e.tensor_copy(out=kf.ap[:], in_=ki.ap[:]), reads=[ki], writes=[kf])
    r = tmp_t
    p.op("dve", lambda e: e.scalar_tensor_tensor(out=r.ap[:], in0=kf.ap[:], scalar=-C1, in1=ang.ap[:],
                                                 op0=ALU.mult, op1=ALU.add), reads=[kf, ang], writes=[r])
    for cc in (C2, C3):
        p.op("dve", lambda e: e.scalar_tensor_tensor(out=r.ap[:], in0=kf.ap[:], scalar=-cc, in1=r.ap[:],
                                                     op0=ALU.mult, op1=ALU.add), reads=[kf, r], writes=[r])

    def wrap(t):
        p.op("dve", lambda e: e.tensor_scalar(out=m.ap[:], in0=t.ap[:], scalar1=PI, scalar2=TWO_PI,
                                              op0=ALU.is_gt, op1=ALU.mult), reads=[t], writes=[m])
        p.op("dve", lambda e: e.tensor_tensor(out=t.ap[:], in0=t.ap[:], in1=m.ap[:], op=ALU.subtract),
             reads=[t, m], writes=[t])
        p.op("dve", lambda e: e.tensor_scalar(out=m.ap[:], in0=t.ap[:], scalar1=-PI, scalar2=TWO_PI,
                                              op0=ALU.is_lt, op1=ALU.mult), reads=[t], writes=[m])
        p.op("dve", lambda e: e.tensor_tensor(out=t.ap[:], in0=t.ap[:], in1=m.ap[:], op=ALU.add),
             reads=[t, m], writes=[t])
        p.op("dve", lambda e: e.tensor_scalar(out=t.ap[:], in0=t.ap[:], scalar1=PI, scalar2=-PI,
                                              op0=ALU.min, op1=ALU.max), reads=[t], writes=[t])
    wrap(r)
    p.op("act", lambda e: e.activation(out=sin_t.ap[:], in_=r.ap[:], func=AF.Sin),
         reads=[r], writes=[sin_t])
    p.op("dve", lambda e: e.tensor_scalar(out=r.ap[:], in0=r.ap[:], scalar1=PI / 2, scalar2=None,
                                          op0=ALU.add), reads=[r], writes=[r])
    wrap(r)
    p.op("act", lambda e: e.activation(out=cos_t.ap[:], in_=r.ap[:], func=AF.Sin),
         reads=[r], writes=[cos_t])


def emit_proj_phase(p, nc, d, cfg):
    nin, nfreq = cfg["nin"], cfg["nfreq"]
    hd = 2 * nfreq
    NB, NF = cfg["nb"], cfg["nf"]
    ntt = NTT
    W = p.sbuf("W", [128, 8, nin], BF16)
    g_bc = p.sbuf("g_bc", [128, DM], F32)
    ident = p.sbuf("ident", [128, 128], BF16)
    cos_t = p.sbuf("cos_t", [128, ntt, nfreq], F32)
    sin_t = p.sbuf("sin_t", [128, ntt, nfreq], F32)
    p.dma("sp", g_bc.ap[:], d["g"].partition_broadcast(128), writes=[g_bc])
    p.dma("sp", ident.ap[:], d["ident"][:, :], writes=[ident])
    if cfg.get("sig") is not None:
        gb_bc = p.sbuf("gb_bc", [128, 24], F32)
        p.dma("sp", gb_bc.ap[:], d["gb"].partition_broadcast(128), writes=[gb_bc])
    with p.scope():
        invf = p.sbuf("invf", [128, nfreq], F32)
        pos_i = p.sbuf("pos_i", [128, ntt], I32)
        pos_f = p.sbuf("pos_f", [128, ntt], F32)
        ang_t = p.sbuf("ang_t", [128, ntt, nfreq], F32)
        p.dma("sp", invf.ap[:], d["invf"][:, :], writes=[invf])
        p.dma("sp", pos_i.ap[:], d["pos"].rearrange("(n p) -> p n", p=128), writes=[pos_i],
              allow_slow_non_contiguous=True)
        emit_sincos(p, pos_i, invf, nfreq, ntt, cos_t, sin_t, ang_t, pos_f)

    CW = 512
    nchunk = (nin + CW - 1) // CW
    Wt = [p.trk(name=f"Wt{j}") for j in range(nchunk)]
    with p.scope():
        stg = [p.sbuf(f"wstg{i}", [128, 8, CW], F32) for i in range(2)]
        wv = d["w"].rearrange("(c p) n -> p c n", p=128)
        cvt_eng = ["dve", "pool", "act"]
        for j in range(nchunk):
            c0 = j * CW
            cw = min(CW, nin - c0)
            s = stg[j % 2]
            p.dma("sp" if j % 2 == 0 else "pool", s.ap[:, :, 0:cw], wv[:, :, c0:c0 + cw], writes=[s])
            for c in range(8):
                en = cvt_eng[(j * 8 + c) % 3]
                if en == "act":
                    p.op("act", lambda e: e.copy(out=W.ap[:, c, c0:c0 + cw], in_=s.ap[:, c, 0:cw]),
                         reads=[s], writes=[Wt[j]])
                else:
                    p.op(en, lambda e: e.tensor_copy(out=W.ap[:, c, c0:c0 + cw], in_=s.ap[:, c, 0:cw]),
                         reads=[s], writes=[Wt[j]])

    NBUF = cfg.get("nbuf", 2)
    xt = [p.sbuf(f"xt{i}", [128, DM], F32) for i in range(2)]
    junk = p.sbuf("junk", [128, DM], F32)
    ssq = [p.sbuf(f"ssq{i}", [128, 1], F32) for i in range(2)]
    rstd = [p.sbuf(f"rstd{i}", [128, 1], F32) for i in range(2)]
    hb = [p.sbuf(f"hb{i}", [128, DM], BF16) for i in range(2)]
    hT = [p.sbuf(f"hT{i}", [128, 8, 128], BF16) for i in range(2)]
    pT = [p.psum(f"pT{i}", [128, 8, 128], BF16) for i in range(2)]
    pp = [p.psum(f"pp{i}", [128, CW], F32) for i in range(4)]
    proj = [p.sbuf(f"proj{i}", [128, nin], F32) for i in range(NBUF)]
    ob = [p.sbuf(f"ob{i}", [128, NB], BF16) for i in range(NBUF)]
    of = [p.sbuf(f"of{i}", [128, NF], F32) for i in range(NBUF)]
    rt = [p.sbuf(f"rt{i}", [128, 8, nfreq], F32) for i in range(4)]
    epst = p.const_tile(EPS)
    xv = d["x"].rearrange("(n p) f -> n p f", p=128)
    pbv = d["pb"].rearrange("(n p) f -> n p f", p=128)
    pfv = d["pf"].rearrange("(n p) f -> n p f", p=128)
    ppi = 0
    for i in range(ntt):
        b = i % 2
        p.dma("sp", xt[b].ap[:], xv[i], writes=[xt[b]])
        p.op("dve", lambda e: e.memset(ssq[b].ap[:], 0.0), writes=[ssq[b]])
        p.op("act", lambda e: e.activation(out=junk.ap[:], in_=xt[b].ap[:], func=AF.Square,
                                           accum_out=ssq[b].ap[:]),
             reads=[xt[b]], writes=[junk, ssq[b]])
        p.op("act", lambda e: e.activation(out=rstd[b].ap[:], in_=ssq[b].ap[:], func=AF.Sqrt,
                                           bias=epst.ap[:], scale=1.0 / DM),
             reads=[ssq[b], epst], writes=[rstd[b]])
        p.op("dve", lambda e: e.reciprocal(out=rstd[b].ap[:], in_=rstd[b].ap[:]),
             reads=[rstd[b]], writes=[rstd[b]])
        p.op("dve", lambda e: e.scalar_tensor_tensor(
            out=hb[b].ap[:], in0=xt[b].ap[:], scalar=rstd[b].ap[:, 0:1], in1=g_bc.ap[:],
            op0=ALU.mult, op1=ALU.mult), reads=[xt[b], rstd[b], g_bc], writes=[hb[b]])
        for c in range(8):
            p.op("pe", lambda e: e.transpose(out=pT[b].ap[:, c, :], in_=hb[b].ap[:, c * 128:(c + 1) * 128],
                                             identity=ident.ap[:]),
                 reads=[hb[b], ident], writes=[pT[b]])
        p.op("act", lambda e: e.copy(out=hT[b].ap[:], in_=pT[b].ap[:]), reads=[pT[b]], writes=[hT[b]])
        pj = proj[i % NBUF]
        for j in range(nchunk):
            c0 = j * CW
            cw = min(CW, nin - c0)
            ps = pp[ppi % 4]
            ppi += 1
            for c in range(8):
                p.op("pe", lambda e: e.matmul(ps.ap[:, 0:cw], lhsT=hT[b].ap[:, c, :],
                                              rhs=W.ap[:, c, c0:c0 + cw],
                                              start=(c == 0), stop=(c == 7)),
                     reads=[hT[b], Wt[j]], writes=[ps])
            p.op("act", lambda e: e.copy(out=pj.ap[:, c0:c0 + cw], in_=ps.ap[:, 0:cw]),
                 reads=[ps], writes=[pj])
        o_b, o_f = ob[i % NBUF], of[i % NBUF]
        ri = 0
        for (so, nh, do, scl) in cfg["rot"]:
            for h0 in range(0, nh, 8):
                hh = min(8, nh - h0)
                src = pj.ap[:, so + h0 * hd: so + (h0 + hh) * hd].rearrange("p (h t f) -> p h t f", h=hh, t=2)
                dst = o_b.ap[:, do + h0 * hd: do + (h0 + hh) * hd].rearrange("p (h t f) -> p h t f", h=hh, t=2)
                x1, x2 = src[:, :, 0, :], src[:, :, 1, :]
                cb = bcast_mid(cos_t.ap[:, i, :], hh)
                sb = bcast_mid(sin_t.ap[:, i, :], hh)
                t1, t2 = rt[ri % 4], rt[(ri + 1) % 4]
                ri += 2
                e1 = "dve"
                e2 = "pool"
                p.op(e1, lambda e: e.tensor_tensor(out=t1.ap[:, 0:hh, :], in0=x1, in1=cb, op=ALU.mult),
                     reads=[pj, cos_t], writes=[t1])
                p.op(e2, lambda e: e.tensor_tensor(out=t2.ap[:, 0:hh, :], in0=x2, in1=sb, op=ALU.mult),
                     reads=[pj, sin_t], writes=[t2])
                p.op(e1, lambda e: e.tensor_tensor(out=dst[:, :, 0, :], in0=t1.ap[:, 0:hh, :],
                                                   in1=t2.ap[:, 0:hh, :], op=ALU.subtract),
                     reads=[t1, t2], writes=[o_b])
                t3, t4 = rt[ri % 4], rt[(ri + 1) % 4]
                ri += 2
                p.op(e2, lambda e: e.tensor_tensor(out=t3.ap[:, 0:hh, :], in0=x1, in1=sb, op=ALU.mult),
                     reads=[pj, sin_t], writes=[t3])
                p.op(e1, lambda e: e.tensor_tensor(out=t4.ap[:, 0:hh, :], in0=x2, in1=cb, op=ALU.mult),
                     reads=[pj, cos_t], writes=[t4])
                p.op(e2, lambda e: e.tensor_tensor(out=dst[:, :, 1, :], in0=t3.ap[:, 0:hh, :],
                                                   in1=t4.ap[:, 0:hh, :], op=ALU.add),
                     reads=[t3, t4], writes=[o_b])
        for (so, w, do) in cfg["cpb"]:
            p.op("act", lambda e: e.copy(out=o_b.ap[:, do:do + w], in_=pj.ap[:, so:so + w]),
                 reads=[pj], writes=[o_b])
        for (so, w, do) in cfg["silu"]:
            p.op("act", lambda e: e.activation(out=o_f.ap[:, do:do + w], in_=pj.ap[:, so:so + w], func=AF.Silu),
                 reads=[pj], writes=[o_f])
        if cfg.get("sig") is not None:
            so, w, do = cfg["sig"]
            p.op("dve", lambda e: e.tensor_tensor(out=pj.ap[:, so:so + w], in0=pj.ap[:, so:so + w],
                                                  in1=gb_bc.ap[:], op=ALU.add),
                 reads=[pj, gb_bc], writes=[pj])
            p.op("act", lambda e: e.activation(out=o_f.ap[:, do:do + w], in_=pj.ap[:, so:so + w], func=AF.Sigmoid),
                 reads=[pj], writes=[o_f])
        p.dma("sp", pbv[i], o_b.ap[:], reads=[o_b], is_output=True)
        p.dma("sp", pfv[i], o_f.ap[:], reads=[o_f], is_output=True)


L0 = dict(nq=0, nkc=512, nvc=640, nks=768, nvs=896, nkw=1024, nvw=1152, ng=1280, nz=1304,
          mq=1816, mk=2328, mv=2840, mz=3352, eq=3864, ez=4120)
PB0 = dict(nq=0, nkc=512, nvc=640, nks=768, nvs=896, nkw=1024, nvw=1152, mq=1280, mk=1792, mv=2304, eq=2816)
PF0 = dict(gates=0, nz=24, mz=536, ez=1048)
CFG0 = dict(
    nin=4376, nfreq=32, nb=3072, nf=1304,
    rot=[(L0["nq"], 8, PB0["nq"], 1.0), (L0["nkc"], 2, PB0["nkc"], 1.0), (L0["nks"], 2, PB0["nks"], 1.0),
         (L0["nkw"], 2, PB0["nkw"], 1.0), (L0["mq"], 8, PB0["mq"], 1.0), (L0["mk"], 8, PB0["mk"], 1.0)],
    cpb=[(L0["nvc"], 128, PB0["nvc"]), (L0["nvs"], 128, PB0["nvs"]), (L0["nvw"], 128, PB0["nvw"]),
         (L0["mv"], 512, PB0["mv"]), (L0["eq"], 256, PB0["eq"])],
    silu=[(L0["nz"], 512, PF0["nz"]), (L0["mz"], 512, PF0["mz"]), (L0["ez"], 256, PF0["ez"])],
    sig=(L0["ng"], 24, PF0["gates"]),
)
L1 = dict(rq=0, rk=1024, rv=2048, rz=4096, eq=6144, ez=6400)
PB1 = dict(rq=0, rk=1024, rv=2048, eq=4096)
PF1 = dict(rz=0, ez=2048)
CFG1 = dict(
    nin=6656, nfreq=128, nb=4352, nf=2304, nbuf=1,
    rot=[(L1["rq"], 4, PB1["rq"], 1.0), (L1["rk"], 4, PB1["rk"], 1.0)],
    cpb=[(L1["rv"], 2048, PB1["rv"]), (L1["eq"], 256, PB1["eq"])],
    silu=[(L1["rz"], 2048, PF1["rz"]), (L1["ez"], 256, PF1["ez"])],
    sig=None,
)


def build_proj(cfg):
    nc = bass.Bass("TRN2", target_bir_lowering=False)
    d = {}
    d["x"] = nc.dram_tensor("x", [TPC, DM], F32, kind="ExternalInput").ap()
    d["pos"] = nc.dram_tensor("pos", [TPC], I32, kind="ExternalInput").ap()
    d["g"] = nc.dram_tensor("g", [DM], F32, kind="ExternalInput").ap()
    d["w"] = nc.dram_tensor("w", [DM, cfg["nin"]], F32, kind="ExternalInput").ap()
    d["invf"] = nc.dram_tensor("invf", [128, cfg["nfreq"]], F32, kind="ExternalInput").ap()
    d["ident"] = nc.dram_tensor("ident", [128, 128], BF16, kind="ExternalInput").ap()
    if cfg.get("sig") is not None:
        d["gb"] = nc.dram_tensor("gb", [24], F32, kind="ExternalInput").ap()
    d["pb"] = nc.dram_tensor("pb", [TPC, cfg["nb"]], BF16, kind="ExternalOutput").ap()
    d["pf"] = nc.dram_tensor("pf", [TPC, cfg["nf"]], F32, kind="ExternalOutput").ap()
    with ExitStack() as es:
        p = Prog(nc, es)
        p.const_tile(EPS)
        emit_proj_phase(p, nc, d, cfg)
        p.finish()
    return nc


BIG = 30000.0
DBG = {}


class StopEmit(Exception):
    pass


def ck(n):
    c = DBG.setdefault("_cnt", {})
    c[n] = c.get(n, 0) + 1
    if DBG.get("stop") == n or DBG.get("stop") == (n, c[n]):
        DBG["stopped"] = True
NQT = 16
SCALE = 0.125


def emit_mem_kv(p, d, ident, misc, kmT, vm_aug):
    with p.scope():
        wst = p.sbuf("mk_wst", [128, 8, 512], F32)
        wkv = p.sbuf("mk_wkv", [128, 8, 512], BF16)
        mx = p.sbuf("mk_mx", [128, 2, DM], F32)
        mg = p.sbuf("mk_g", [128, DM], F32)
        mh = p.sbuf("mk_h", [128, DM], BF16)
        memT = p.sbuf("mk_T", [128, 8, 256], BF16)
        junk = p.sbuf("mk_junk", [128, DM], F32)
        ssq = p.sbuf("mk_ssq", [128, 1], F32)
        epst = p.const_tile(EPS)
        p.dma("sp", wst.ap[:], d["wkv"].rearrange("(c p) n -> p c n", p=128), writes=[wst])
        p.dma("sp", mx.ap[:], d["memx"].rearrange("(t p) f -> p t f", p=128), writes=[mx])
        p.dma("sp", mg.ap[:], d["memg"].partition_broadcast(128), writes=[mg])
        for c in range(8):
            p.op("pool" if c % 2 else "dve", lambda e: e.tensor_copy(out=wkv.ap[:, c, :], in_=wst.ap[:, c, :]),
                 reads=[wst], writes=[wkv])
        for t in range(2):
            p.op("dve", lambda e: e.memset(ssq.ap[:], 0.0), writes=[ssq])
            p.op("act", lambda e: e.activation(out=junk.ap[:], in_=mx.ap[:, t, :], func=AF.Square,
                                               accum_out=ssq.ap[:]), reads=[mx], writes=[junk, ssq])
            p.op("act", lambda e: e.activation(out=ssq.ap[:], in_=ssq.ap[:], func=AF.Sqrt,
                                               bias=epst.ap[:], scale=1.0 / DM), reads=[ssq, epst], writes=[ssq])
            p.op("dve", lambda e: e.reciprocal(out=ssq.ap[:], in_=ssq.ap[:]), reads=[ssq], writes=[ssq])
            p.op("dve", lambda e: e.scalar_tensor_tensor(out=mh.ap[:], in0=mx.ap[:, t, :], scalar=ssq.ap[:, 0:1],
                                                         in1=mg.ap[:], op0=ALU.mult, op1=ALU.mult),
                 reads=[mx, ssq, mg], writes=[mh])
            for half in range(2):
                mt_ = misc[half]
                pv = mt_.ap[:].bitcast(BF16)
                for cc in range(4):
                    c = half * 4 + cc
                    p.op("pe", lambda e: e.transpose(out=pv[:, cc * 128:(cc + 1) * 128],
                                                     in_=mh.ap[:, c * 128:(c + 1) * 128], identity=ident.ap[:]),
                         reads=[mh, ident], writes=[mt_])
                p.op("act", lambda e: e.copy(
                    out=memT.ap[:, half * 4:half * 4 + 4, t * 128:(t + 1) * 128],
                    in_=pv[:, 0:512].rearrange("p (c m) -> p c m", c=4)), reads=[mt_], writes=[memT])
        for hp in range(2):
            ps = misc[hp]
            for c in range(8):
                p.op("pe", lambda e: e.matmul(ps.ap[:, 0:256], lhsT=wkv.ap[:, c, hp * 128:(hp + 1) * 128],
                                              rhs=memT.ap[:, c, :], start=(c == 0), stop=(c == 7)),
                     reads=[wkv, memT], writes=[ps])
            p.op("act", lambda e: e.copy(out=kmT.ap[:, hp, :], in_=ps.ap[:, 0:256]), reads=[ps], writes=[kmT])
        p.op("pool", lambda e: e.memset(vm_aug.ap[:], 1.0), writes=[vm_aug])
        for mt in range(2):
            ps = misc[mt]
            for c in range(8):
                p.op("pe", lambda e: e.matmul(ps.ap[:, 0:256], lhsT=memT.ap[:, c, mt * 128:(mt + 1) * 128],
                                              rhs=wkv.ap[:, c, 256:512], start=(c == 0), stop=(c == 7)),
                     reads=[wkv, memT], writes=[ps])
            p.op("act", lambda e: e.copy(out=vm_aug.ap[:, mt, :, 0:64],
                                         in_=ps.ap[:, 0:256].rearrange("p (h f) -> p h f", h=4)),
                 reads=[ps], writes=[vm_aug])


def emit_attn_phase(p, nc, d, stages=("nsa", "moba", "mem", "out")):
    ident = p.sbuf("ident", [128, 128], BF16)
    p.dma("sp", ident.ap[:], d["ident"][:, :], writes=[ident])
    causal = p.sbuf("causal", [128, 128], BF16)
    dmask = p.sbuf("dmask", [128, 8, 128], BF16)
    for nm, t in (("causal", causal), ("dmask", dmask)):
        p.dma("sp", t.ap[:], d[nm], writes=[t])
    KA = p.sbuf("KA", [128, T], BF16)
    VA = p.sbuf("VA", [128, 128, 2, 65], BF16)
    QT = p.sbuf("QT", [128, NQT * 512], BF16)
    OG = p.sbuf("OG", [128, NQT, 1280], BF16)
    SA = [p.psum(f"SA{i}", [128, 512]) for i in range(2)]
    OC = p.psum("OC", [128, 2, 512])
    OA = [p.psum(f"OA{i}", [128, 512]) for i in range(2)]
    MISC = [p.psum(f"MISC{i}", [128, 512]) for i in range(2)]
    PT = [p.sbuf(f"PT{i}", [128, 512], BF16) for i in range(4)]
    zt = [p.sbuf(f"zt{i}", [128, 512], F32) for i in range(2)]
    rin = [p.sbuf(f"rin{i}", [128, 4], F32) for i in range(4)]
    SA3 = SA + [MISC[1]]
    cnt = {"s3": 0, "s": 0, "pt": 0, "oa": 0, "z": 0, "rin": 0, "misc": 0, "otsb": 0}

    def nxt(key, lst):
        t = lst[cnt[key] % len(lst)]
        cnt[key] += 1
        return t

    def kv_load(kap, vap):
        for q4 in range(4):
            p.dma("sp" if q4 % 2 == 0 else "pool", KA.ap[:, q4 * 4096:(q4 + 1) * 4096],
                  kap[:, q4 * 4096:(q4 + 1) * 4096], writes=[KA])
        if vap is not None:
            for q4 in range(4):
                p.dma("sp" if q4 % 2 == 0 else "pool", VA.ap[:, q4 * 32:(q4 + 1) * 32],
                      vap[:, q4 * 32:(q4 + 1) * 32], writes=[VA])

    identf = p.sbuf("identf", [128, 128], F32)
    p.op("dve", lambda e: e.tensor_copy(out=identf.ap[:], in_=ident.ap[:]), reads=[ident], writes=[identf])
    otsb = [p.sbuf(f"otsb{i}", [65, 512], F32) for i in range(2)]
    tools = dict(identf=identf, otsb=otsb, MISC=MISC, rin=rin, nxt=nxt)

    def finish_heads(oa, clamp=True):
        return finish_T(p, tools, oa, clamp)

    def stage_nsa():
        band = p.sbuf("band", [128, 128], BF16)
        wm0 = p.sbuf("wm0", [128, 4, 128], BF16)
        cmask = p.sbuf("cmask", [128, NQT, 2, 128], BF16)
        gates = p.sbuf("gates", [128, NQT, 24], F32)
        for nm, t in (("band", band), ("wm0", wm0), ("cmask", cmask), ("gates", gates)):
            p.dma("sp", t.ap[:], d[nm], writes=[t])
        kcT = p.sbuf("kcT", [64, 2, 1024], BF16)
        VC = p.sbuf("VC", [128, 8, 2, 321], BF16)
        p.op("pool", lambda e: e.memset(VC.ap[:], 1.0), writes=[VC])
        for g in range(2):
            p.dma("sp", VC.ap[:, :, g, 65:321], d["ovl"], writes=[VC])
        p.op("pool", lambda e: e.memset(kcT.ap[:], 0.0), writes=[kcT])
        w1s = p.sbuf("w1s", [128, 32, 64], F32)
        w1b = p.sbuf("w1b", [128, 32, 64], BF16)
        w2s = p.sbuf("w2s", [64, 128], F32)
        w2b = p.sbuf("w2b", [64, 128], BF16)
        pes = p.sbuf("pes", [128, 32], F32)
        peb = p.sbuf("peb", [128, 32], BF16)
        cbias = p.sbuf("cbias", [64, 1], F32)
        hidT = p.sbuf("hidT", [64, 1024], BF16)
        for kind in ("k", "v"):
            p.dma("sp", w1s.ap[:], d["w1" + kind], writes=[w1s])
            p.dma("sp", w2s.ap[:], d["w2" + kind], writes=[w2s])
            p.dma("sp", pes.ap[:], d["pe" + kind], writes=[pes])
            p.op("dve", lambda e: e.tensor_copy(out=w1b.ap[:], in_=w1s.ap[:]), reads=[w1s], writes=[w1b])
            p.op("dve", lambda e: e.tensor_copy(out=w2b.ap[:], in_=w2s.ap[:]), reads=[w2s], writes=[w2b])
            p.op("dve", lambda e: e.tensor_copy(out=peb.ap[:], in_=pes.ap[:]), reads=[pes], writes=[peb])
            kv_load(d["nkcT" if kind == "k" else "nvcT"], None)
            ms = nxt("misc", MISC)
            for l in range(32):
                p.op("pe", lambda e: e.matmul(ms.ap[0:64, 0:1], lhsT=w1b.ap[0:64, l, :], rhs=peb.ap[0:64, l:l + 1],
                                              start=(l == 0), stop=(l == 31)), reads=[w1b, peb], writes=[ms])
            p.op("act", lambda e: e.copy(out=cbias.ap[:], in_=ms.ap[0:64, 0:1]), reads=[ms], writes=[cbias])
            for g in range(2):
                pb = 64 * g
                for nh in range(2):
                    n0 = nh * 512
                    nn = 512 if nh == 0 else 511
                    s = nxt("s", SA)
                    for l in range(32):
                        st = n0 * 16 + l
                        rhs = KA.ap[pb:pb + 64, st: st + (nn - 1) * 16 + 1: 16]
                        p.op("pe", lambda e: e.matmul(s.ap[0:64, 0:nn], lhsT=w1b.ap[pb:pb + 64, l, :], rhs=rhs,
                                                      start=(l == 0), stop=(l == 31)), reads=[w1b, KA], writes=[s])
                    p.op("act", lambda e: e.activation(out=hidT.ap[:, n0:n0 + nn], in_=s.ap[0:64, 0:nn], func=AF.Silu,
                                                       bias=cbias.ap[:], scale=1.0), reads=[s, cbias], writes=[hidT])
                if kind == "k":
                    for nh in range(2):
                        n0 = nh * 512
                        nn = 512 if nh == 0 else 511
                        s = nxt("s", SA)
                        p.op("pe", lambda e: e.matmul(s.ap[:, 0:nn], lhsT=w2b.ap[:, :], rhs=hidT.ap[:, n0:n0 + nn],
                                                      start=True, stop=True), reads=[w2b, hidT], writes=[s])
                        p.op("act", lambda e: e.copy(out=kcT.ap[0:64, g, n0:n0 + nn], in_=s.ap[0:64, 0:nn]),
                             reads=[s], writes=[kcT])
                else:
                    for nt in range(8):
                        nn = 128 if nt < 7 else 127
                        ms = nxt("misc", MISC)
                        p.op("pe", lambda e: e.matmul(ms.ap[0:nn, 0:64], lhsT=hidT.ap[:, nt * 128:nt * 128 + nn],
                                                      rhs=w2b.ap[:, 0:64], start=True, stop=True),
                             reads=[w2b, hidT], writes=[ms])
                        p.op("act", lambda e: e.copy(out=VC.ap[0:nn, nt, g, 0:64], in_=ms.ap[0:nn, 0:64]),
                             reads=[ms], writes=[VC])
        for q4 in range(4):
            p.dma("sp" if q4 % 2 == 0 else "pool", VA.ap[:, q4 * 32:(q4 + 1) * 32],
                  d["vs"][:, q4 * 32:(q4 + 1) * 32], writes=[VA])
        rhsW = [p.sbuf(f"rhsW{i}", [128, 4, 512], BF16) for i in range(2)]
        rWm = [p.trk(name=f"rWm{i}") for i in range(2)]
        abt = [p.sbuf(f"abt{i}", [128, 2, 256], F32) for i in range(2)]
        kwt = [p.sbuf(f"kwt{i}", [64, 640], BF16) for i in range(2)]
        vwt = [p.sbuf(f"vwt{i}", [128, 5, 2, 65], BF16) for i in range(2)]
        imp = p.sbuf("imp", [128, 256], F32)
        wk = p.sbuf("impw", [128, 256], F32)
        m8 = p.sbuf("m8", [128, 16], F32)
        selp = p.sbuf("selp", [128, 320], BF16)
        p.op("pool", lambda e: e.memset(selp.ap[:], 0.0), writes=[selp])
        oacc = p.sbuf("oacc", [128, 4, 64], F32)
        gw = p.sbuf("gw", [128, 4], F32)
        imp2 = [imp, p.sbuf("imp_b", [128, 256], F32)]
        selp2 = [selp, p.sbuf("selp_b", [128, 320], BF16)]
        p.op("pool", lambda e: e.memset(selp2[1].ap[:], 0.0), writes=[selp2[1]])
        oacc2 = [oacc, p.sbuf("oacc_b", [128, 4, 64], F32)]
        zt2 = [p.sbuf(f"nz{i}", [128, 256], F32) for i in range(2)]
        state = {}

        def front(g, j, it):
            imp = imp2[it % 2]
            selp = selp2[it % 2]
            oacc = oacc2[it % 2]
            ab = abt[it % 2]
            kw_, vw_ = kwt[it % 2], vwt[it % 2]
            rw, rm = rhsW[it % 2], rWm[it % 2]
            nwin = (16 * j + 15) // 64 + 1
            p.dma("sp", ab.ap[:], d["ab"][j], writes=[ab])
            p.dma("sp", kw_.ap[:], d["kw"][j, g], writes=[kw_])
            p.dma("sp", vw_.ap[:], d["vw"][j], writes=[vw_])
            z = zt2[it % 2]
            p.dma("sp", z.ap[:, 0:256], d["zs"][j][:, g * 256:(g + 1) * 256], writes=[z])
            for w in range(nwin):
                p.dma("sp", rw.ap[0:64, w, :], d["qn"][j, g], writes=[rw])
            ntl = j // 2
            for hp in range(2):
                for nt in range(ntl + 1):
                    s = nxt("s", SA)
                    nmask = 1 if nt >= ntl - 1 else 0
                    p.op("pe", lambda e: e.matmul(s.ap[:, 0:256], lhsT=kcT.ap[0:64, g, nt * 128:(nt + 1) * 128],
                                                  rhs=rw.ap[0:64, 0, hp * 256:(hp + 1) * 256],
                                                  start=True, stop=(nmask == 0)), reads=[kcT, rw], writes=[s])
                    if nmask:
                        wsel = nt - (ntl - 1)
                        if ntl == 0:
                            wsel = 1
                        rb = cmask.ap[:, j, wsel, :].unsqueeze(1).to_broadcast([128, 2, 128])
                        p.op("pe", lambda e: e.matmul(s.ap[:, 0:256], lhsT=ident.ap[:, :], rhs=rb,
                                                      start=False, stop=True), reads=[ident, cmask], writes=[s])
                    pt = nxt("pt", PT)
                    p.op("act", lambda e: e.activation(out=pt.ap[:, 0:256], in_=s.ap[:, 0:256], func=AF.Exp,
                                                       scale=SCALE), reads=[s], writes=[pt])
                    for i2 in range(2):
                        p.op("pe", lambda e: e.matmul(OC.ap[:, i2, 0:321], lhsT=pt.ap[:, i2 * 128:(i2 + 1) * 128],
                                                      rhs=VC.ap[:, nt, g, :], start=(nt == 0), stop=(nt == ntl)),
                             reads=[pt, VC], writes=[OC])
                r = nxt("rin", rin)
                p.op("dve", lambda e: e.tensor_scalar(out=r.ap[:, 0:2], in0=OC.ap[:, :, 64], scalar1=1e-30,
                                                      scalar2=None, op0=ALU.max), reads=[OC], writes=[r])
                p.op("dve", lambda e: e.reciprocal(out=r.ap[:, 0:2], in_=r.ap[:, 0:2]), reads=[r], writes=[r])
                for i2 in range(2):
                    i = hp * 2 + i2
                    if i == 0:
                        p.op("dve", lambda e: e.tensor_scalar(out=imp.ap[:], in0=OC.ap[:, i2, 65:321],
                                                              scalar1=r.ap[:, i2:i2 + 1], scalar2=None, op0=ALU.mult),
                             reads=[OC, r], writes=[imp])
                    else:
                        p.op("dve", lambda e: e.scalar_tensor_tensor(
                            out=imp.ap[:], in0=OC.ap[:, i2, 65:321], scalar=r.ap[:, i2:i2 + 1], in1=imp.ap[:],
                            op0=ALU.mult, op1=ALU.add), reads=[OC, r, imp], writes=[imp])
                p.op("dve", lambda e: e.tensor_tensor(out=gw.ap[:, 0:2], in0=r.ap[:, 0:2],
                                                      in1=gates.ap[:, j, g * 4 + hp * 2: g * 4 + hp * 2 + 2],
                                                      op=ALU.mult), reads=[r, gates], writes=[gw])
                for i2 in range(2):
                    i = hp * 2 + i2
                    p.op("dve", lambda e: e.tensor_scalar(out=oacc.ap[:, i, :], in0=OC.ap[:, i2, 0:64],
                                                          scalar1=gw.ap[:, i2:i2 + 1], scalar2=None, op0=ALU.mult),
                         reads=[OC, gw], writes=[oacc])
            p.op("dve", lambda e: e.tensor_tensor(out=imp.ap[:], in0=imp.ap[:], in1=ab.ap[:, 0, :], op=ALU.mult),
                 reads=[imp, ab], writes=[imp])
            p.op("dve", lambda e: e.tensor_tensor(out=imp.ap[:], in0=imp.ap[:], in1=ab.ap[:, 1, :], op=ALU.add),
                 reads=[imp, ab], writes=[imp])
            p.op("dve", lambda e: e.memset(imp.ap[:, 0:1], 3e9), writes=[imp])
            p.op("dve", lambda e: e.max(out=m8.ap[:, 0:8], in_=imp.ap[:]), reads=[imp], writes=[m8])
            p.op("dve", lambda e: e.match_replace(out=wk.ap[:], in_to_replace=m8.ap[:, 0:8], in_values=imp.ap[:],
                                                  imm_value=-3e30), reads=[imp, m8], writes=[wk])
            p.op("dve", lambda e: e.max(out=m8.ap[:, 8:16], in_=wk.ap[:]), reads=[wk], writes=[m8])
            p.op("dve", lambda e: e.tensor_scalar(out=selp.ap[:, 64:320], in0=imp.ap[:], scalar1=m8.ap[:, 15:16],
                                                  scalar2=None, op0=ALU.is_ge), reads=[imp, m8], writes=[selp])
            state[it] = dict(ab=ab, kw_=kw_, vw_=vw_, rw=rw, rm=rm, z=z, nwin=nwin, selp=selp, oacc=oacc)

        def back(g, j, it):
            st_ = state.pop(it)
            kw_, vw_, rw, rm, z, nwin, selp, oacc = (st_[k_] for k_ in ("kw_", "vw_", "rw", "rm", "z", "nwin", "selp", "oacc"))
            nonlocal_oa = None
            for w in range(nwin):
                ms = nxt("misc", MISC)
                p.op("pe", lambda e: e.matmul(ms.ap[:, 0:128], lhsT=selp.ap[:, 64 * w:64 * w + 128],
                                              rhs=ident.ap[:, :], start=True, stop=True),
                     reads=[selp, ident], writes=[ms])
                p.op("dve", lambda e: e.tensor_scalar(
                    out=rw.ap[64:128, w, :].rearrange("p (i q) -> p i q", i=4),
                    in0=ms.ap[64:128, 0:128].unsqueeze(1).to_broadcast([64, 4, 128]),
                    scalar1=-1.0, scalar2=BIG, op0=ALU.add, op1=ALU.mult), reads=[ms], writes=[rm])
            oa = nxt("oa", OA)
            nkt = 8 * j + 8
            pend = None

            def sel_pv(kt_, pt_):
                p.op("pe", lambda e: e.matmul(oa.ap[0:65, :], lhsT=VA.ap[:, kt_, g, :], rhs=pt_.ap[:, :],
                                              start=(kt_ == 0), stop=(kt_ == nkt - 1)), reads=[pt_, VA], writes=[oa])
            pendq = []
            for kt in range(nkt):
                s = nxt("s3", SA3)
                amb = kt >= 8 * j
                p.op("pe", lambda e: e.matmul(s.ap[:, :], lhsT=KA.ap[:, kt * 128:(kt + 1) * 128],
                                              rhs=rw.ap[:, kt // 32, :], start=True, stop=(not amb)),
                     reads=[KA, rw, rm], writes=[s])
                if amb:
                    rb = dmask.ap[:, kt - 8 * j, :].unsqueeze(1).to_broadcast([128, 4, 128])
                    p.op("pe", lambda e: e.matmul(s.ap[:, :], lhsT=ident.ap[:, :], rhs=rb, start=False, stop=True),
                         reads=[ident, dmask], writes=[s])
                pt = nxt("pt", PT)
                p.op("act", lambda e: e.activation(out=pt.ap[:], in_=s.ap[:], func=AF.Exp, scale=SCALE),
                     reads=[s], writes=[pt])
                pendq.append((kt, pt))
                if len(pendq) > 2:
                    sel_pv(*pendq.pop(0))
            for x_ in pendq:
                sel_pv(*x_)
            r, ov, oa = finish_heads(oa)
            p.op("dve", lambda e: e.tensor_tensor(out=gw.ap[:], in0=r.ap[:], in1=gates.ap[:, j, 8 + g * 4: 8 + g * 4 + 4],
                                                  op=ALU.mult), reads=[r, gates], writes=[gw])
            for i in range(4):
                p.op("dve", lambda e: e.scalar_tensor_tensor(out=oacc.ap[:, i, :], in0=ov[:, i, 0:64],
                                                             scalar=gw.ap[:, i:i + 1], in1=oacc.ap[:, i, :],
                                                             op0=ALU.mult, op1=ALU.add),
                     reads=[oa, gw, oacc], writes=[oacc])
            oa = nxt("oa", OA)
            for w in range(5):
                s = nxt("s", SA)
                msk = []
                if w == 0:
                    msk.append((band, band.ap[:, :]))
                if w == 4:
                    msk.append((causal, causal.ap[:, :]))
                if j == 0 and w < 4:
                    msk.append((wm0, wm0.ap[:, w, :]))
                p.op("pe", lambda e: e.matmul(s.ap[:, :], lhsT=kw_.ap[0:64, w * 128:(w + 1) * 128],
                                              rhs=rw.ap[0:64, 0, :], start=True, stop=(len(msk) == 0)),
                     reads=[kw_, rw], writes=[s])
                for mi, (mt_, map_) in enumerate(msk):
                    rb = map_.unsqueeze(1).to_broadcast([128, 4, 128])
                    p.op("pe", lambda e: e.matmul(s.ap[:, :], lhsT=ident.ap[:, :], rhs=rb, start=False,
                                                  stop=(mi == len(msk) - 1)), reads=[ident, mt_], writes=[s])
                pt = nxt("pt", PT)
                p.op("act", lambda e: e.activation(out=pt.ap[:], in_=s.ap[:], func=AF.Exp, scale=SCALE),
                     reads=[s], writes=[pt])
                p.op("pe", lambda e: e.matmul(oa.ap[0:65, :], lhsT=vw_.ap[:, w, g, :], rhs=pt.ap[:, :],
                                              start=(w == 0), stop=(w == 4)), reads=[pt, vw_], writes=[oa])
            r, ov, oa = finish_heads(oa)
            p.op("dve", lambda e: e.tensor_tensor(out=gw.ap[:], in0=r.ap[:], in1=gates.ap[:, j, 16 + g * 4: 16 + g * 4 + 4],
                                                  op=ALU.mult), reads=[r, gates], writes=[gw])
            for i in range(4):
                p.op("dve", lambda e: e.scalar_tensor_tensor(out=oacc.ap[:, i, :], in0=ov[:, i, 0:64],
                                                             scalar=gw.ap[:, i:i + 1], in1=oacc.ap[:, i, :],
                                                             op0=ALU.mult, op1=ALU.add),
                     reads=[oa, gw, oacc], writes=[oacc])
            p.op("dve", lambda e: e.tensor_tensor(out=OG.ap[:, j, g * 256:(g + 1) * 256],
                                                  in0=oacc.ap[:].rearrange("p h f -> p (h f)"),
                                                  in1=z.ap[:, 0:256], op=ALU.mult), reads=[oacc, z], writes=[OG])
        seq = [(g, j) for g in range(DBG.get("g0", 0), DBG.get("ngrp", 2)) for j in range(DBG.get('nj', NQT))]

        def load_k(g, first):
            for q4 in range(4):
                eng_ = "sp" if q4 % 2 == 0 else "pool"
                p.dma(eng_, KA.ap[0:64, q4 * 4096:(q4 + 1) * 4096], d["ksT"][g][:, q4 * 4096:(q4 + 1) * 4096], writes=[KA])
                if first:
                    p.dma(eng_, KA.ap[64:128, q4 * 4096:(q4 + 1) * 4096], d["epn"][:, q4 * 4096:(q4 + 1) * 4096],
                          writes=[KA])
        front(seq[0][0], seq[0][1], 0)
        for k_i, (g, j) in enumerate(seq):
            if k_i + 1 < len(seq):
                front(seq[k_i + 1][0], seq[k_i + 1][1], k_i + 1)
            if k_i == 0 or seq[k_i - 1][0] != g:
                load_k(g, k_i == 0)
            back(g, j, k_i)
    if "nsa" in stages:
        with p.scope():
            stage_nsa()
    else:
        p.op("pool", lambda e: e.memset(OG.ap[:, :, 0:512], 0.0), writes=[OG])

    def stage_moba():
        kmean = p.sbuf("kmean", [64, 64], F32)
        kmb = p.sbuf("kmb", [64, 64], BF16)
        mabt = [p.sbuf(f"mabt{i}", [128, 4, 3, 64], F32) for i in range(2)]
        gt = p.sbuf("gt", [128, 4, 64], F32)
        m8m = p.sbuf("m8m", [128, 4, 8], F32)
        selmm = p.sbuf("selmm", [128, 4, 64], F32)
        selmb = p.sbuf("selmb", [128, 4, 128], BF16)
        p.op("pool", lambda e: e.memset(selmb.ap[:], 0.0), writes=[selmb])
        QTm = [p.trk(name=f"QTm{G}") for G in range(4)]
        for q4 in range(4):
            p.dma("sp" if q4 % 2 == 0 else "pool", KA.ap[64:128, q4 * 4096:(q4 + 1) * 4096],
                  d["epm"][:, q4 * 4096:(q4 + 1) * 4096], writes=[KA])
        for h in range(2 * DBG.get('nhp', 4)):
            hp, hh = h // 2, h % 2
            if hh == 0:
                for q4 in range(4):
                    p.dma("sp" if q4 % 2 == 0 else "pool", VA.ap[:, q4 * 32:(q4 + 1) * 32],
                          d["mv"][hp][:, q4 * 32:(q4 + 1) * 32], writes=[VA])
            for q4 in range(4):
                p.dma("sp" if q4 % 2 == 0 else "pool", KA.ap[0:64, q4 * 4096:(q4 + 1) * 4096],
                      d["mkT"][h][:, q4 * 4096:(q4 + 1) * 4096], writes=[KA])
            p.dma("sp", QT.ap[0:64, 0:2048], d["mq"][h], writes=[QT])
            p.op("dve", lambda e: e.tensor_reduce(out=kmean.ap[:], in_=KA.ap[0:64, :].rearrange("p (n s) -> p n s", s=256),
                                                  axis=AX.X, op=ALU.add), reads=[KA], writes=[kmean])
            p.op("dve", lambda e: e.tensor_scalar(out=kmb.ap[:], in0=kmean.ap[:], scalar1=1.0 / 256, scalar2=None,
                                                  op0=ALU.mult), reads=[kmean], writes=[kmb])
            for G in range(DBG.get('ng', 4)):
                j0 = G * 4
                mab = mabt[(h * 4 + G) % 2]
                p.dma("sp", mab.ap[:], d["mab"][G], writes=[mab])
                z = nxt("z", zt)
                p.dma("sp", z.ap[:, 0:256].rearrange("p (r f) -> p r f", r=4),
                      d["zs"][j0:j0 + 4, :, 512 + h * 64: 512 + (h + 1) * 64].rearrange("r p f -> p r f"),
                      writes=[z], allow_slow_non_contiguous=True)
                ms = nxt("misc", MISC)
                for r_ in range(4):
                    p.op("pe", lambda e: e.matmul(ms.ap[:, r_ * 64:(r_ + 1) * 64],
                                                  lhsT=QT.ap[0:64, (j0 + r_) * 128:(j0 + r_ + 1) * 128],
                                                  rhs=kmb.ap[:, :], start=True, stop=True),
                         reads=[QT, kmb], writes=[ms])
                msv = ms.ap[:, 0:256].rearrange("p (r n) -> p r n", r=4)
                p.op("dve", lambda e: e.tensor_tensor(out=gt.ap[:], in0=msv, in1=mab.ap[:, :, 0, :], op=ALU.mult),
                     reads=[ms, mab], writes=[gt])
                p.op("dve", lambda e: e.tensor_tensor(out=gt.ap[:], in0=gt.ap[:], in1=mab.ap[:, :, 1, :], op=ALU.add),
                     reads=[gt, mab], writes=[gt])
                for r_ in range(4):
                    p.op("dve", lambda e: e.max(out=m8m.ap[:, r_, :], in_=gt.ap[:, r_, :]), reads=[gt], writes=[m8m])
                for r_ in range(4):
                    p.op("dve", lambda e: e.tensor_scalar(out=selmm.ap[:, r_, :], in0=gt.ap[:, r_, :],
                                                          scalar1=m8m.ap[:, r_, 2:3], scalar2=None, op0=ALU.is_ge),
                         reads=[gt, m8m], writes=[selmm])
                p.op("dve", lambda e: e.tensor_tensor(out=selmm.ap[:], in0=selmm.ap[:], in1=mab.ap[:, :, 0, :],
                                                      op=ALU.mult), reads=[selmm, mab], writes=[selmm])
                p.op("dve", lambda e: e.tensor_tensor(out=selmb.ap[:, :, 64:128], in0=selmm.ap[:], in1=mab.ap[:, :, 2, :],
                                                      op=ALU.add), reads=[selmm, mab], writes=[selmb])
                ms2 = nxt("misc", MISC)
                for r_ in range(4):
                    p.op("pe", lambda e: e.matmul(ms2.ap[:, r_ * 128:(r_ + 1) * 128], lhsT=selmb.ap[:, r_, :],
                                                  rhs=ident.ap[:, :], start=True, stop=True),
                         reads=[selmb, ident], writes=[ms2])
                p.op("dve", lambda e: e.tensor_scalar(out=QT.ap[64:128, j0 * 128:(j0 + 4) * 128], in0=ms2.ap[64:128, :],
                                                      scalar1=-1.0, scalar2=BIG, op0=ALU.add, op1=ALU.mult),
                     reads=[ms2], writes=[QTm[G]])
                oa = nxt("oa", OA)
                nkt = 8 * (j0 + 3) + 8
                pend = None

                def mo_pv(kt_, pt_, c0_):
                    p.op("pe", lambda e: e.matmul(oa.ap[0:65, c0_:512], lhsT=VA.ap[:, kt_, hh, :],
                                                  rhs=pt_.ap[:, c0_:512], start=(kt_ == 0), stop=(kt_ == nkt - 1)),
                         reads=[pt_, VA], writes=[oa])
                pendq = []
                for kt in range(nkt):
                    jlo = max(j0, kt // 8)
                    c0 = (jlo - j0) * 128
                    amb = kt // 8 >= j0
                    s = nxt("s3", SA3)
                    p.op("pe", lambda e: e.matmul(s.ap[:, c0:512], lhsT=KA.ap[:, kt * 128:(kt + 1) * 128],
                                                  rhs=QT.ap[:, j0 * 128 + c0:(j0 + 4) * 128],
                                                  start=True, stop=(not amb)), reads=[KA, QT, QTm[G]], writes=[s])
                    if amb:
                        p.op("pe", lambda e: e.matmul(s.ap[:, c0:c0 + 128], lhsT=ident.ap[:, :],
                                                      rhs=dmask.ap[:, kt % 8, :], start=False, stop=True),
                             reads=[ident, dmask], writes=[s])
                    pt = nxt("pt", PT)
                    p.op("act", lambda e: e.activation(out=pt.ap[:, c0:512], in_=s.ap[:, c0:512], func=AF.Exp,
                                                       scale=SCALE), reads=[s], writes=[pt])
                    pendq.append((kt, pt, c0))
                    if len(pendq) > 2:
                        mo_pv(*pendq.pop(0))
                for x_ in pendq:
                    mo_pv(*x_)
                r, ov, oa = finish_heads(oa)
                for r_ in range(4):
                    p.op("dve", lambda e: e.scalar_tensor_tensor(
                        out=OG.ap[:, j0 + r_, 512 + h * 64: 512 + (h + 1) * 64], in0=ov[:, r_, 0:64],
                        scalar=r.ap[:, r_:r_ + 1], in1=z.ap[:, r_ * 64:(r_ + 1) * 64], op0=ALU.mult, op1=ALU.mult),
                        reads=[oa, r, z], writes=[OG])
    if "moba" in stages:
        with p.scope():
            stage_moba()
    else:
        p.op("pool", lambda e: e.memset(OG.ap[:, :, 512:1024], 0.0), writes=[OG])

    if "mem" in stages:
        kmT = p.sbuf("kmT", [128, 2, 256], BF16)
        vm_aug = p.sbuf("vm_aug", [128, 2, 4, 65], BF16)
        emit_mem_kv(p, d, ident, MISC, kmT, vm_aug)
        emit_mem_attn(p, d, kmT, vm_aug, QT, SA, OA, PT, zt, tools, nxt, OG, 1024, 1024)
    else:
        p.op("pool", lambda e: e.memset(OG.ap[:, :, 1024:1280], 0.0), writes=[OG])

    if "out" in stages:
        emit_outproj(p, d, ident, OG, 1280, MISC, SA, KA, None)
    return OG


def finish_T(p, tools, oa, clamp=True):
    nxt = tools["nxt"]
    ot = nxt("otsb", tools["otsb"])
    p.op("act", lambda e: e.copy(out=ot.ap[:], in_=oa.ap[0:65, :]), reads=[oa], writes=[ot])
    ms = nxt("misc", tools["MISC"])
    for r_ in range(4):
        p.op("pe", lambda e: e.transpose(out=ms.ap[:, r_ * 65:(r_ + 1) * 65], in_=ot.ap[:, r_ * 128:(r_ + 1) * 128],
                                         identity=tools["identf"].ap[0:65, 0:65]),
             reads=[ot, tools["identf"]], writes=[ms])
    ov = ms.ap[:, 0:260].rearrange("p (h f) -> p h f", h=4)
    r = nxt("rin", tools["rin"])
    if clamp:
        p.op("dve", lambda e: e.tensor_scalar(out=r.ap[:, 0:4], in0=ov[:, :, 64], scalar1=1e-30, scalar2=None,
                                              op0=ALU.max), reads=[ms], writes=[r])
        p.op("dve", lambda e: e.reciprocal(out=r.ap[:, 0:4], in_=r.ap[:, 0:4]), reads=[r], writes=[r])
    else:
        p.op("dve", lambda e: e.reciprocal(out=r.ap[:, 0:4], in_=ov[:, :, 64]), reads=[ms], writes=[r])
    return r, ov, ms


def emit_mem_attn(p, d, kmT, vm_aug, QT, SA, OA, PT, zt, tools, nxt, OG, col0, zcol0):
    for hp in range(2):
        p.dma("sp", QT.ap[:, 0:2048], d["eqT"][hp], writes=[QT])
        for hh in range(2):
            pb = 64 * hh
            h = hp * 2 + hh
            for G in range(4):
                j0 = G * 4
                z = nxt("z", zt)
                p.dma("sp", z.ap[:, 0:256].rearrange("p (r f) -> p r f", r=4),
                      d["zs"][j0:j0 + 4, :, zcol0 + h * 64: zcol0 + (h + 1) * 64].rearrange("r p f -> p r f"),
                      writes=[z], allow_slow_non_contiguous=True)
                oa = nxt("oa", OA)
                for mt in range(2):
                    s = nxt("s", SA)
                    p.op("pe", lambda e: e.matmul(s.ap[:, :], lhsT=kmT.ap[pb:pb + 64, hp, mt * 128:(mt + 1) * 128],
                                                  rhs=QT.ap[pb:pb + 64, j0 * 128:(j0 + 4) * 128], start=True, stop=True),
                         reads=[kmT, QT], writes=[s])
                    pt = nxt("pt", PT)
                    p.op("act", lambda e: e.activation(out=pt.ap[:], in_=s.ap[:], func=AF.Exp, scale=SCALE),
                         reads=[s], writes=[pt])
                    p.op("pe", lambda e: e.matmul(oa.ap[0:65, :], lhsT=vm_aug.ap[:, mt, h, :], rhs=pt.ap[:, :],
                                                  start=(mt == 0), stop=(mt == 1)), reads=[pt, vm_aug], writes=[oa])
                r, ov, oa = finish_T(p, tools, oa, clamp=False)
                for r_ in range(4):
                    p.op("dve", lambda e: e.scalar_tensor_tensor(
                        out=OG.ap[:, j0 + r_, col0 + h * 64: col0 + (h + 1) * 64], in0=ov[:, r_, 0:64],
                        scalar=r.ap[:, r_:r_ + 1], in1=z.ap[:, r_ * 64:(r_ + 1) * 64], op0=ALU.mult, op1=ALU.mult),
                        reads=[oa, r, z], writes=[OG])


def emit_outproj(p, d, ident, OG, nfeat, MISC, SA, WB, final_g):
    nch = nfeat // 128
    wst = [p.sbuf(f"op_wst{i}", [128, DM], F32) for i in range(2)]
    Wv = WB.ap[:, 0:nch * DM].rearrange("p (c n) -> p c n", c=nch)
    wv = d["wout"].rearrange("(c p) n -> p c n", p=128)
    for c in range(nch):
        s = wst[c % 2]
        p.dma("sp", s.ap[:], wv[:, c, :], writes=[s])
        p.op("dve" if c % 2 == 0 else "pool", lambda e: e.tensor_copy(out=Wv[:, c, :], in_=s.ap[:]),
             reads=[s], writes=[WB])
    OGT = [p.sbuf(f"OGT{i}", [128, nch, 128], BF16) for i in range(2)]
    xt = [p.sbuf(f"op_x{i}", [128, DM], F32) for i in range(2)]
    if final_g is not None:
        junk = p.sbuf("op_junk", [128, DM], F32)
        ssq = [p.sbuf(f"op_ssq{i}", [128, 1], F32) for i in range(2)]
        epst = p.const_tile(EPS)
    xv = d["x"].rearrange("(n p) f -> n p f", p=128)
    ov = d["xo"].rearrange("(n p) f -> n p f", p=128)
    k = 0
    for j in range(NQT):
        ogt = OGT[j % 2]
        x_ = xt[j % 2]
        p.dma("sp", x_.ap[:], xv[j], writes=[x_])
        for c0 in range(0, nch, 8):
            cn = min(8, nch - c0)
            ms = MISC[k % 2]
            k += 1
            pv = ms.ap[:].bitcast(BF16)
            for cc in range(cn):
                p.op("pe", lambda e: e.transpose(out=pv[:, cc * 128:(cc + 1) * 128],
                                                 in_=OG.ap[:, j, (c0 + cc) * 128:(c0 + cc + 1) * 128], identity=ident.ap[:]),
                     reads=[OG, ident], writes=[ms])
            p.op("act", lambda e: e.copy(out=ogt.ap[:, c0:c0 + cn, :],
                                         in_=pv[:, 0:cn * 128].rearrange("p (c m) -> p c m", c=cn)),
                 reads=[ms], writes=[ogt])
        for nh in range(2):
            s = SA[nh]
            for c in range(nch):
                p.op("pe", lambda e: e.matmul(s.ap[:, :], lhsT=ogt.ap[:, c, :], rhs=Wv[:, c, nh * 512:(nh + 1) * 512],
                                              start=(c == 0), stop=(c == nch - 1)), reads=[ogt, WB], writes=[s])
            p.op("dve", lambda e: e.tensor_tensor(out=x_.ap[:, nh * 512:(nh + 1) * 512], in0=s.ap[:, :],
                                                  in1=x_.ap[:, nh * 512:(nh + 1) * 512], op=ALU.add),
                 reads=[s, x_], writes=[x_])
        if final_g is not None:
            sq = ssq[j % 2]
            p.op("dve", lambda e: e.memset(sq.ap[:], 0.0), writes=[sq])
            p.op("act", lambda e: e.activation(out=junk.ap[:], in_=x_.ap[:], func=AF.Square, accum_out=sq.ap[:]),
                 reads=[x_], writes=[junk, sq])
            p.op("act", lambda e: e.activation(out=sq.ap[:], in_=sq.ap[:], func=AF.Sqrt, bias=epst.ap[:], scale=1.0 / DM),
                 reads=[sq, epst], writes=[sq])
            p.op("dve", lambda e: e.reciprocal(out=sq.ap[:], in_=sq.ap[:]), reads=[sq], writes=[sq])
            p.op("dve", lambda e: e.scalar_tensor_tensor(out=x_.ap[:], in0=x_.ap[:], scalar=sq.ap[:, 0:1],
                                                         in1=final_g.ap[:], op0=ALU.mult, op1=ALU.mult),
                 reads=[x_, sq, final_g], writes=[x_])
        p.dma("sp", ov[j], x_.ap[:], reads=[x_], is_output=True)


P2_INPUTS = dict(
    ident=([128, 128], BF16), causal=([128, 128], BF16), band=([128, 128], BF16), dmask=([128, 8, 128], BF16),
    wm0=([128, 4, 128], BF16), cmask=([128, NQT, 2, 128], BF16), epn=([64, T], BF16),
    gates=([128, NQT, 24], F32), ovl=([128, 8, 256], BF16),
    w1k=([128, 32, 64], F32), w1v=([128, 32, 64], F32), w2k=([64, 128], F32), w2v=([64, 128], F32),
    pek=([128, 32], F32), pev=([128, 32], F32),
    nkcT=([128, T], BF16), nvcT=([128, T], BF16), ksT=([2, 64, T], BF16), vs=([128, 128, 2, 65], BF16),
    qn=([NQT, 2, 64, 512], BF16), ab=([NQT, 128, 2, 256], F32), kw=([NQT, 2, 64, 640], BF16),
    vw=([NQT, 128, 5, 2, 65], BF16), zs=([NQT, 128, 1280], F32),
    mkT=([8, 64, T], BF16), epm=([64, T], BF16), mv=([4, 128, 128, 2, 65], BF16), mq=([8, 64, 2048], BF16),
    mab=([4, 128, 4, 3, 64], F32),
    eqT=([2, 128, 2048], BF16), memx=([256, DM], F32), memg=([DM], F32), wkv=([DM, 512], F32),
    wout=([1280, DM], F32), x=([TPC, DM], F32),
)


def build_attn(stages=("nsa", "moba", "mem", "out"), debug_og=False):
    nc = bass.Bass("TRN2", target_bir_lowering=False)
    d = {}
    for k, (shp, dt) in P2_INPUTS.items():
        d[k] = nc.dram_tensor(k, shp, dt, kind="ExternalInput").ap()
    d["xo"] = nc.dram_tensor("xo", [TPC, DM], F32, kind="ExternalOutput").ap()
    if debug_og:
        d["og"] = nc.dram_tensor("og", [128, NQT, 1280], BF16, kind="ExternalOutput").ap()
    with ExitStack() as es:
        p = Prog(nc, es)
        p.const_tile(EPS)
        try:
            OG = emit_attn_phase(p, nc, d, stages)
            if debug_og:
                p.dma("sp", d["og"], OG.ap[:], reads=[OG], is_output=True)
        except StopEmit:
            p.es = p.es_perm
        p.finish()
    return nc


def _bf(a):
    return np.ascontiguousarray(a).astype(NPBF)


def core_tokens(c):
    return np.concatenate([np.arange((8 * j + c) * 128, (8 * j + c + 1) * 128) for j in range(NQT)])


def static_tables():
    k = np.arange(128)[:, None]
    q = np.arange(128)[None, :]
    tb = {}
    tb["ident"] = np.eye(128, dtype=NPBF)
    tb["causal"] = _bf(np.where(k <= q, 0.0, -BIG))
    tb["band"] = _bf(np.where(k > q, 0.0, -BIG))
    tb["epn"] = _bf((np.arange(64)[:, None] == ((np.arange(T)[None, :] // 64) % 64)).astype(np.float32))
    n = np.arange(1024)
    s = np.arange(256)
    cs = n * 16
    ov = ((cs[:, None] < s[None, :] * 64 + 64) & (cs[:, None] + 32 > s[None, :] * 64)).astype(np.float32)
    ov[1023] = 0
    tb["ovl"] = _bf(ov.reshape(8, 128, 256).transpose(1, 0, 2))
    tb["epm"] = _bf((np.arange(64)[:, None] == (np.arange(T)[None, :] // 256)).astype(np.float32))
    return tb


def core_tables(c):
    k = np.arange(128)[:, None]
    q = np.arange(128)[None, :]
    tb = {}
    dm = np.zeros((128, 8, 128), np.float32)
    for a in range(8):
        if a == c:
            dm[:, a, :] = np.where(k <= q, 0.0, -BIG)
        elif a > c:
            dm[:, a, :] = -BIG
    tb["dmask"] = _bf(dm)
    wm = np.zeros((128, 4, 128), np.float32)
    for w in range(4):
        if c - 4 + w < 0:
            wm[:, w, :] = -BIG
    tb["wm0"] = _bf(wm)
    cm = np.zeros((128, NQT, 2, 128), np.float32)
    ab = np.zeros((NQT, 128, 2, 256), np.float32)
    mab = np.zeros((4, 128, 4, 3, 64), np.float32)
    s = np.arange(256)[None, :]
    nb = np.arange(64)
    for j in range(NQT):
        qt = 8 * j + c
        ntl = j // 2
        for wsel in range(2):
            nt = ntl - 1 + wsel
            if nt < 0:
                continue
            n = nt * 128 + k
            t = qt * 128 + q
            cm[:, j, wsel, :] = np.where((16 * n + 31 <= t) & (n < 1023), 0.0, -BIG)
        t = (qt * 128 + np.arange(128))[:, None]
        cur = t // 64
        forced = (s == cur) | (s == cur - 1)
        valid = s <= cur
        ab[j, :, 0, :] = np.where(valid & ~forced, 1.0, 0.0)
        b = np.zeros((128, 256), np.float32)
        b = np.where(s == cur, 1e9, b)
        b = np.where(s == cur - 1, 2e9, b)
        b = np.where(~valid, -1e30, b)
        ab[j, :, 1, :] = b
        curm = qt // 2
        mab[j // 4, :, j % 4, 0, :] = (nb < curm).astype(np.float32)[None]
        mab[j // 4, :, j % 4, 1, :] = np.where(nb < curm, 0.0, -1e30)[None]
        mab[j // 4, :, j % 4, 2, :] = (nb == curm).astype(np.float32)[None]
    tb["cmask"] = _bf(cm)
    tb["ab"] = ab
    tb["mab"] = mab
    return tb


def with_ones(a):
    return np.concatenate([a, np.ones(a.shape[:-1] + (1,), a.dtype)], -1)


def prep_attn_shared(PB, inputs):
    sh = {}
    sh["nkcT"] = np.ascontiguousarray(PB[:, PB0["nkc"]:PB0["nkc"] + 128].T)
    sh["nvcT"] = np.ascontiguousarray(PB[:, PB0["nvc"]:PB0["nvc"] + 128].T)
    sh["ksT"] = np.ascontiguousarray(PB[:, PB0["nks"]:PB0["nks"] + 128].T).reshape(2, 64, T)
    v = PB[:, PB0["nvs"]:PB0["nvs"] + 128].reshape(128, 128, 2, 64).transpose(1, 0, 2, 3)
    sh["vs"] = np.ascontiguousarray(with_ones(v))
    sh["mkT"] = np.ascontiguousarray(PB[:, PB0["mk"]:PB0["mk"] + 512].reshape(T, 8, 64).transpose(1, 2, 0))
    v = PB[:, PB0["mv"]:PB0["mv"] + 512].reshape(128, 128, 4, 2, 64).transpose(2, 1, 0, 3, 4)
    sh["mv"] = np.ascontiguousarray(with_ones(v))
    for kind in ("k", "v"):
        w1 = inputs[f"l0_cmp_w1_{kind}"].transpose(1, 0, 2)
        sh["w1" + kind] = np.ascontiguousarray(np.concatenate([w1, w1], 0))
        w2 = inputs[f"l0_cmp_w2_{kind}"]
        sh["w2" + kind] = np.ascontiguousarray(np.concatenate([w2, w2], 1))
        pe = inputs[f"l0_cmp_pe_{kind}"].T
        sh["pe" + kind] = np.ascontiguousarray(np.concatenate([pe, pe], 0))
    sh["memx"] = np.ascontiguousarray(inputs["mem"][0])
    sh["memg"] = inputs["mem_norm_g"]
    sh["wkv"] = inputs["l0_w_mem_kv"]
    sh["wout"] = inputs["l0_w_out"]
    sh["kw_full"] = PB[:, PB0["nkw"]:PB0["nkw"] + 128]
    sh["vw_full"] = PB[:, PB0["nvw"]:PB0["nvw"] + 128]
    return sh


def prep_attn_core(c, PB, PF, x, sh, st):
    m = {}
    for k_ in ("nkcT", "nvcT", "ksT", "vs", "mkT", "mv", "w1k", "w1v", "w2k", "w2v", "pek", "pev",
               "memx", "memg", "wkv", "wout"):
        m[k_] = sh[k_]
    for k_ in ("ident", "causal", "band", "epn", "ovl", "epm"):
        m[k_] = st[k_]
    m.update(core_tables(c))
    tq = core_tokens(c)
    pbq = PB[tq]
    pfq = PF[tq]
    a = pbq[:, PB0["nq"]:PB0["nq"] + 512].reshape(NQT, 128, 2, 4, 64).transpose(0, 2, 4, 3, 1)
    m["qn"] = np.ascontiguousarray(a).reshape(NQT, 2, 64, 512)
    kw = np.zeros((NQT, 2, 64, 640), NPBF)
    vw = np.zeros((NQT, 128, 5, 2, 64), NPBF)
    for j in range(NQT):
        qt = 8 * j + c
        for w in range(5):
            kt = qt - 4 + w
            if kt < 0:
                continue
            kw[j, :, :, w * 128:(w + 1) * 128] = sh["kw_full"][kt * 128:(kt + 1) * 128].T.reshape(2, 64, 128)
            vw[j, :, w] = sh["vw_full"][kt * 128:(kt + 1) * 128].reshape(128, 2, 64)
    m["kw"] = kw
    m["vw"] = np.ascontiguousarray(with_ones(vw))
    m["gates"] = np.ascontiguousarray(pfq[:, 0:24].reshape(NQT, 128, 24).transpose(1, 0, 2))
    m["zs"] = np.ascontiguousarray(pfq[:, 24:1304].reshape(NQT, 128, 1280))
    m["mq"] = np.ascontiguousarray(pbq[:, PB0["mq"]:PB0["mq"] + 512].reshape(2048, 8, 64).transpose(1, 2, 0))
    m["eqT"] = np.ascontiguousarray(pbq[:, PB0["eq"]:PB0["eq"] + 256].reshape(2048, 2, 128).transpose(1, 2, 0))
    m["x"] = np.ascontiguousarray(x[tq])
    return m


NPRE = 112
RET_G = [1.0 - 2.0 ** (-5.0 - h) for h in range(4)]


def emit_ret_phase(p, nc, d):
    ident = p.sbuf("ident", [128, 128], BF16)
    p.dma("sp", ident.ap[:], d["ident"][:, :], writes=[ident])
    B = [p.psum(f"B{i}", [128, 512]) for i in range(8)]
    OG = p.sbuf("OG", [128, NQT, 2304], BF16)
    WB = p.sbuf("WB", [128, 18 * DM], BF16)
    QT = p.sbuf("QT", [128, 2048], BF16)
    Sf = p.sbuf("Sf", [128, 4, 2, 512], F32)
    Sb = p.sbuf("Sb", [128, 4, 2, 512], BF16)
    dt = p.sbuf("dt", [128, 4, 128], F32)
    kdec = p.sbuf("kdec", [128, 4], F32)
    qdec = p.sbuf("qdec", [128, 4, 128], F32)
    for nm, t in (("dt", dt), ("kdec", kdec), ("qdec", qdec)):
        p.dma("sp", t.ap[:], d[nm], writes=[t])
    epst = p.const_tile(EPS)
    with p.scope():
        scp = p.sbuf("scp", [128, NPRE, 4], F32)
        p.dma("sp", scp.ap[:], d["scp"], writes=[scp])
        kpt = [p.sbuf(f"kpt{i}", [128, 4, 256], BF16) for i in range(4)]
        vpt = [p.sbuf(f"vpt{i}", [128, 2048], BF16) for i in range(4)]
        kps = [p.sbuf(f"kps{i}", [128, 4, 256], BF16) for i in range(4)]
        for j in range(NPRE):
            k_, v_, ks_ = kpt[j % 4], vpt[j % 4], kps[j % 4]
            p.dma("sp", k_.ap[:], d["kp"][j].rearrange("p (h f) -> p h f", h=4), writes=[k_])
            p.dma("act", v_.ap[:], d["vp"][j], writes=[v_])
            p.op("dve", lambda e: e.tensor_tensor(out=ks_.ap[:], in0=k_.ap[:],
                                                  in1=scp.ap[:, j, :].unsqueeze(2).to_broadcast([128, 4, 256]),
                                                  op=ALU.mult), reads=[k_, scp], writes=[ks_])
            for h in range(4):
                for dc in range(2):
                    bk = B[h * 2 + dc]
                    p.op("pe", lambda e: e.matmul(bk.ap[:, :], lhsT=ks_.ap[:, h, dc * 128:(dc + 1) * 128],
                                                  rhs=v_.ap[:, h * 512:(h + 1) * 512], start=(j == 0),
                                                  stop=(j == NPRE - 1)), reads=[ks_, v_], writes=[bk])
        for h in range(4):
            for dc in range(2):
                bk = B[h * 2 + dc]
                p.op("act", lambda e: e.copy(out=Sf.ap[:, h, dc, :], in_=bk.ap[:, :]), reads=[bk], writes=[Sf])
        p.op("dve", lambda e: e.tensor_copy(out=Sb.ap[:], in_=Sf.ap[:]), reads=[Sf], writes=[Sb])
    with p.scope():
        qTt = [p.sbuf(f"qTt{i}", [128, 4, 2, 128], BF16) for i in range(2)]
        kTt = [p.sbuf(f"kTt{i}", [128, 4, 2, 128], BF16) for i in range(2)]
        ktt = [p.sbuf(f"ktt{i}", [128, 4, 256], BF16) for i in range(2)]
        vtt = [p.sbuf(f"vtt{i}", [128, 2048], BF16) for i in range(2)]
        zt = [p.sbuf(f"rzt{i}", [128, 2048], F32) for i in range(2)]
        Ab = [p.sbuf(f"Ab{i}", [128, 128], BF16) for i in range(2)]
        qs = [p.sbuf(f"qs{i}", [128, 2, 128], BF16) for i in range(2)]
        ks2 = [p.sbuf(f"ks2{i}", [128, 256], BF16) for i in range(2)]
        junk = p.sbuf("rjunk", [128, 512], F32)
        tmpn = p.sbuf("tmpn", [128, 512], F32)
        st = [p.sbuf(f"rst{i}", [128, 4], F32) for i in range(2)]
        AT = B[2]
        Ot = [B[3], B[4]]
        SU = [B[5], B[6]]
        k = 0
        for n in range(NQT):
            q_, kT_, kt_, v_, z_ = qTt[n % 2], kTt[n % 2], ktt[n % 2], vtt[n % 2], zt[n % 2]
            p.dma("sp", q_.ap[:], d["qT"][n], writes=[q_])
            p.dma("sp", kT_.ap[:], d["kT"][n], writes=[kT_])
            p.dma("pool", kt_.ap[:], d["kt"][n].rearrange("p (h f) -> p h f", h=4), writes=[kt_])
            p.dma("pool", v_.ap[:], d["v"][n], writes=[v_])
            p.dma("sp", z_.ap[:], d["zs"][n][:, 0:2048], writes=[z_])
            for h in range(4):
                ab_, qs_, ks_, s_ = Ab[k % 2], qs[k % 2], ks2[k % 2], st[k % 2]
                O = Ot[k % 2]
                k += 1
                for dc in range(2):
                    p.op("pe", lambda e: e.matmul(AT.ap[:, 0:128], lhsT=kT_.ap[:, h, dc, :], rhs=q_.ap[:, h, dc, :],
                                                  start=(dc == 0), stop=(dc == 1)), reads=[kT_, q_], writes=[AT])
                p.op("dve", lambda e: e.tensor_tensor(out=ab_.ap[:], in0=AT.ap[:, 0:128], in1=dt.ap[:, h, :],
                                                      op=ALU.mult), reads=[AT, dt], writes=[ab_])
                p.op("pool", lambda e: e.tensor_tensor(out=qs_.ap[:], in0=q_.ap[:, h, :, :],
                                                       in1=qdec.ap[:, h, :].unsqueeze(1).to_broadcast([128, 2, 128]),
                                                       op=ALU.mult), reads=[q_, qdec], writes=[qs_])
                p.op("pe", lambda e: e.matmul(O.ap[:, :], lhsT=ab_.ap[:, :], rhs=v_.ap[:, h * 512:(h + 1) * 512],
                                              start=True, stop=False), reads=[ab_, v_], writes=[O])
                for dc in range(2):
                    p.op("pe", lambda e: e.matmul(O.ap[:, :], lhsT=qs_.ap[:, dc, :], rhs=Sb.ap[:, h, dc, :],
                                                  start=False, stop=(dc == 1)), reads=[qs_, Sb], writes=[O])
                p.op("dve", lambda e: e.tensor_scalar(out=ks_.ap[:], in0=kt_.ap[:, h, :], scalar1=kdec.ap[:, h:h + 1],
                                                      scalar2=None, op0=ALU.mult), reads=[kt_, kdec], writes=[ks_])
                cd = float(RET_G[h] ** 128)
                for dc in range(2):
                    su = SU[dc]
                    p.op("pe", lambda e: e.matmul(su.ap[:, :], lhsT=ks_.ap[:, dc * 128:(dc + 1) * 128],
                                                  rhs=v_.ap[:, h * 512:(h + 1) * 512], start=True, stop=True),
                         reads=[ks_, v_], writes=[su])
                    p.op("dve", lambda e: e.scalar_tensor_tensor(out=Sf.ap[:, h, dc, :], in0=Sf.ap[:, h, dc, :],
                                                                 scalar=cd, in1=su.ap[:, :], op0=ALU.mult, op1=ALU.add),
                         reads=[Sf, su], writes=[Sf])
                p.op("act", lambda e: e.copy(out=Sb.ap[:, h, :, :], in_=Sf.ap[:, h, :, :]), reads=[Sf], writes=[Sb])
                p.op("dve", lambda e: e.memset(s_.ap[:], 0.0), writes=[s_])
                p.op("act", lambda e: e.activation(out=junk.ap[:], in_=O.ap[:], func=AF.Identity,
                                                   accum_out=s_.ap[:, 0:1]), reads=[O], writes=[junk, s_])
                p.op("act", lambda e: e.activation(out=junk.ap[:], in_=O.ap[:], func=AF.Square,
                                                   accum_out=s_.ap[:, 1:2]), reads=[O], writes=[junk, s_])
                p.op("dve", lambda e: e.tensor_scalar(out=s_.ap[:, 0:1], in0=s_.ap[:, 0:1], scalar1=1.0 / 512,
                                                      scalar2=None, op0=ALU.mult), reads=[s_], writes=[s_])
                p.op("dve", lambda e: e.tensor_tensor(out=s_.ap[:, 2:3], in0=s_.ap[:, 0:1], in1=s_.ap[:, 0:1],
                                                      op=ALU.mult), reads=[s_], writes=[s_])
                p.op("dve", lambda e: e.scalar_tensor_tensor(out=s_.ap[:, 3:4], in0=s_.ap[:, 1:2], scalar=1.0 / 512,
                                                             in1=s_.ap[:, 2:3], op0=ALU.mult, op1=ALU.subtract),
                     reads=[s_], writes=[s_])
                p.op("act", lambda e: e.activation(out=s_.ap[:, 3:4], in_=s_.ap[:, 3:4], func=AF.Sqrt,
                                                   bias=epst.ap[:], scale=1.0), reads=[s_, epst], writes=[s_])
                p.op("dve", lambda e: e.reciprocal(out=s_.ap[:, 3:4], in_=s_.ap[:, 3:4]), reads=[s_], writes=[s_])
                p.op("dve", lambda e: e.tensor_scalar(out=tmpn.ap[:], in0=O.ap[:], scalar1=s_.ap[:, 0:1],
                                                      scalar2=s_.ap[:, 3:4], op0=ALU.subtract, op1=ALU.mult),
                     reads=[O, s_], writes=[tmpn])
                p.op("pool", lambda e: e.tensor_tensor(out=OG.ap[:, n, h * 512:(h + 1) * 512], in0=tmpn.ap[:],
                                                       in1=z_.ap[:, h * 512:(h + 1) * 512], op=ALU.mult),
                     reads=[tmpn, z_], writes=[OG])
    SA = [B[0], B[1]]
    OA = [B[2], B[3]]
    MISC = [B[4], B[5]]
    PT = [p.sbuf(f"PT{i}", [128, 512], BF16) for i in range(3)]
    ztm = [p.sbuf(f"zt{i}", [128, 512], F32) for i in range(2)]
    rin = [p.sbuf(f"rin{i}", [128, 4], F32) for i in range(4)]
    identf = p.sbuf("identf", [128, 128], F32)
    p.op("dve", lambda e: e.tensor_copy(out=identf.ap[:], in_=ident.ap[:]), reads=[ident], writes=[identf])
    otsb = [p.sbuf(f"otsb{i}", [65, 512], F32) for i in range(2)]
    cnt = {}

    def nxt(key, lst):
        t = lst[cnt.get(key, 0) % len(lst)]
        cnt[key] = cnt.get(key, 0) + 1
        return t
    tools = dict(identf=identf, otsb=otsb, MISC=MISC, rin=rin, nxt=nxt)
    kmT = p.sbuf("kmT", [128, 2, 256], BF16)
    vm_aug = p.sbuf("vm_aug", [128, 2, 4, 65], BF16)
    fg = p.sbuf("fg", [128, DM], F32)
    p.dma("sp", fg.ap[:], d["fg"].partition_broadcast(128), writes=[fg])
    emit_mem_kv(p, d, ident, MISC, kmT, vm_aug)
    emit_mem_attn(p, d, kmT, vm_aug, QT, SA, OA, PT, ztm, tools, nxt, OG, 2048, 2048)
    emit_outproj(p, d, ident, OG, 2304, MISC, SA, WB, fg)


P4_INPUTS = dict(
    ident=([128, 128], BF16), dt=([128, 4, 128], F32), kdec=([128, 4], F32), qdec=([128, 4, 128], F32),
    scp=([128, NPRE, 4], F32), kp=([NPRE, 128, 1024], BF16), vp=([NPRE, 128, 2048], BF16),
    qT=([NQT, 128, 4, 2, 128], BF16), kT=([NQT, 128, 4, 2, 128], BF16), kt=([NQT, 128, 1024], BF16),
    v=([NQT, 128, 2048], BF16), zs=([NQT, 128, 2304], F32), eqT=([2, 128, 2048], BF16),
    memx=([256, DM], F32), memg=([DM], F32), wkv=([DM, 512], F32), wout=([2304, DM], F32),
    x=([TPC, DM], F32), fg=([DM], F32),
)


def build_ret():
    nc = bass.Bass("TRN2", target_bir_lowering=False)
    d = {}
    for k, (shp, dt_) in P4_INPUTS.items():
        d[k] = nc.dram_tensor(k, shp, dt_, kind="ExternalInput").ap()
    d["xo"] = nc.dram_tensor("xo", [TPC, DM], F32, kind="ExternalOutput").ap()
    with ExitStack() as es:
        p = Prog(nc, es)
        p.const_tile(EPS)
        emit_ret_phase(p, nc, d)
        p.finish()
    return nc


def ret_tables(c):
    g = np.array(RET_G, np.float64)
    i = np.arange(128, dtype=np.float64)
    tb = {}
    diff = i[None, :] - i[:, None]
    dtab = np.where(diff[:, None, :] >= 0, g[None, :, None] ** np.maximum(diff[:, None, :], 0.0), 0.0) / 16.0
    tb["dt"] = dtab.astype(np.float32)
    kd = g[None, :] ** (127.0 - i[:, None])
    tb["kdec"] = (kd / 16.0).astype(np.float32)
    qd = g[:, None] ** (i[None, :] + 1.0)
    tb["qdec"] = np.ascontiguousarray(np.broadcast_to(qd[None], (128, 4, 128))).astype(np.float32)
    scp = np.zeros((128, NPRE, 4), np.float64)
    J = 16 * c
    for j in range(min(J, NPRE)):
        scp[:, j, :] = kd / 16.0 * (g[None, :] ** (128.0 * (J - 1 - j)))
    tb["scp"] = scp.astype(np.float32)
    return tb


def prep_ret_shared(PB, inputs):
    sh = {}
    sh["kp"] = np.ascontiguousarray(PB[:NPRE * 128, PB1["rk"]:PB1["rk"] + 1024].reshape(NPRE, 128, 1024))
    sh["vp"] = np.ascontiguousarray(PB[:NPRE * 128, PB1["rv"]:PB1["rv"] + 2048].reshape(NPRE, 128, 2048))
    sh["memx"] = np.ascontiguousarray(inputs["mem"][0])
    sh["memg"] = inputs["mem_norm_g"]
    sh["wkv"] = inputs["l1_w_mem_kv"]
    sh["wout"] = inputs["l1_w_out"]
    sh["fg"] = inputs["final_norm_g"]
    sh["ident"] = np.eye(128, dtype=NPBF)
    return sh


def prep_ret_core(c, PB, PF, x1, sh):
    m = dict(sh)
    m.update(ret_tables(c))
    t0 = c * TPC
    pb = PB[t0:t0 + TPC]
    pf = PF[t0:t0 + TPC]
    for nm, off in (("qT", PB1["rq"]), ("kT", PB1["rk"])):
        a = pb[:, off:off + 1024].reshape(NQT, 128, 4, 2, 128).transpose(0, 4, 2, 3, 1)
        m[nm] = np.ascontiguousarray(a)
    m["kt"] = np.ascontiguousarray(pb[:, PB1["rk"]:PB1["rk"] + 1024].reshape(NQT, 128, 1024))
    m["v"] = np.ascontiguousarray(pb[:, PB1["rv"]:PB1["rv"] + 2048].reshape(NQT, 128, 2048))
    m["zs"] = np.ascontiguousarray(pf.reshape(NQT, 128, 2304))
    m["eqT"] = np.ascontiguousarray(pb[:, PB1["eq"]:PB1["eq"] + 256].reshape(TPC, 2, 128).transpose(1, 2, 0))
    m["x"] = np.ascontiguousarray(x1[t0:t0 + TPC])
    return m


def _proj_maps(cfg, x, pos, g, w, invf, gb=None):
    maps = []
    for c in range(NCORE):
        m = {"x": np.ascontiguousarray(x[c * TPC:(c + 1) * TPC]), "pos": np.ascontiguousarray(pos[c * TPC:(c + 1) * TPC]),
             "g": g, "w": w, "invf": np.ascontiguousarray(np.broadcast_to(invf[None], (128, cfg["nfreq"]))),
             "ident": np.eye(128, dtype=NPBF)}
        if gb is not None:
            m["gb"] = gb
        maps.append(m)
    return maps


_NC_CACHE = {}


def _get(name, fn):
    if name not in _NC_CACHE:
        _NC_CACHE[name] = fn()
    return _NC_CACHE[name]


def kernel(**inputs):
    inputs = {k: np.asarray(v) for k, v in inputs.items()}
    x = np.ascontiguousarray(inputs["x"][0], dtype=np.float32)
    pos = np.ascontiguousarray(inputs["positions"][0]).astype(np.int32)
    cores = list(range(NCORE))
    invf0 = (1.0 / (10000.0 ** (np.arange(0, 64, 2, dtype=np.float32) / np.float32(64)))).astype(np.float32)
    invf1 = (1.0 / (10000.0 ** np.linspace(0.0, 1.0, 128, dtype=np.float32))).astype(np.float32)
    nc1 = _get("p1", lambda: build_proj(CFG0))
    r1 = run_bass_kernel_spmd(nc1, _proj_maps(CFG0, x, pos, inputs["l0_norm_g"], inputs["l0_w_in"], invf0,
                                              inputs["l0_nsa_gate_b"]), core_ids=cores).results
    PB = np.concatenate([r["pb"] for r in r1], 0)
    PF = np.concatenate([r["pf"] for r in r1], 0)
    nc2 = _get("p2", lambda: build_attn())
    sh = prep_attn_shared(PB, inputs)
    st = static_tables()
    r2 = run_bass_kernel_spmd(nc2, [prep_attn_core(c, PB, PF, x, sh, st) for c in cores], core_ids=cores).results
    x1 = np.empty((T, DM), np.float32)
    for c in cores:
        x1[core_tokens(c)] = r2[c]["xo"]
    nc3 = _get("p3", lambda: build_proj(CFG1))
    r3 = run_bass_kernel_spmd(nc3, _proj_maps(CFG1, x1, pos, inputs["l1_norm_g"], inputs["l1_w_in"], invf1),
                              core_ids=cores).results
    PB_1 = np.concatenate([r["pb"] for r in r3], 0)
    PF_1 = np.concatenate([r["pf"] for r in r3], 0)
    nc4 = _get("p4", lambda: build_ret())
    sh4 = prep_ret_shared(PB_1, inputs)
    r4 = run_bass_kernel_spmd(nc4, [prep_ret_core(c, PB_1, PF_1, x1, sh4) for c in cores], core_ids=cores).results
    out = np.concatenate([r["xo"] for r in r4], 0)
    return out[None].astype(np.float32)
```

```python
import numpy as np
import ml_dtypes
from contextlib import ExitStack
import concourse.bass as bass
import concourse.mybir as mybir
from concourse.bass_utils import run_bass_kernel_spmd

F32 = mybir.dt.float32
BF16 = mybir.dt.bfloat16
I32 = mybir.dt.int32
AF = mybir.ActivationFunctionType
ALU = mybir.AluOpType
AX = mybir.AxisListType
NPBF = ml_dtypes.bfloat16

SEM_ROT = 20000


class Trk:
    __slots__ = ("w", "r", "ap", "name")

    def __init__(self, ap=None, name=""):
        self.w = None
        self.r = {}
        self.ap = ap
        self.name = name


class Prog:
    def __init__(self, nc, es, n_dma_sems=24):
        self.nc = nc
        self.es = es
        self.es_perm = es
        self.eng = {"pe": nc.tensor, "dve": nc.vector, "act": nc.scalar,
                    "pool": nc.gpsimd, "sp": nc.sync}
        self.sems = {}
        self.cur = {}
        self.seen = {e: {} for e in self.eng}
        self.nsem = 0
        for e in self.eng:
            self._new_eng_sem(e)
        self.dma_pool = {}
        self.dma_idx = {}
        for q in ("sp", "act", "pool"):
            self.dma_pool[q] = []
            for i in range(n_dma_sems if q == "sp" else 8):
                k = ("dma", q, i)
                self.sems[k] = es.enter_context(nc.semaphore(f"d_{q}_{i}"))
                self.dma_pool[q].append([k, 0])
            self.dma_idx[q] = 0
        self.out_events = []
        self.ninst = 0

    def _new_eng_sem(self, e):
        k = ("eng", e, self.nsem)
        self.nsem += 1
        self.sems[k] = self.es_perm.enter_context(self.nc.semaphore(f"s_{e}_{self.nsem}"))
        self.cur[e] = [k, 0]

    def sbuf(self, name, shape, dtype, perm=False):
        t = (self.es_perm if perm else self.es).enter_context(self.nc.sbuf_tensor("sb_" + name, list(shape), dtype))
        return Trk(t, name)

    def psum(self, name, shape, dtype=F32):
        t = self.es.enter_context(self.nc.psum_tensor("ps_" + name, list(shape), dtype))
        return Trk(t, name)

    def trk(self, ap=None, name=""):
        return Trk(ap, name)

    def const_tile(self, val):
        if not hasattr(self, "_cb"):
            self._cb = {}
        if val not in self._cb:
            t = self.sbuf(f"cb{len(self._cb)}", [128, 1], F32, perm=True)
            self.op("pool", lambda e: e.memset(t.ap[:], val), writes=[t])
            self._cb[val] = t
        return self._cb[val]

    def _wait(self, e, ev):
        if ev is None:
            return
        k, v = ev
        if self.seen[e].get(k, 0) >= v:
            return
        self.eng[e].wait_ge(self.sems[k], v)
        self.seen[e][k] = v

    def _deps(self, e, reads, writes, same_eng_key):
        for t in reads:
            if t.w is not None:
                self._wait(e, t.w)
        for t in writes:
            if t.w is not None and t.w[0] != same_eng_key:
                self._wait(e, t.w)
            for k, v in t.r.items():
                if k != same_eng_key:
                    self._wait(e, (k, v))

    def op(self, e, fn, reads=(), writes=()):
        if DBG.get("stopped"):
            return None
        ck = self.cur[e]
        if ck[1] >= SEM_ROT:
            self._new_eng_sem(e)
            ck = self.cur[e]
        self._deps(e, reads, writes, ck[0])
        ins = fn(self.eng[e])
        ck[1] += 1
        ins.then_inc(self.sems[ck[0]], 1)
        ev = (ck[0], ck[1])
        for t in reads:
            t.r[ck[0]] = ck[1]
        for t in writes:
            t.w = ev
            t.r = {}
        self.ninst += 1
        return ev

    def dma(self, q, out, in_, reads=(), writes=(), is_output=False, **kw):
        if DBG.get("stopped"):
            return None
        pool = self.dma_pool[q]
        slot = pool[self.dma_idx[q] % len(pool)]
        self.dma_idx[q] += 1
        k = slot[0]
        if slot[1] > 0:
            self._wait(q, (k, slot[1]))
        self._deps(q, reads, writes, None)
        ins = self.eng[q].dma_start(out=out, in_=in_, **kw)
        slot[1] += 16
        ins.then_inc(self.sems[k], 16)
        ev = (k, slot[1])
        for t in reads:
            t.r[k] = slot[1]
        for t in writes:
            t.w = ev
            t.r = {}
        if is_output:
            self.out_events.append(ev)
        self.ninst += 1
        return ev

    def barrier(self):
        if DBG.get("stopped"):
            return
        for e in self.eng:
            for e2 in self.eng:
                if e2 != e and e2 != "sp":
                    k, v = self.cur[e2]
                    if v > 0:
                        self._wait(e, (k, v))
            for q in self.dma_pool:
                for k, v in self.dma_pool[q]:
                    if v > 0:
                        self._wait(e, (k, v))

    def scope(self):
        prog = self

        class _Scope:
            def __enter__(self_):
                self_.old = prog.es
                self_.st = ExitStack()
                self_.st.__enter__()
                prog.es = self_.st
                return self_

            def __exit__(self_, *a):
                prog.barrier()
                prog.es = self_.old
                return self_.st.__exit__(*a)
        return _Scope()

    def finish(self):
        for q in self.dma_pool:
            for k, v in self.dma_pool[q]:
                if v > 0:
                    self._wait("sp", (k, v))
        for e in self.eng:
            k, v = self.cur[e]
            if v > 0 and e != "sp":
                self._wait("sp", (k, v))


T = 16384
DM = 1024
NCORE = 8
TPC = T // NCORE
NTT = TPC // 128
EPS = 1e-6
TWO_PI = float(2.0 * np.pi)
PI = float(np.pi)


def bcast_mid(ap2d, h):
    p, n = ap2d.shape
    return ap2d.unsqueeze(1).to_broadcast([p, h, n])


def emit_sincos(p, pos_i32, invf_bc, nfreq, ntt, cos_t, sin_t, tmp_t, posf_t):
    C1 = 6.28125
    C2 = float(np.float32(2.0 * np.pi - 6.28125))
    C3 = float(2.0 * np.pi - 6.28125 - np.float64(np.float32(2.0 * np.pi - 6.28125)))
    ki = p.sbuf("sc_ki", [128, ntt, nfreq], I32)
    kf = p.sbuf("sc_kf", [128, ntt, nfreq], F32)
    ang = p.sbuf("sc_ang", [128, ntt, nfreq], F32)
    m = p.sbuf("sc_m", [128, ntt, nfreq], F32)
    p.op("dve", lambda e: e.tensor_copy(out=posf_t.ap[:], in_=pos_i32.ap[:]),
         reads=[pos_i32], writes=[posf_t])
    for i in range(ntt):
        p.op("dve", lambda e: e.tensor_scalar(
            out=ang.ap[:, i, :], in0=invf_bc.ap[:], scalar1=posf_t.ap[:, i:i + 1],
            scalar2=None, op0=ALU.mult), reads=[invf_bc, posf_t], writes=[ang])
    p.op("dve", lambda e: e.tensor_scalar(out=ki.ap[:], in0=ang.ap[:], scalar1=1.0 / TWO_PI,
                                          scalar2=None, op0=ALU.mult), reads=[ang], writes=[ki])
    p.op("dve", lambda e: e.tensor_copy(out=kf.ap[:], in_=ki.ap[:]), reads=[ki], writes=[kf])
    r = tmp_t
    p.op("dve", lambda e: e.scalar_tensor_tensor(out=r.ap[:], in0=kf.ap[:], scalar=-C1, in1=ang.ap[:],
                                                 op0=ALU.mult, op1=ALU.add), reads=[kf, ang], writes=[r])
    for cc in (C2, C3):
        p.op("dve", lambda e: e.scalar_tensor_tensor(out=r.ap[:], in0=kf.ap[:], scalar=-cc, in1=r.ap[:],
                                                     op0=ALU.mult, op1=ALU.add), reads=[kf, r], writes=[r])

    def wrap(t):
        p.op("dve", lambda e: e.tensor_scalar(out=m.ap[:], in0=t.ap[:], scalar1=PI, scalar2=TWO_PI,
                                              op0=ALU.is_gt, op1=ALU.mult), reads=[t], writes=[m])
        p.op("dve", lambda e: e.tensor_tensor(out=t.ap[:], in0=t.ap[:], in1=m.ap[:], op=ALU.subtract),
             reads=[t, m], writes=[t])
        p.op("dve", lambda e: e.tensor_scalar(out=m.ap[:], in0=t.ap[:], scalar1=-PI, scalar2=TWO_PI,
                                              op0=ALU.is_lt, op1=ALU.mult), reads=[t], writes=[m])
        p.op("dve", lambda e: e.tensor_tensor(out=t.ap[:], in0=t.ap[:], in1=m.ap[:], op=ALU.add),
             reads=[t, m], writes=[t])
        p.op("dve", lambda e: e.tensor_scalar(out=t.ap[:], in0=t.ap[:], scalar1=PI, scalar2=-PI,
                                              op0=ALU.min, op1=ALU.max), reads=[t], writes=[t])
    wrap(r)
    p.op("act", lambda e: e.activation(out=sin_t.ap[:], in_=r.ap[:], func=AF.Sin),
         reads=[r], writes=[sin_t])
    p.op("dve", lambda e: e.tensor_scalar(out=r.ap[:], in0=r.ap[:], scalar1=PI / 2, scalar2=None,
                                          op0=ALU.add), reads=[r], writes=[r])
    wrap(r)
    p.op("act", lambda e: e.activation(out=cos_t.ap[:], in_=r.ap[:], func=AF.Sin),
         reads=[r], writes=[cos_t])


def emit_proj_phase(p, nc, d, cfg):
    nin, nfreq = cfg["nin"], cfg["nfreq"]
    hd = 2 * nfreq
    NB, NF = cfg["nb"], cfg["nf"]
    ntt = NTT
    W = p.sbuf("W", [128, 8, nin], BF16)
    g_bc = p.sbuf("g_bc", [128, DM], F32)
    ident = p.sbuf("ident", [128, 128], BF16)
    cos_t = p.sbuf("cos_t", [128, ntt, nfreq], F32)
    sin_t = p.sbuf("sin_t", [128, ntt, nfreq], F32)
    p.dma("sp", g_bc.ap[:], d["g"].partition_broadcast(128), writes=[g_bc])
    p.dma("sp", ident.ap[:], d["ident"][:, :], writes=[ident])
    if cfg.get("sig") is not None:
        gb_bc = p.sbuf("gb_bc", [128, 24], F32)
        p.dma("sp", gb_bc.ap[:], d["gb"].partition_broadcast(128), writes=[gb_bc])
    with p.scope():
        invf = p.sbuf("invf", [128, nfreq], F32)
        pos_i = p.sbuf("pos_i", [128, ntt], I32)
        pos_f = p.sbuf("pos_f", [128, ntt], F32)
        ang_t = p.sbuf("ang_t", [128, ntt, nfreq], F32)
        p.dma("sp", invf.ap[:], d["invf"][:, :], writes=[invf])
        p.dma("sp", pos_i.ap[:], d["pos"].rearrange("(n p) -> p n", p=128), writes=[pos_i],
              allow_slow_non_contiguous=True)
        emit_sincos(p, pos_i, invf, nfreq, ntt, cos_t, sin_t, ang_t, pos_f)

    CW = 512
    nchunk = (nin + CW - 1) // CW
    Wt = [p.trk(name=f"Wt{j}") for j in range(nchunk)]
    with p.scope():
        stg = [p.sbuf(f"wstg{i}", [128, 8, CW], F32) for i in range(2)]
        wv = d["w"].rearrange("(c p) n -> p c n", p=128)
        cvt_eng = ["dve", "pool", "act"]
        for j in range(nchunk):
            c0 = j * CW
            cw = min(CW, nin - c0)
            s = stg[j % 2]
            p.dma("sp" if j % 2 == 0 else "pool", s.ap[:, :, 0:cw], wv[:, :, c0:c0 + cw], writes=[s])
            for c in range(8):
                en = cvt_eng[(j * 8 + c) % 3]
                if en == "act":
                    p.op("act", lambda e: e.copy(out=W.ap[:, c, c0:c0 + cw], in_=s.ap[:, c, 0:cw]),
                         reads=[s], writes=[Wt[j]])
                else:
                    p.op(en, lambda e: e.tensor_copy(out=W.ap[:, c, c0:c0 + cw], in_=s.ap[:, c, 0:cw]),
                         reads=[s], writes=[Wt[j]])

    NBUF = cfg.get("nbuf", 2)
    xt = [p.sbuf(f"xt{i}", [128, DM], F32) for i in range(2)]
    junk = p.sbuf("junk", [128, DM], F32)
    ssq = [p.sbuf(f"ssq{i}", [128, 1], F32) for i in range(2)]
    rstd = [p.sbuf(f"rstd{i}", [128, 1], F32) for i in range(2)]
    hb = [p.sbuf(f"hb{i}", [128, DM], BF16) for i in range(2)]
    hT = [p.sbuf(f"hT{i}", [128, 8, 128], BF16) for i in range(2)]
    pT = [p.psum(f"pT{i}", [128, 8, 128], BF16) for i in range(2)]
    pp = [p.psum(f"pp{i}", [128, CW], F32) for i in range(4)]
    proj = [p.sbuf(f"proj{i}", [128, nin], F32) for i in range(NBUF)]
    ob = [p.sbuf(f"ob{i}", [128, NB], BF16) for i in range(NBUF)]
    of = [p.sbuf(f"of{i}", [128, NF], F32) for i in range(NBUF)]
    rt = [p.sbuf(f"rt{i}", [128, 8, nfreq], F32) for i in range(4)]
    epst = p.const_tile(EPS)
    xv = d["x"].rearrange("(n p) f -> n p f", p=128)
    pbv = d["pb"].rearrange("(n p) f -> n p f", p=128)
    pfv = d["pf"].rearrange("(n p) f -> n p f", p=128)
    ppi = 0
    for i in range(ntt):
        b = i % 2
        p.dma("sp", xt[b].ap[:], xv[i], writes=[xt[b]])
        p.op("dve", lambda e: e.memset(ssq[b].ap[:], 0.0), writes=[ssq[b]])
        p.op("act", lambda e: e.activation(out=junk.ap[:], in_=xt[b].ap[:], func=AF.Square,
                                           accum_out=ssq[b].ap[:]),
             reads=[xt[b]], writes=[junk, ssq[b]])
        p.op("act", lambda e: e.activation(out=rstd[b].ap[:], in_=ssq[b].ap[:], func=AF.Sqrt,
                                           bias=epst.ap[:], scale=1.0 / DM),
             reads=[ssq[b], epst], writes=[rstd[b]])
        p.op("dve", lambda e: e.reciprocal(out=rstd[b].ap[:], in_=rstd[b].ap[:]),
             reads=[rstd[b]], writes=[rstd[b]])
        p.op("dve", lambda e: e.scalar_tensor_tensor(
            out=hb[b].ap[:], in0=xt[b].ap[:], scalar=rstd[b].ap[:, 0:1], in1=g_bc.ap[:],
            op0=ALU.mult, op1=ALU.mult), reads=[xt[b], rstd[b], g_bc], writes=[hb[b]])
        for c in range(8):
            p.op("pe", lambda e: e.transpose(out=pT[b].ap[:, c, :], in_=hb[b].ap[:, c * 128:(c + 1) * 128],
                                             identity=ident.ap[:]),
                 reads=[hb[b], ident], writes=[pT[b]])
        p.op("act", lambda e: e.copy(out=hT[b].ap[:], in_=pT[b].ap[:]), reads=[pT[b]], writes=[hT[b]])
        pj = proj[i % NBUF]
        for j in range(nchunk):
            c0 = j * CW
            cw = min(CW, nin - c0)
            ps = pp[ppi % 4]
            ppi += 1
            for c in range(8):
                p.op("pe", lambda e: e.matmul(ps.ap[:, 0:cw], lhsT=hT[b].ap[:, c, :],
                                              rhs=W.ap[:, c, c0:c0 + cw],
                                              start=(c == 0), stop=(c == 7)),
                     reads=[hT[b], Wt[j]], writes=[ps])
            p.op("act", lambda e: e.copy(out=pj.ap[:, c0:c0 + cw], in_=ps.ap[:, 0:cw]),
                 reads=[ps], writes=[pj])
        o_b, o_f = ob[i % NBUF], of[i % NBUF]
        ri = 0
        for (so, nh, do, scl) in cfg["rot"]:
            for h0 in range(0, nh, 8):
                hh = min(8, nh - h0)
                src = pj.ap[:, so + h0 * hd: so + (h0 + hh) * hd].rearrange("p (h t f) -> p h t f", h=hh, t=2)
                dst = o_b.ap[:, do + h0 * hd: do + (h0 + hh) * hd].rearrange("p (h t f) -> p h t f", h=hh, t=2)
                x1, x2 = src[:, :, 0, :], src[:, :, 1, :]
                cb = bcast_mid(cos_t.ap[:, i, :], hh)
                sb = bcast_mid(sin_t.ap[:, i, :], hh)
                t1, t2 = rt[ri % 4], rt[(ri + 1) % 4]
                ri += 2
                e1 = "dve"
                e2 = "pool"
                p.op(e1, lambda e: e.tensor_tensor(out=t1.ap[:, 0:hh, :], in0=x1, in1=cb, op=ALU.mult),
                     reads=[pj, cos_t], writes=[t1])
                p.op(e2, lambda e: e.tensor_tensor(out=t2.ap[:, 0:hh, :], in0=x2, in1=sb, op=ALU.mult),
                     reads=[pj, sin_t], writes=[t2])
                p.op(e1, lambda e: e.tensor_tensor(out=dst[:, :, 0, :], in0=t1.ap[:, 0:hh, :],
                                                   in1=t2.ap[:, 0:hh, :], op=ALU.subtract),
                     reads=[t1, t2], writes=[o_b])
                t3, t4 = rt[ri % 4], rt[(ri + 1) % 4]
                ri += 2
                p.op(e2, lambda e: e.tensor_tensor(out=t3.ap[:, 0:hh, :], in0=x1, in1=sb, op=ALU.mult),
                     reads=[pj, sin_t], writes=[t3])
                p.op(e1, lambda e: e.tensor_tensor(out=t4.ap[:, 0:hh, :], in0=x2, in1=cb, op=ALU.mult),
                     reads=[pj, cos_t], writes=[t4])
                p.op(e2, lambda e: e.tensor_tensor(out=dst[:, :, 1, :], in0=t3.ap[:, 0:hh, :],
                                                   in1=t4.ap[:, 0:hh, :], op=ALU.add),
                     reads=[t3, t4], writes=[o_b])
        for (so, w, do) in cfg["cpb"]:
            p.op("act", lambda e: e.copy(out=o_b.ap[:, do:do + w], in_=pj.ap[:, so:so + w]),
                 reads=[pj], writes=[o_b])
        for (so, w, do) in cfg["silu"]:
            p.op("act", lambda e: e.activation(out=o_f.ap[:, do:do + w], in_=pj.ap[:, so:so + w], func=AF.Silu),
                 reads=[pj], writes=[o_f])
        if cfg.get("sig") is not None:
            so, w, do = cfg["sig"]
            p.op("dve", lambda e: e.tensor_tensor(out=pj.ap[:, so:so + w], in0=pj.ap[:, so:so + w],
                                                  in1=gb_bc.ap[:], op=ALU.add),
                 reads=[pj, gb_bc], writes=[pj])
            p.op("act", lambda e: e.activation(out=o_f.ap[:, do:do + w], in_=pj.ap[:, so:so + w], func=AF.Sigmoid),
                 reads=[pj], writes=[o_f])
        p.dma("sp", pbv[i], o_b.ap[:], reads=[o_b], is_output=True)
        p.dma("sp", pfv[i], o_f.ap[:], reads=[o_f], is_output=True)


L0 = dict(nq=0, nkc=512, nvc=640, nks=768, nvs=896, nkw=1024, nvw=1152, ng=1280, nz=1304,
          mq=1816, mk=2328, mv=2840, mz=3352, eq=3864, ez=4120)
PB0 = dict(nq=0, nkc=512, nvc=640, nks=768, nvs=896, nkw=1024, nvw=1152, mq=1280, mk=1792, mv=2304, eq=2816)
PF0 = dict(gates=0, nz=24, mz=536, ez=1048)
CFG0 = dict(
    nin=4376, nfreq=32, nb=3072, nf=1304,
    rot=[(L0["nq"], 8, PB0["nq"], 1.0), (L0["nkc"], 2, PB0["nkc"], 1.0), (L0["nks"], 2, PB0["nks"], 1.0),
         (L0["nkw"], 2, PB0["nkw"], 1.0), (L0["mq"], 8, PB0["mq"], 1.0), (L0["mk"], 8, PB0["mk"], 1.0)],
    cpb=[(L0["nvc"], 128, PB0["nvc"]), (L0["nvs"], 128, PB0["nvs"]), (L0["nvw"], 128, PB0["nvw"]),
         (L0["mv"], 512, PB0["mv"]), (L0["eq"], 256, PB0["eq"])],
    silu=[(L0["nz"], 512, PF0["nz"]), (L0["mz"], 512, PF0["mz"]), (L0["ez"], 256, PF0["ez"])],
    sig=(L0["ng"], 24, PF0["gates"]),
)
L1 = dict(rq=0, rk=1024, rv=2048, rz=4096, eq=6144, ez=6400)
PB1 = dict(rq=0, rk=1024, rv=2048, eq=4096)
PF1 = dict(rz=0, ez=2048)
CFG1 = dict(
    nin=6656, nfreq=128, nb=4352, nf=2304, nbuf=1,
    rot=[(L1["rq"], 4, PB1["rq"], 1.0), (L1["rk"], 4, PB1["rk"], 1.0)],
    cpb=[(L1["rv"], 2048, PB1["rv"]), (L1["eq"], 256, PB1["eq"])],
    silu=[(L1["rz"], 2048, PF1["rz"]), (L1["ez"], 256, PF1["ez"])],
    sig=None,
)


def build_proj(cfg):
    nc = bass.Bass("TRN2", target_bir_lowering=False)
    d = {}
    d["x"] = nc.dram_tensor("x", [TPC, DM], F32, kind="ExternalInput").ap()
    d["pos"] = nc.dram_tensor("pos", [TPC], I32, kind="ExternalInput").ap()
    d["g"] = nc.dram_tensor("g", [DM], F32, kind="ExternalInput").ap()
    d["w"] = nc.dram_tensor("w", [DM, cfg["nin"]], F32, kind="ExternalInput").ap()
    d["invf"] = nc.dram_tensor("invf", [128, cfg["nfreq"]], F32, kind="ExternalInput").ap()
    d["ident"] = nc.dram_tensor("ident", [128, 128], BF16, kind="ExternalInput").ap()
    if cfg.get("sig") is not None:
        d["gb"] = nc.dram_tensor("gb", [24], F32, kind="ExternalInput").ap()
    d["pb"] = nc.dram_tensor("pb", [TPC, cfg["nb"]], BF16, kind="ExternalOutput").ap()
    d["pf"] = nc.dram_tensor("pf", [TPC, cfg["nf"]], F32, kind="ExternalOutput").ap()
    with ExitStack() as es:
        p = Prog(nc, es)
        p.const_tile(EPS)
        emit_proj_phase(p, nc, d, cfg)
        p.finish()
    return nc


BIG = 30000.0
DBG = {}


class StopEmit(Exception):
    pass


def ck(n):
    c = DBG.setdefault("_cnt", {})
    c[n] = c.get(n, 0) + 1
    if DBG.get("stop") == n or DBG.get("stop") == (n, c[n]):
        DBG["stopped"] = True
NQT = 16
SCALE = 0.125


def emit_mem_kv(p, d, ident, misc, kmT, vm_aug):
    with p.scope():
        wst = p.sbuf("mk_wst", [128, 8, 512], F32)
        wkv = p.sbuf("mk_wkv", [128, 8, 512], BF16)
        mx = p.sbuf("mk_mx", [128, 2, DM], F32)
        mg = p.sbuf("mk_g", [128, DM], F32)
        mh = p.sbuf("mk_h", [128, DM], BF16)
        memT = p.sbuf("mk_T", [128, 8, 256], BF16)
        junk = p.sbuf("mk_junk", [128, DM], F32)
        ssq = p.sbuf("mk_ssq", [128, 1], F32)
        epst = p.const_tile(EPS)
        p.dma("sp", wst.ap[:], d["wkv"].rearrange("(c p) n -> p c n", p=128), writes=[wst])
        p.dma("sp", mx.ap[:], d["memx"].rearrange("(t p) f -> p t f", p=128), writes=[mx])
        p.dma("sp", mg.ap[:], d["memg"].partition_broadcast(128), writes=[mg])
        for c in range(8):
            p.op("pool" if c % 2 else "dve", lambda e: e.tensor_copy(out=wkv.ap[:, c, :], in_=wst.ap[:, c, :]),
                 reads=[wst], writes=[wkv])
        for t in range(2):
            p.op("dve", lambda e: e.memset(ssq.ap[:], 0.0), writes=[ssq])
            p.op("act", lambda e: e.activation(out=junk.ap[:], in_=mx.ap[:, t, :], func=AF.Square,
                                               accum_out=ssq.ap[:]), reads=[mx], writes=[junk, ssq])
            p.op("act", lambda e: e.activation(out=ssq.ap[:], in_=ssq.ap[:], func=AF.Sqrt,
                                               bias=epst.ap[:], scale=1.0 / DM), reads=[ssq, epst], writes=[ssq])
            p.op("dve", lambda e: e.reciprocal(out=ssq.ap[:], in_=ssq.ap[:]), reads=[ssq], writes=[ssq])
            p.op("dve", lambda e: e.scalar_tensor_tensor(out=mh.ap[:], in0=mx.ap[:, t, :], scalar=ssq.ap[:, 0:1],
                                                         in1=mg.ap[:], op0=ALU.mult, op1=ALU.mult),
                 reads=[mx, ssq, mg], writes=[mh])
            for half in range(2):
                mt_ = misc[half]
                pv = mt_.ap[:].bitcast(BF16)
                for cc in range(4):
                    c = half * 4 + cc
                    p.op("pe", lambda e: e.transpose(out=pv[:, cc * 128:(cc + 1) * 128],
                                                     in_=mh.ap[:, c * 128:(c + 1) * 128], identity=ident.ap[:]),
                         reads=[mh, ident], writes=[mt_])
                p.op("act", lambda e: e.copy(
                    out=memT.ap[:, half * 4:half * 4 + 4, t * 128:(t + 1) * 128],
                    in_=pv[:, 0:512].rearrange("p (c m) -> p c m", c=4)), reads=[mt_], writes=[memT])
        for hp in range(2):
            ps = misc[hp]
            for c in range(8):
                p.op("pe", lambda e: e.matmul(ps.ap[:, 0:256], lhsT=wkv.ap[:, c, hp * 128:(hp + 1) * 128],
                                              rhs=memT.ap[:, c, :], start=(c == 0), stop=(c == 7)),
                     reads=[wkv, memT], writes=[ps])
            p.op("act", lambda e: e.copy(out=kmT.ap[:, hp, :], in_=ps.ap[:, 0:256]), reads=[ps], writes=[kmT])
        p.op("pool", lambda e: e.memset(vm_aug.ap[:], 1.0), writes=[vm_aug])
        for mt in range(2):
            ps = misc[mt]
            for c in range(8):
                p.op("pe", lambda e: e.matmul(ps.ap[:, 0:256], lhsT=memT.ap[:, c, mt * 128:(mt + 1) * 128],
                                              rhs=wkv.ap[:, c, 256:512], start=(c == 0), stop=(c == 7)),
                     reads=[wkv, memT], writes=[ps])
            p.op("act", lambda e: e.copy(out=vm_aug.ap[:, mt, :, 0:64],
                                         in_=ps.ap[:, 0:256].rearrange("p (h f) -> p h f", h=4)),
                 reads=[ps], writes=[vm_aug])


def emit_attn_phase(p, nc, d, stages=("nsa", "moba", "mem", "out")):
    ident = p.sbuf("ident", [128, 128], BF16)
    p.dma("sp", ident.ap[:], d["ident"][:, :], writes=[ident])
    causal = p.sbuf("causal", [128, 128], BF16)
    dmask = p.sbuf("dmask", [128, 8, 128], BF16)
    for nm, t in (("causal", causal), ("dmask", dmask)):
        p.dma("sp", t.ap[:], d[nm], writes=[t])
    KA = p.sbuf("KA", [128, T], BF16)
    VA = p.sbuf("VA", [128, 128, 2, 65], BF16)
    QT = p.sbuf("QT", [128, NQT * 512], BF16)
    OG = p.sbuf("OG", [128, NQT, 1280], BF16)
    SA = [p.psum(f"SA{i}", [128, 512]) for i in range(2)]
    OC = p.psum("OC", [128, 2, 512])
    OA = [p.psum(f"OA{i}", [128, 512]) for i in range(2)]
    MISC = [p.psum(f"MISC{i}", [128, 512]) for i in range(2)]
    PT = [p.sbuf(f"PT{i}", [128, 512], BF16) for i in range(4)]
    zt = [p.sbuf(f"zt{i}", [128, 512], F32) for i in range(2)]
    rin = [p.sbuf(f"rin{i}", [128, 4], F32) for i in range(4)]
    SA3 = SA + [MISC[1]]
    cnt = {"s3": 0, "s": 0, "pt": 0, "oa": 0, "z": 0, "rin": 0, "misc": 0, "otsb": 0}

    def nxt(key, lst):
        t = lst[cnt[key] % len(lst)]
        cnt[key] += 1
        return t

    def kv_load(kap, vap):
        for q4 in range(4):
            p.dma("sp" if q4 % 2 == 0 else "pool", KA.ap[:, q4 * 4096:(q4 + 1) * 4096],
                  kap[:, q4 * 4096:(q4 + 1) * 4096], writes=[KA])
        if vap is not None:
            for q4 in range(4):
                p.dma("sp" if q4 % 2 == 0 else "pool", VA.ap[:, q4 * 32:(q4 + 1) * 32],
                      vap[:, q4 * 32:(q4 + 1) * 32], writes=[VA])

    identf = p.sbuf("identf", [128, 128], F32)
    p.op("dve", lambda e: e.tensor_copy(out=identf.ap[:], in_=ident.ap[:]), reads=[ident], writes=[identf])
    otsb = [p.sbuf(f"otsb{i}", [65, 512], F32) for i in range(2)]
    tools = dict(identf=identf, otsb=otsb, MISC=MISC, rin=rin, nxt=nxt)

    def finish_heads(oa, clamp=True):
        return finish_T(p, tools, oa, clamp)

    def stage_nsa():
        band = p.sbuf("band", [128, 128], BF16)
        wm0 = p.sbuf("wm0", [128, 4, 128], BF16)
        cmask = p.sbuf("cmask", [128, NQT, 2, 128], BF16)
        gates = p.sbuf("gates", [128, NQT, 24], F32)
        for nm, t in (("band", band), ("wm0", wm0), ("cmask", cmask), ("gates", gates)):
            p.dma("sp", t.ap[:], d[nm], writes=[t])
        kcT = p.sbuf("kcT", [64, 2, 1024], BF16)
        VC = p.sbuf("VC", [128, 8, 2, 321], BF16)
        p.op("pool", lambda e: e.memset(VC.ap[:], 1.0), writes=[VC])
        for g in range(2):
            p.dma("sp", VC.ap[:, :, g, 65:321], d["ovl"], writes=[VC])
        p.op("pool", lambda e: e.memset(kcT.ap[:], 0.0), writes=[kcT])
        w1s = p.sbuf("w1s", [128, 32, 64], F32)
        w1b = p.sbuf("w1b", [128, 32, 64], BF16)
        w2s = p.sbuf("w2s", [64, 128], F32)
        w2b = p.sbuf("w2b", [64, 128], BF16)
        pes = p.sbuf("pes", [128, 32], F32)
        peb = p.sbuf("peb", [128, 32], BF16)
        cbias = p.sbuf("cbias", [64, 1], F32)
        hidT = p.sbuf("hidT", [64, 1024], BF16)
        for kind in ("k", "v"):
            p.dma("sp", w1s.ap[:], d["w1" + kind], writes=[w1s])
            p.dma("sp", w2s.ap[:], d["w2" + kind], writes=[w2s])
            p.dma("sp", pes.ap[:], d["pe" + kind], writes=[pes])
            p.op("dve", lambda e: e.tensor_copy(out=w1b.ap[:], in_=w1s.ap[:]), reads=[w1s], writes=[w1b])
            p.op("dve", lambda e: e.tensor_copy(out=w2b.ap[:], in_=w2s.ap[:]), reads=[w2s], writes=[w2b])
            p.op("dve", lambda e: e.tensor_copy(out=peb.ap[:], in_=pes.ap[:]), reads=[pes], writes=[peb])
            kv_load(d["nkcT" if kind == "k" else "nvcT"], None)
            ms = nxt("misc", MISC)
            for l in range(32):
                p.op("pe", lambda e: e.matmul(ms.ap[0:64, 0:1], lhsT=w1b.ap[0:64, l, :], rhs=peb.ap[0:64, l:l + 1],
                                              start=(l == 0), stop=(l == 31)), reads=[w1b, peb], writes=[ms])
            p.op("act", lambda e: e.copy(out=cbias.ap[:], in_=ms.ap[0:64, 0:1]), reads=[ms], writes=[cbias])
            for g in range(2):
                pb = 64 * g
                for nh in range(2):
                    n0 = nh * 512
                    nn = 512 if nh == 0 else 511
                    s = nxt("s", SA)
                    for l in range(32):
                        st = n0 * 16 + l
                        rhs = KA.ap[pb:pb + 64, st: st + (nn - 1) * 16 + 1: 16]
                        p.op("pe", lambda e: e.matmul(s.ap[0:64, 0:nn], lhsT=w1b.ap[pb:pb + 64, l, :], rhs=rhs,
                                                      start=(l == 0), stop=(l == 31)), reads=[w1b, KA], writes=[s])
                    p.op("act", lambda e: e.activation(out=hidT.ap[:, n0:n0 + nn], in_=s.ap[0:64, 0:nn], func=AF.Silu,
                                                       bias=cbias.ap[:], scale=1.0), reads=[s, cbias], writes=[hidT])
                if kind == "k":
                    for nh in range(2):
                        n0 = nh * 512
                        nn = 512 if nh == 0 else 511
                        s = nxt("s", SA)
                        p.op("pe", lambda e: e.matmul(s.ap[:, 0:nn], lhsT=w2b.ap[:, :], rhs=hidT.ap[:, n0:n0 + nn],
                                                      start=True, stop=True), reads=[w2b, hidT], writes=[s])
                        p.op("act", lambda e: e.copy(out=kcT.ap[0:64, g, n0:n0 + nn], in_=s.ap[0:64, 0:nn]),
                             reads=[s], writes=[kcT])
                else:
                    for nt in range(8):
                        nn = 128 if nt < 7 else 127
                        ms = nxt("misc", MISC)
                        p.op("pe", lambda e: e.matmul(ms.ap[0:nn, 0:64], lhsT=hidT.ap[:, nt * 128:nt * 128 + nn],
                                                      rhs=w2b.ap[:, 0:64], start=True, stop=True),
                             reads=[w2b, hidT], writes=[ms])
                        p.op("act", lambda e: e.copy(out=VC.ap[0:nn, nt, g, 0:64], in_=ms.ap[0:nn, 0:64]),
                             reads=[ms], writes=[VC])
        for q4 in range(4):
            p.dma("sp" if q4 % 2 == 0 else "pool", VA.ap[:, q4 * 32:(q4 + 1) * 32],
                  d["vs"][:, q4 * 32:(q4 + 1) * 32], writes=[VA])
        rhsW = [p.sbuf(f"rhsW{i}", [128, 4, 512], BF16) for i in range(2)]
        rWm = [p.trk(name=f"rWm{i}") for i in range(2)]
        abt = [p.sbuf(f"abt{i}", [128, 2, 256], F32) for i in range(2)]
        kwt = [p.sbuf(f"kwt{i}", [64, 640], BF16) for i in range(2)]
        vwt = [p.sbuf(f"vwt{i}", [128, 5, 2, 65], BF16) for i in range(2)]
        imp = p.sbuf("imp", [128, 256], F32)
        wk = p.sbuf("impw", [128, 256], F32)
        m8 = p.sbuf("m8", [128, 16], F32)
        selp = p.sbuf("selp", [128, 320], BF16)
        p.op("pool", lambda e: e.memset(selp.ap[:], 0.0), writes=[selp])
        oacc = p.sbuf("oacc", [128, 4, 64], F32)
        gw = p.sbuf("gw", [128, 4], F32)
        it = 0
        for g in range(DBG.get("g0", 0), DBG.get("ngrp", 2)):
            for q4 in range(4):
                eng_ = "sp" if q4 % 2 == 0 else "pool"
                p.dma(eng_, KA.ap[0:64, q4 * 4096:(q4 + 1) * 4096], d["ksT"][g][:, q4 * 4096:(q4 + 1) * 4096], writes=[KA])
                if g == DBG.get("g0", 0):
                    p.dma(eng_, KA.ap[64:128, q4 * 4096:(q4 + 1) * 4096], d["epn"][:, q4 * 4096:(q4 + 1) * 4096],
                          writes=[KA])
            for j in range(DBG.get('nj', NQT)):
                ab = abt[it % 2]
                kw_, vw_ = kwt[it % 2], vwt[it % 2]
                rw, rm = rhsW[it % 2], rWm[it % 2]
                it += 1
                nwin = (16 * j + 15) // 64 + 1
                p.dma("sp", ab.ap[:], d["ab"][j], writes=[ab])
                p.dma("sp", kw_.ap[:], d["kw"][j, g], writes=[kw_])
                p.dma("sp", vw_.ap[:], d["vw"][j], writes=[vw_])
                z = nxt("z", zt)
                p.dma("sp", z.ap[:, 0:256], d["zs"][j][:, g * 256:(g + 1) * 256], writes=[z])
                for w in range(nwin):
                    p.dma("sp", rw.ap[0:64, w, :], d["qn"][j, g], writes=[rw])
                ntl = j // 2
                for hp in range(2):
                    for nt in range(ntl + 1):
                        s = nxt("s", SA)
                        nmask = 1 if nt >= ntl - 1 else 0
                        p.op("pe", lambda e: e.matmul(s.ap[:, 0:256], lhsT=kcT.ap[0:64, g, nt * 128:(nt + 1) * 128],
                                                      rhs=rw.ap[0:64, 0, hp * 256:(hp + 1) * 256],
                                                      start=True, stop=(nmask == 0)), reads=[kcT, rw], writes=[s])
                        if nmask:
                            wsel = nt - (ntl - 1)
                            if ntl == 0:
                                wsel = 1
                            rb = cmask.ap[:, j, wsel, :].unsqueeze(1).to_broadcast([128, 2, 128])
                            p.op("pe", lambda e: e.matmul(s.ap[:, 0:256], lhsT=ident.ap[:, :], rhs=rb,
                                                          start=False, stop=True), reads=[ident, cmask], writes=[s])
                        pt = nxt("pt", PT)
                        p.op("act", lambda e: e.activation(out=pt.ap[:, 0:256], in_=s.ap[:, 0:256], func=AF.Exp,
                                                           scale=SCALE), reads=[s], writes=[pt])
                        for i2 in range(2):
                            p.op("pe", lambda e: e.matmul(OC.ap[:, i2, 0:321], lhsT=pt.ap[:, i2 * 128:(i2 + 1) * 128],
                                                          rhs=VC.ap[:, nt, g, :], start=(nt == 0), stop=(nt == ntl)),
                                 reads=[pt, VC], writes=[OC])
                    r = nxt("rin", rin)
                    p.op("dve", lambda e: e.tensor_scalar(out=r.ap[:, 0:2], in0=OC.ap[:, :, 64], scalar1=1e-30,
                                                          scalar2=None, op0=ALU.max), reads=[OC], writes=[r])
                    p.op("dve", lambda e: e.reciprocal(out=r.ap[:, 0:2], in_=r.ap[:, 0:2]), reads=[r], writes=[r])
                    for i2 in range(2):
                        i = hp * 2 + i2
                        if i == 0:
                            p.op("dve", lambda e: e.tensor_scalar(out=imp.ap[:], in0=OC.ap[:, i2, 65:321],
                                                                  scalar1=r.ap[:, i2:i2 + 1], scalar2=None, op0=ALU.mult),
                                 reads=[OC, r], writes=[imp])
                        else:
                            p.op("dve", lambda e: e.scalar_tensor_tensor(
                                out=imp.ap[:], in0=OC.ap[:, i2, 65:321], scalar=r.ap[:, i2:i2 + 1], in1=imp.ap[:],
                                op0=ALU.mult, op1=ALU.add), reads=[OC, r, imp], writes=[imp])
                    p.op("dve", lambda e: e.tensor_tensor(out=gw.ap[:, 0:2], in0=r.ap[:, 0:2],
                                                          in1=gates.ap[:, j, g * 4 + hp * 2: g * 4 + hp * 2 + 2],
                                                          op=ALU.mult), reads=[r, gates], writes=[gw])
                    for i2 in range(2):
                        i = hp * 2 + i2
                        p.op("dve", lambda e: e.tensor_scalar(out=oacc.ap[:, i, :], in0=OC.ap[:, i2, 0:64],
                                                              scalar1=gw.ap[:, i2:i2 + 1], scalar2=None, op0=ALU.mult),
                             reads=[OC, gw], writes=[oacc])
                p.op("dve", lambda e: e.tensor_tensor(out=imp.ap[:], in0=imp.ap[:], in1=ab.ap[:, 0, :], op=ALU.mult),
                     reads=[imp, ab], writes=[imp])
                p.op("dve", lambda e: e.tensor_tensor(out=imp.ap[:], in0=imp.ap[:], in1=ab.ap[:, 1, :], op=ALU.add),
                     reads=[imp, ab], writes=[imp])
                p.op("dve", lambda e: e.memset(imp.ap[:, 0:1], 3e9), writes=[imp])
                p.op("dve", lambda e: e.max(out=m8.ap[:, 0:8], in_=imp.ap[:]), reads=[imp], writes=[m8])
                p.op("dve", lambda e: e.match_replace(out=wk.ap[:], in_to_replace=m8.ap[:, 0:8], in_values=imp.ap[:],
                                                      imm_value=-3e30), reads=[imp, m8], writes=[wk])
                p.op("dve", lambda e: e.max(out=m8.ap[:, 8:16], in_=wk.ap[:]), reads=[wk], writes=[m8])
                p.op("dve", lambda e: e.tensor_scalar(out=selp.ap[:, 64:320], in0=imp.ap[:], scalar1=m8.ap[:, 15:16],
                                                      scalar2=None, op0=ALU.is_ge), reads=[imp, m8], writes=[selp])
                for w in range(nwin):
                    ms = nxt("misc", MISC)
                    p.op("pe", lambda e: e.matmul(ms.ap[:, 0:128], lhsT=selp.ap[:, 64 * w:64 * w + 128],
                                                  rhs=ident.ap[:, :], start=True, stop=True),
                         reads=[selp, ident], writes=[ms])
                    p.op("dve", lambda e: e.tensor_scalar(
                        out=rw.ap[64:128, w, :].rearrange("p (i q) -> p i q", i=4),
                        in0=ms.ap[64:128, 0:128].unsqueeze(1).to_broadcast([64, 4, 128]),
                        scalar1=-1.0, scalar2=BIG, op0=ALU.add, op1=ALU.mult), reads=[ms], writes=[rm])
                oa = nxt("oa", OA)
                nkt = 8 * j + 8
                pend = None

                def sel_pv(kt_, pt_):
                    p.op("pe", lambda e: e.matmul(oa.ap[0:65, :], lhsT=VA.ap[:, kt_, g, :], rhs=pt_.ap[:, :],
                                                  start=(kt_ == 0), stop=(kt_ == nkt - 1)), reads=[pt_, VA], writes=[oa])
                pendq = []
                for kt in range(nkt):
                    s = nxt("s3", SA3)
                    amb = kt >= 8 * j
                    p.op("pe", lambda e: e.matmul(s.ap[:, :], lhsT=KA.ap[:, kt * 128:(kt + 1) * 128],
                                                  rhs=rw.ap[:, kt // 32, :], start=True, stop=(not amb)),
                         reads=[KA, rw, rm], writes=[s])
                    if amb:
                        rb = dmask.ap[:, kt - 8 * j, :].unsqueeze(1).to_broadcast([128, 4, 128])
                        p.op("pe", lambda e: e.matmul(s.ap[:, :], lhsT=ident.ap[:, :], rhs=rb, start=False, stop=True),
                             reads=[ident, dmask], writes=[s])
                    pt = nxt("pt", PT)
                    p.op("act", lambda e: e.activation(out=pt.ap[:], in_=s.ap[:], func=AF.Exp, scale=SCALE),
                         reads=[s], writes=[pt])
                    pendq.append((kt, pt))
                    if len(pendq) > 2:
                        sel_pv(*pendq.pop(0))
                for x_ in pendq:
                    sel_pv(*x_)
                r, ov, oa = finish_heads(oa)
                p.op("dve", lambda e: e.tensor_tensor(out=gw.ap[:], in0=r.ap[:], in1=gates.ap[:, j, 8 + g * 4: 8 + g * 4 + 4],
                                                      op=ALU.mult), reads=[r, gates], writes=[gw])
                for i in range(4):
                    p.op("dve", lambda e: e.scalar_tensor_tensor(out=oacc.ap[:, i, :], in0=ov[:, i, 0:64],
                                                                 scalar=gw.ap[:, i:i + 1], in1=oacc.ap[:, i, :],
                                                                 op0=ALU.mult, op1=ALU.add),
                         reads=[oa, gw, oacc], writes=[oacc])
                oa = nxt("oa", OA)
                for w in range(5):
                    s = nxt("s", SA)
                    msk = []
                    if w == 0:
                        msk.append((band, band.ap[:, :]))
                    if w == 4:
                        msk.append((causal, causal.ap[:, :]))
                    if j == 0 and w < 4:
                        msk.append((wm0, wm0.ap[:, w, :]))
                    p.op("pe", lambda e: e.matmul(s.ap[:, :], lhsT=kw_.ap[0:64, w * 128:(w + 1) * 128],
                                                  rhs=rw.ap[0:64, 0, :], start=True, stop=(len(msk) == 0)),
                         reads=[kw_, rw], writes=[s])
                    for mi, (mt_, map_) in enumerate(msk):
                        rb = map_.unsqueeze(1).to_broadcast([128, 4, 128])
                        p.op("pe", lambda e: e.matmul(s.ap[:, :], lhsT=ident.ap[:, :], rhs=rb, start=False,
                                                      stop=(mi == len(msk) - 1)), reads=[ident, mt_], writes=[s])
                    pt = nxt("pt", PT)
                    p.op("act", lambda e: e.activation(out=pt.ap[:], in_=s.ap[:], func=AF.Exp, scale=SCALE),
                         reads=[s], writes=[pt])
                    p.op("pe", lambda e: e.matmul(oa.ap[0:65, :], lhsT=vw_.ap[:, w, g, :], rhs=pt.ap[:, :],
                                                  start=(w == 0), stop=(w == 4)), reads=[pt, vw_], writes=[oa])
                r, ov, oa = finish_heads(oa)
                p.op("dve", lambda e: e.tensor_tensor(out=gw.ap[:], in0=r.ap[:], in1=gates.ap[:, j, 16 + g * 4: 16 + g * 4 + 4],
                                                      op=ALU.mult), reads=[r, gates], writes=[gw])
                for i in range(4):
                    p.op("dve", lambda e: e.scalar_tensor_tensor(out=oacc.ap[:, i, :], in0=ov[:, i, 0:64],
                                                                 scalar=gw.ap[:, i:i + 1], in1=oacc.ap[:, i, :],
                                                                 op0=ALU.mult, op1=ALU.add),
                         reads=[oa, gw, oacc], writes=[oacc])
                p.op("dve", lambda e: e.tensor_tensor(out=OG.ap[:, j, g * 256:(g + 1) * 256],
                                                      in0=oacc.ap[:].rearrange("p h f -> p (h f)"),
                                                      in1=z.ap[:, 0:256], op=ALU.mult), reads=[oacc, z], writes=[OG])
    if "nsa" in stages:
        with p.scope():
            stage_nsa()
    else:
        p.op("pool", lambda e: e.memset(OG.ap[:, :, 0:512], 0.0), writes=[OG])

    def stage_moba():
        kmean = p.sbuf("kmean", [64, 64], F32)
        kmb = p.sbuf("kmb", [64, 64], BF16)
        mabt = [p.sbuf(f"mabt{i}", [128, 4, 3, 64], F32) for i in range(2)]
        gt = p.sbuf("gt", [128, 4, 64], F32)
        m8m = p.sbuf("m8m", [128, 4, 8], F32)
        selmm = p.sbuf("selmm", [128, 4, 64], F32)
        selmb = p.sbuf("selmb", [128, 4, 128], BF16)
        p.op("pool", lambda e: e.memset(selmb.ap[:], 0.0), writes=[selmb])
        QTm = [p.trk(name=f"QTm{G}") for G in range(4)]
        for q4 in range(4):
            p.dma("sp" if q4 % 2 == 0 else "pool", KA.ap[64:128, q4 * 4096:(q4 + 1) * 4096],
                  d["epm"][:, q4 * 4096:(q4 + 1) * 4096], writes=[KA])
        for h in range(2 * DBG.get('nhp', 4)):
            hp, hh = h // 2, h % 2
            if hh == 0:
                for q4 in range(4):
                    p.dma("sp" if q4 % 2 == 0 else "pool", VA.ap[:, q4 * 32:(q4 + 1) * 32],
                          d["mv"][hp][:, q4 * 32:(q4 + 1) * 32], writes=[VA])
            for q4 in range(4):
                p.dma("sp" if q4 % 2 == 0 else "pool", KA.ap[0:64, q4 * 4096:(q4 + 1) * 4096],
                      d["mkT"][h][:, q4 * 4096:(q4 + 1) * 4096], writes=[KA])
            p.dma("sp", QT.ap[0:64, 0:2048], d["mq"][h], writes=[QT])
            p.op("dve", lambda e: e.tensor_reduce(out=kmean.ap[:], in_=KA.ap[0:64, :].rearrange("p (n s) -> p n s", s=256),
                                                  axis=AX.X, op=ALU.add), reads=[KA], writes=[kmean])
            p.op("dve", lambda e: e.tensor_scalar(out=kmb.ap[:], in0=kmean.ap[:], scalar1=1.0 / 256, scalar2=None,
                                                  op0=ALU.mult), reads=[kmean], writes=[kmb])
            for G in range(DBG.get('ng', 4)):
                j0 = G * 4
                mab = mabt[(h * 4 + G) % 2]
                p.dma("sp", mab.ap[:], d["mab"][G], writes=[mab])
                z = nxt("z", zt)
                p.dma("sp", z.ap[:, 0:256].rearrange("p (r f) -> p r f", r=4),
                      d["zs"][j0:j0 + 4, :, 512 + h * 64: 512 + (h + 1) * 64].rearrange("r p f -> p r f"),
                      writes=[z], allow_slow_non_contiguous=True)
                ms = nxt("misc", MISC)
                for r_ in range(4):
                    p.op("pe", lambda e: e.matmul(ms.ap[:, r_ * 64:(r_ + 1) * 64],
                                                  lhsT=QT.ap[0:64, (j0 + r_) * 128:(j0 + r_ + 1) * 128],
                                                  rhs=kmb.ap[:, :], start=True, stop=True),
                         reads=[QT, kmb], writes=[ms])
                msv = ms.ap[:, 0:256].rearrange("p (r n) -> p r n", r=4)
                p.op("dve", lambda e: e.tensor_tensor(out=gt.ap[:], in0=msv, in1=mab.ap[:, :, 0, :], op=ALU.mult),
                     reads=[ms, mab], writes=[gt])
                p.op("dve", lambda e: e.tensor_tensor(out=gt.ap[:], in0=gt.ap[:], in1=mab.ap[:, :, 1, :], op=ALU.add),
                     reads=[gt, mab], writes=[gt])
                for r_ in range(4):
                    p.op("dve", lambda e: e.max(out=m8m.ap[:, r_, :], in_=gt.ap[:, r_, :]), reads=[gt], writes=[m8m])
                for r_ in range(4):
                    p.op("dve", lambda e: e.tensor_scalar(out=selmm.ap[:, r_, :], in0=gt.ap[:, r_, :],
                                                          scalar1=m8m.ap[:, r_, 2:3], scalar2=None, op0=ALU.is_ge),
                         reads=[gt, m8m], writes=[selmm])
                p.op("dve", lambda e: e.tensor_tensor(out=selmm.ap[:], in0=selmm.ap[:], in1=mab.ap[:, :, 0, :],
                                                      op=ALU.mult), reads=[selmm, mab], writes=[selmm])
                p.op("dve", lambda e: e.tensor_tensor(out=selmb.ap[:, :, 64:128], in0=selmm.ap[:], in1=mab.ap[:, :, 2, :],
                                                      op=ALU.add), reads=[selmm, mab], writes=[selmb])
                ms2 = nxt("misc", MISC)
                for r_ in range(4):
                    p.op("pe", lambda e: e.matmul(ms2.ap[:, r_ * 128:(r_ + 1) * 128], lhsT=selmb.ap[:, r_, :],
                                                  rhs=ident.ap[:, :], start=True, stop=True),
                         reads=[selmb, ident], writes=[ms2])
                p.op("dve", lambda e: e.tensor_scalar(out=QT.ap[64:128, j0 * 128:(j0 + 4) * 128], in0=ms2.ap[64:128, :],
                                                      scalar1=-1.0, scalar2=BIG, op0=ALU.add, op1=ALU.mult),
                     reads=[ms2], writes=[QTm[G]])
                oa = nxt("oa", OA)
                nkt = 8 * (j0 + 3) + 8
                pend = None

                def mo_pv(kt_, pt_, c0_):
                    p.op("pe", lambda e: e.matmul(oa.ap[0:65, c0_:512], lhsT=VA.ap[:, kt_, hh, :],
                                                  rhs=pt_.ap[:, c0_:512], start=(kt_ == 0), stop=(kt_ == nkt - 1)),
                         reads=[pt_, VA], writes=[oa])
                pendq = []
                for kt in range(nkt):
                    jlo = max(j0, kt // 8)
                    c0 = (jlo - j0) * 128
                    amb = kt // 8 >= j0
                    s = nxt("s3", SA3)
                    p.op("pe", lambda e: e.matmul(s.ap[:, c0:512], lhsT=KA.ap[:, kt * 128:(kt + 1) * 128],
                                                  rhs=QT.ap[:, j0 * 128 + c0:(j0 + 4) * 128],
                                                  start=True, stop=(not amb)), reads=[KA, QT, QTm[G]], writes=[s])
                    if amb:
                        p.op("pe", lambda e: e.matmul(s.ap[:, c0:c0 + 128], lhsT=ident.ap[:, :],
                                                      rhs=dmask.ap[:, kt % 8, :], start=False, stop=True),
                             reads=[ident, dmask], writes=[s])
                    pt = nxt("pt", PT)
                    p.op("act", lambda e: e.activation(out=pt.ap[:, c0:512], in_=s.ap[:, c0:512], func=AF.Exp,
                                                       scale=SCALE), reads=[s], writes=[pt])
                    pendq.append((kt, pt, c0))
                    if len(pendq) > 2:
                        mo_pv(*pendq.pop(0))
                for x_ in pendq:
                    mo_pv(*x_)
                r, ov, oa = finish_heads(oa)
                for r_ in range(4):
                    p.op("dve", lambda e: e.scalar_tensor_tensor(
                        out=OG.ap[:, j0 + r_, 512 + h * 64: 512 + (h + 1) * 64], in0=ov[:, r_, 0:64],
                        scalar=r.ap[:, r_:r_ + 1], in1=z.ap[:, r_ * 64:(r_ + 1) * 64], op0=ALU.mult, op1=ALU.mult),
                        reads=[oa, r, z], writes=[OG])
    if "moba" in stages:
        with p.scope():
            stage_moba()
    else:
        p.op("pool", lambda e: e.memset(OG.ap[:, :, 512:1024], 0.0), writes=[OG])

    if "mem" in stages:
        kmT = p.sbuf("kmT", [128, 2, 256], BF16)
        vm_aug = p.sbuf("vm_aug", [128, 2, 4, 65], BF16)
        emit_mem_kv(p, d, ident, MISC, kmT, vm_aug)
        emit_mem_attn(p, d, kmT, vm_aug, QT, SA, OA, PT, zt, tools, nxt, OG, 1024, 1024)
    else:
        p.op("pool", lambda e: e.memset(OG.ap[:, :, 1024:1280], 0.0), writes=[OG])

    if "out" in stages:
        emit_outproj(p, d, ident, OG, 1280, MISC, SA, KA, None)
    return OG


def finish_T(p, tools, oa, clamp=True):
    nxt = tools["nxt"]
    ot = nxt("otsb", tools["otsb"])
    p.op("act", lambda e: e.copy(out=ot.ap[:], in_=oa.ap[0:65, :]), reads=[oa], writes=[ot])
    ms = nxt("misc", tools["MISC"])
    for r_ in range(4):
        p.op("pe", lambda e: e.transpose(out=ms.ap[:, r_ * 65:(r_ + 1) * 65], in_=ot.ap[:, r_ * 128:(r_ + 1) * 128],
                                         identity=tools["identf"].ap[0:65, 0:65]),
             reads=[ot, tools["identf"]], writes=[ms])
    ov = ms.ap[:, 0:260].rearrange("p (h f) -> p h f", h=4)
    r = nxt("rin", tools["rin"])
    if clamp:
        p.op("dve", lambda e: e.tensor_scalar(out=r.ap[:, 0:4], in0=ov[:, :, 64], scalar1=1e-30, scalar2=None,
                                              op0=ALU.max), reads=[ms], writes=[r])
        p.op("dve", lambda e: e.reciprocal(out=r.ap[:, 0:4], in_=r.ap[:, 0:4]), reads=[r], writes=[r])
    else:
        p.op("dve", lambda e: e.reciprocal(out=r.ap[:, 0:4], in_=ov[:, :, 64]), reads=[ms], writes=[r])
    return r, ov, ms


def emit_mem_attn(p, d, kmT, vm_aug, QT, SA, OA, PT, zt, tools, nxt, OG, col0, zcol0):
    for hp in range(2):
        p.dma("sp", QT.ap[:, 0:2048], d["eqT"][hp], writes=[QT])
        for hh in range(2):
            pb = 64 * hh
            h = hp * 2 + hh
            for G in range(4):
                j0 = G * 4
                z = nxt("z", zt)
                p.dma("sp", z.ap[:, 0:256].rearrange("p (r f) -> p r f", r=4),
                      d["zs"][j0:j0 + 4, :, zcol0 + h * 64: zcol0 + (h + 1) * 64].rearrange("r p f -> p r f"),
                      writes=[z], allow_slow_non_contiguous=True)
                oa = nxt("oa", OA)
                for mt in range(2):
                    s = nxt("s", SA)
                    p.op("pe", lambda e: e.matmul(s.ap[:, :], lhsT=kmT.ap[pb:pb + 64, hp, mt * 128:(mt + 1) * 128],
                                                  rhs=QT.ap[pb:pb + 64, j0 * 128:(j0 + 4) * 128], start=True, stop=True),
                         reads=[kmT, QT], writes=[s])
                    pt = nxt("pt", PT)
                    p.op("act", lambda e: e.activation(out=pt.ap[:], in_=s.ap[:], func=AF.Exp, scale=SCALE),
                         reads=[s], writes=[pt])
                    p.op("pe", lambda e: e.matmul(oa.ap[0:65, :], lhsT=vm_aug.ap[:, mt, h, :], rhs=pt.ap[:, :],
                                                  start=(mt == 0), stop=(mt == 1)), reads=[pt, vm_aug], writes=[oa])
                r, ov, oa = finish_T(p, tools, oa, clamp=False)
                for r_ in range(4):
                    p.op("dve", lambda e: e.scalar_tensor_tensor(
                        out=OG.ap[:, j0 + r_, col0 + h * 64: col0 + (h + 1) * 64], in0=ov[:, r_, 0:64],
                        scalar=r.ap[:, r_:r_ + 1], in1=z.ap[:, r_ * 64:(r_ + 1) * 64], op0=ALU.mult, op1=ALU.mult),
                        reads=[oa, r, z], writes=[OG])


def emit_outproj(p, d, ident, OG, nfeat, MISC, SA, WB, final_g):
    nch = nfeat // 128
    wst = [p.sbuf(f"op_wst{i}", [128, DM], F32) for i in range(2)]
    Wv = WB.ap[:, 0:nch * DM].rearrange("p (c n) -> p c n", c=nch)
    wv = d["wout"].rearrange("(c p) n -> p c n", p=128)
    for c in range(nch):
        s = wst[c % 2]
        p.dma("sp", s.ap[:], wv[:, c, :], writes=[s])
        p.op("dve" if c % 2 == 0 else "pool", lambda e: e.tensor_copy(out=Wv[:, c, :], in_=s.ap[:]),
             reads=[s], writes=[WB])
    OGT = [p.sbuf(f"OGT{i}", [128, nch, 128], BF16) for i in range(2)]
    xt = [p.sbuf(f"op_x{i}", [128, DM], F32) for i in range(2)]
    if final_g is not None:
        junk = p.sbuf("op_junk", [128, DM], F32)
        ssq = [p.sbuf(f"op_ssq{i}", [128, 1], F32) for i in range(2)]
        epst = p.const_tile(EPS)
    xv = d["x"].rearrange("(n p) f -> n p f", p=128)
    ov = d["xo"].rearrange("(n p) f -> n p f", p=128)
    k = 0
    for j in range(NQT):
        ogt = OGT[j % 2]
        x_ = xt[j % 2]
        p.dma("sp", x_.ap[:], xv[j], writes=[x_])
        for c0 in range(0, nch, 8):
            cn = min(8, nch - c0)
            ms = MISC[k % 2]
            k += 1
            pv = ms.ap[:].bitcast(BF16)
            for cc in range(cn):
                p.op("pe", lambda e: e.transpose(out=pv[:, cc * 128:(cc + 1) * 128],
                                                 in_=OG.ap[:, j, (c0 + cc) * 128:(c0 + cc + 1) * 128], identity=ident.ap[:]),
                     reads=[OG, ident], writes=[ms])
            p.op("act", lambda e: e.copy(out=ogt.ap[:, c0:c0 + cn, :],
                                         in_=pv[:, 0:cn * 128].rearrange("p (c m) -> p c m", c=cn)),
                 reads=[ms], writes=[ogt])
        for nh in range(2):
            s = SA[nh]
            for c in range(nch):
                p.op("pe", lambda e: e.matmul(s.ap[:, :], lhsT=ogt.ap[:, c, :], rhs=Wv[:, c, nh * 512:(nh + 1) * 512],
                                              start=(c == 0), stop=(c == nch - 1)), reads=[ogt, WB], writes=[s])
            p.op("dve", lambda e: e.tensor_tensor(out=x_.ap[:, nh * 512:(nh + 1) * 512], in0=s.ap[:, :],
                                                  in1=x_.ap[:, nh * 512:(nh + 1) * 512], op=ALU.add),
                 reads=[s, x_], writes=[x_])
        if final_g is not None:
            sq = ssq[j % 2]
            p.op("dve", lambda e: e.memset(sq.ap[:], 0.0), writes=[sq])
            p.op("act", lambda e: e.activation(out=junk.ap[:], in_=x_.ap[:], func=AF.Square, accum_out=sq.ap[:]),
                 reads=[x_], writes=[junk, sq])
            p.op("act", lambda e: e.activation(out=sq.ap[:], in_=sq.ap[:], func=AF.Sqrt, bias=epst.ap[:], scale=1.0 / DM),
                 reads=[sq, epst], writes=[sq])
            p.op("dve", lambda e: e.reciprocal(out=sq.ap[:], in_=sq.ap[:]), reads=[sq], writes=[sq])
            p.op("dve", lambda e: e.scalar_tensor_tensor(out=x_.ap[:], in0=x_.ap[:], scalar=sq.ap[:, 0:1],
                                                         in1=final_g.ap[:], op0=ALU.mult, op1=ALU.mult),
                 reads=[x_, sq, final_g], writes=[x_])
        p.dma("sp", ov[j], x_.ap[:], reads=[x_], is_output=True)


P2_INPUTS = dict(
    ident=([128, 128], BF16), causal=([128, 128], BF16), band=([128, 128], BF16), dmask=([128, 8, 128], BF16),
    wm0=([128, 4, 128], BF16), cmask=([128, NQT, 2, 128], BF16), epn=([64, T], BF16),
    gates=([128, NQT, 24], F32), ovl=([128, 8, 256], BF16),
    w1k=([128, 32, 64], F32), w1v=([128, 32, 64], F32), w2k=([64, 128], F32), w2v=([64, 128], F32),
    pek=([128, 32], F32), pev=([128, 32], F32),
    nkcT=([128, T], BF16), nvcT=([128, T], BF16), ksT=([2, 64, T], BF16), vs=([128, 128, 2, 65], BF16),
    qn=([NQT, 2, 64, 512], BF16), ab=([NQT, 128, 2, 256], F32), kw=([NQT, 2, 64, 640], BF16),
    vw=([NQT, 128, 5, 2, 65], BF16), zs=([NQT, 128, 1280], F32),
    mkT=([8, 64, T], BF16), epm=([64, T], BF16), mv=([4, 128, 128, 2, 65], BF16), mq=([8, 64, 2048], BF16),
    mab=([4, 128, 4, 3, 64], F32),
    eqT=([2, 128, 2048], BF16), memx=([256, DM], F32), memg=([DM], F32), wkv=([DM, 512], F32),
    wout=([1280, DM], F32), x=([TPC, DM], F32),
)


def build_attn(stages=("nsa", "moba", "mem", "out"), debug_og=False):
    nc = bass.Bass("TRN2", target_bir_lowering=False)
    d = {}
    for k, (shp, dt) in P2_INPUTS.items():
        d[k] = nc.dram_tensor(k, shp, dt, kind="ExternalInput").ap()
    d["xo"] = nc.dram_tensor("xo", [TPC, DM], F32, kind="ExternalOutput").ap()
    if debug_og:
        d["og"] = nc.dram_tensor("og", [128, NQT, 1280], BF16, kind="ExternalOutput").ap()
    with ExitStack() as es:
        p = Prog(nc, es)
        p.const_tile(EPS)
        try:
            OG = emit_attn_phase(p, nc, d, stages)
            if debug_og:
                p.dma("sp", d["og"], OG.ap[:], reads=[OG], is_output=True)
        except StopEmit:
            p.es = p.es_perm
        p.finish()
    return nc


def _bf(a):
    return np.ascontiguousarray(a).astype(NPBF)


def core_tokens(c):
    return np.concatenate([np.arange((8 * j + c) * 128, (8 * j + c + 1) * 128) for j in range(NQT)])


def static_tables():
    k = np.arange(128)[:, None]
    q = np.arange(128)[None, :]
    tb = {}
    tb["ident"] = np.eye(128, dtype=NPBF)
    tb["causal"] = _bf(np.where(k <= q, 0.0, -BIG))
    tb["band"] = _bf(np.where(k > q, 0.0, -BIG))
    tb["epn"] = _bf((np.arange(64)[:, None] == ((np.arange(T)[None, :] // 64) % 64)).astype(np.float32))
    n = np.arange(1024)
    s = np.arange(256)
    cs = n * 16
    ov = ((cs[:, None] < s[None, :] * 64 + 64) & (cs[:, None] + 32 > s[None, :] * 64)).astype(np.float32)
    ov[1023] = 0
    tb["ovl"] = _bf(ov.reshape(8, 128, 256).transpose(1, 0, 2))
    tb["epm"] = _bf((np.arange(64)[:, None] == (np.arange(T)[None, :] // 256)).astype(np.float32))
    return tb


def core_tables(c):
    k = np.arange(128)[:, None]
    q = np.arange(128)[None, :]
    tb = {}
    dm = np.zeros((128, 8, 128), np.float32)
    for a in range(8):
        if a == c:
            dm[:, a, :] = np.where(k <= q, 0.0, -BIG)
        elif a > c:
            dm[:, a, :] = -BIG
    tb["dmask"] = _bf(dm)
    wm = np.zeros((128, 4, 128), np.float32)
    for w in range(4):
        if c - 4 + w < 0:
            wm[:, w, :] = -BIG
    tb["wm0"] = _bf(wm)
    cm = np.zeros((128, NQT, 2, 128), np.float32)
    ab = np.zeros((NQT, 128, 2, 256), np.float32)
    mab = np.zeros((4, 128, 4, 3, 64), np.float32)
    s = np.arange(256)[None, :]
    nb = np.arange(64)
    for j in range(NQT):
        qt = 8 * j + c
        ntl = j // 2
        for wsel in range(2):
            nt = ntl - 1 + wsel
            if nt < 0:
                continue
            n = nt * 128 + k
            t = qt * 128 + q
            cm[:, j, wsel, :] = np.where((16 * n + 31 <= t) & (n < 1023), 0.0, -BIG)
        t = (qt * 128 + np.arange(128))[:, None]
        cur = t // 64
        forced = (s == cur) | (s == cur - 1)
        valid = s <= cur
        ab[j, :, 0, :] = np.where(valid & ~forced, 1.0, 0.0)
        b = np.zeros((128, 256), np.float32)
        b = np.where(s == cur, 1e9, b)
        b = np.where(s == cur - 1, 2e9, b)
        b = np.where(~valid, -1e30, b)
        ab[j, :, 1, :] = b
        curm = qt // 2
        mab[j // 4, :, j % 4, 0, :] = (nb < curm).astype(np.float32)[None]
        mab[j // 4, :, j % 4, 1, :] = np.where(nb < curm, 0.0, -1e30)[None]
        mab[j // 4, :, j % 4, 2, :] = (nb == curm).astype(np.float32)[None]
    tb["cmask"] = _bf(cm)
    tb["ab"] = ab
    tb["mab"] = mab
    return tb


def with_ones(a):
    return np.concatenate([a, np.ones(a.shape[:-1] + (1,), a.dtype)], -1)


def prep_attn_shared(PB, inputs):
    sh = {}
    sh["nkcT"] = np.ascontiguousarray(PB[:, PB0["nkc"]:PB0["nkc"] + 128].T)
    sh["nvcT"] = np.ascontiguousarray(PB[:, PB0["nvc"]:PB0["nvc"] + 128].T)
    sh["ksT"] = np.ascontiguousarray(PB[:, PB0["nks"]:PB0["nks"] + 128].T).reshape(2, 64, T)
    v = PB[:, PB0["nvs"]:PB0["nvs"] + 128].reshape(128, 128, 2, 64).transpose(1, 0, 2, 3)
    sh["vs"] = np.ascontiguousarray(with_ones(v))
    sh["mkT"] = np.ascontiguousarray(PB[:, PB0["mk"]:PB0["mk"] + 512].reshape(T, 8, 64).transpose(1, 2, 0))
    v = PB[:, PB0["mv"]:PB0["mv"] + 512].reshape(128, 128, 4, 2, 64).transpose(2, 1, 0, 3, 4)
    sh["mv"] = np.ascontiguousarray(with_ones(v))
    for kind in ("k", "v"):
        w1 = inputs[f"l0_cmp_w1_{kind}"].transpose(1, 0, 2)
        sh["w1" + kind] = np.ascontiguousarray(np.concatenate([w1, w1], 0))
        w2 = inputs[f"l0_cmp_w2_{kind}"]
        sh["w2" + kind] = np.ascontiguousarray(np.concatenate([w2, w2], 1))
        pe = inputs[f"l0_cmp_pe_{kind}"].T
        sh["pe" + kind] = np.ascontiguousarray(np.concatenate([pe, pe], 0))
    sh["memx"] = np.ascontiguousarray(inputs["mem"][0])
    sh["memg"] = inputs["mem_norm_g"]
    sh["wkv"] = inputs["l0_w_mem_kv"]
    sh["wout"] = inputs["l0_w_out"]
    sh["kw_full"] = PB[:, PB0["nkw"]:PB0["nkw"] + 128]
    sh["vw_full"] = PB[:, PB0["nvw"]:PB0["nvw"] + 128]
    return sh


def prep_attn_core(c, PB, PF, x, sh, st):
    m = {}
    for k_ in ("nkcT", "nvcT", "ksT", "vs", "mkT", "mv", "w1k", "w1v", "w2k", "w2v", "pek", "pev",
               "memx", "memg", "wkv", "wout"):
        m[k_] = sh[k_]
    for k_ in ("ident", "causal", "band", "epn", "ovl", "epm"):
        m[k_] = st[k_]
    m.update(core_tables(c))
    tq = core_tokens(c)
    pbq = PB[tq]
    pfq = PF[tq]
    a = pbq[:, PB0["nq"]:PB0["nq"] + 512].reshape(NQT, 128, 2, 4, 64).transpose(0, 2, 4, 3, 1)
    m["qn"] = np.ascontiguousarray(a).reshape(NQT, 2, 64, 512)
    kw = np.zeros((NQT, 2, 64, 640), NPBF)
    vw = np.zeros((NQT, 128, 5, 2, 64), NPBF)
    for j in range(NQT):
        qt = 8 * j + c
        for w in range(5):
            kt = qt - 4 + w
            if kt < 0:
                continue
            kw[j, :, :, w * 128:(w + 1) * 128] = sh["kw_full"][kt * 128:(kt + 1) * 128].T.reshape(2, 64, 128)
            vw[j, :, w] = sh["vw_full"][kt * 128:(kt + 1) * 128].reshape(128, 2, 64)
    m["kw"] = kw
    m["vw"] = np.ascontiguousarray(with_ones(vw))
    m["gates"] = np.ascontiguousarray(pfq[:, 0:24].reshape(NQT, 128, 24).transpose(1, 0, 2))
    m["zs"] = np.ascontiguousarray(pfq[:, 24:1304].reshape(NQT, 128, 1280))
    m["mq"] = np.ascontiguousarray(pbq[:, PB0["mq"]:PB0["mq"] + 512].reshape(2048, 8, 64).transpose(1, 2, 0))
    m["eqT"] = np.ascontiguousarray(pbq[:, PB0["eq"]:PB0["eq"] + 256].reshape(2048, 2, 128).transpose(1, 2, 0))
    m["x"] = np.ascontiguousarray(x[tq])
    return m


NPRE = 112
RET_G = [1.0 - 2.0 ** (-5.0 - h) for h in range(4)]


def emit_ret_phase(p, nc, d):
    ident = p.sbuf("ident", [128, 128], BF16)
    p.dma("sp", ident.ap[:], d["ident"][:, :], writes=[ident])
    B = [p.psum(f"B{i}", [128, 512]) for i in range(8)]
    OG = p.sbuf("OG", [128, NQT, 2304], BF16)
    WB = p.sbuf("WB", [128, 18 * DM], BF16)
    QT = p.sbuf("QT", [128, 2048], BF16)
    Sf = p.sbuf("Sf", [128, 4, 2, 512], F32)
    Sb = p.sbuf("Sb", [128, 4, 2, 512], BF16)
    dt = p.sbuf("dt", [128, 4, 128], F32)
    kdec = p.sbuf("kdec", [128, 4], F32)
    qdec = p.sbuf("qdec", [128, 4, 128], F32)
    for nm, t in (("dt", dt), ("kdec", kdec), ("qdec", qdec)):
        p.dma("sp", t.ap[:], d[nm], writes=[t])
    epst = p.const_tile(EPS)
    with p.scope():
        scp = p.sbuf("scp", [128, NPRE, 4], F32)
        p.dma("sp", scp.ap[:], d["scp"], writes=[scp])
        kpt = [p.sbuf(f"kpt{i}", [128, 4, 256], BF16) for i in range(4)]
        vpt = [p.sbuf(f"vpt{i}", [128, 2048], BF16) for i in range(4)]
        kps = [p.sbuf(f"kps{i}", [128, 4, 256], BF16) for i in range(4)]
        for j in range(NPRE):
            k_, v_, ks_ = kpt[j % 4], vpt[j % 4], kps[j % 4]
            p.dma("sp", k_.ap[:], d["kp"][j].rearrange("p (h f) -> p h f", h=4), writes=[k_])
            p.dma("act", v_.ap[:], d["vp"][j], writes=[v_])
            p.op("dve", lambda e: e.tensor_tensor(out=ks_.ap[:], in0=k_.ap[:],
                                                  in1=scp.ap[:, j, :].unsqueeze(2).to_broadcast([128, 4, 256]),
                                                  op=ALU.mult), reads=[k_, scp], writes=[ks_])
            for h in range(4):
                for dc in range(2):
                    bk = B[h * 2 + dc]
                    p.op("pe", lambda e: e.matmul(bk.ap[:, :], lhsT=ks_.ap[:, h, dc * 128:(dc + 1) * 128],
                                                  rhs=v_.ap[:, h * 512:(h + 1) * 512], start=(j == 0),
                                                  stop=(j == NPRE - 1)), reads=[ks_, v_], writes=[bk])
        for h in range(4):
            for dc in range(2):
                bk = B[h * 2 + dc]
                p.op("act", lambda e: e.copy(out=Sf.ap[:, h, dc, :], in_=bk.ap[:, :]), reads=[bk], writes=[Sf])
        p.op("dve", lambda e: e.tensor_copy(out=Sb.ap[:], in_=Sf.ap[:]), reads=[Sf], writes=[Sb])
    with p.scope():
        qTt = [p.sbuf(f"qTt{i}", [128, 4, 2, 128], BF16) for i in range(2)]
        kTt = [p.sbuf(f"kTt{i}", [128, 4, 2, 128], BF16) for i in range(2)]
        ktt = [p.sbuf(f"ktt{i}", [128, 4, 256], BF16) for i in range(2)]
        vtt = [p.sbuf(f"vtt{i}", [128, 2048], BF16) for i in range(2)]
        zt = [p.sbuf(f"rzt{i}", [128, 2048], F32) for i in range(2)]
        Ab = [p.sbuf(f"Ab{i}", [128, 128], BF16) for i in range(2)]
        qs = [p.sbuf(f"qs{i}", [128, 2, 128], BF16) for i in range(2)]
        ks2 = [p.sbuf(f"ks2{i}", [128, 256], BF16) for i in range(2)]
        junk = p.sbuf("rjunk", [128, 512], F32)
        tmpn = p.sbuf("tmpn", [128, 512], F32)
        st = [p.sbuf(f"rst{i}", [128, 4], F32) for i in range(2)]
        AT = B[2]
        Ot = [B[3], B[4]]
        SU = [B[5], B[6]]
        k = 0
        for n in range(NQT):
            q_, kT_, kt_, v_, z_ = qTt[n % 2], kTt[n % 2], ktt[n % 2], vtt[n % 2], zt[n % 2]
            p.dma("sp", q_.ap[:], d["qT"][n], writes=[q_])
            p.dma("sp", kT_.ap[:], d["kT"][n], writes=[kT_])
            p.dma("pool", kt_.ap[:], d["kt"][n].rearrange("p (h f) -> p h f", h=4), writes=[kt_])
            p.dma("pool", v_.ap[:], d["v"][n], writes=[v_])
            p.dma("sp", z_.ap[:], d["zs"][n][:, 0:2048], writes=[z_])
            for h in range(4):
                ab_, qs_, ks_, s_ = Ab[k % 2], qs[k % 2], ks2[k % 2], st[k % 2]
                O = Ot[k % 2]
                k += 1
                for dc in range(2):
                    p.op("pe", lambda e: e.matmul(AT.ap[:, 0:128], lhsT=kT_.ap[:, h, dc, :], rhs=q_.ap[:, h, dc, :],
                                                  start=(dc == 0), stop=(dc == 1)), reads=[kT_, q_], writes=[AT])
                p.op("dve", lambda e: e.tensor_tensor(out=ab_.ap[:], in0=AT.ap[:, 0:128], in1=dt.ap[:, h, :],
                                                      op=ALU.mult), reads=[AT, dt], writes=[ab_])
                p.op("pool", lambda e: e.tensor_tensor(out=qs_.ap[:], in0=q_.ap[:, h, :, :],
                                                       in1=qdec.ap[:, h, :].unsqueeze(1).to_broadcast([128, 2, 128]),
                                                       op=ALU.mult), reads=[q_, qdec], writes=[qs_])
                p.op("pe", lambda e: e.matmul(O.ap[:, :], lhsT=ab_.ap[:, :], rhs=v_.ap[:, h * 512:(h + 1) * 512],
                                              start=True, stop=False), reads=[ab_, v_], writes=[O])
                for dc in range(2):
                    p.op("pe", lambda e: e.matmul(O.ap[:, :], lhsT=qs_.ap[:, dc, :], rhs=Sb.ap[:, h, dc, :],
                                                  start=False, stop=(dc == 1)), reads=[qs_, Sb], writes=[O])
                p.op("dve", lambda e: e.tensor_scalar(out=ks_.ap[:], in0=kt_.ap[:, h, :], scalar1=kdec.ap[:, h:h + 1],
                                                      scalar2=None, op0=ALU.mult), reads=[kt_, kdec], writes=[ks_])
                cd = float(RET_G[h] ** 128)
                for dc in range(2):
                    su = SU[dc]
                    p.op("pe", lambda e: e.matmul(su.ap[:, :], lhsT=ks_.ap[:, dc * 128:(dc + 1) * 128],
                                                  rhs=v_.ap[:, h * 512:(h + 1) * 512], start=True, stop=True),
                         reads=[ks_, v_], writes=[su])
                    p.op("dve", lambda e: e.scalar_tensor_tensor(out=Sf.ap[:, h, dc, :], in0=Sf.ap[:, h, dc, :],
                                                                 scalar=cd, in1=su.ap[:, :], op0=ALU.mult, op1=ALU.add),
                         reads=[Sf, su], writes=[Sf])
                p.op("act", lambda e: e.copy(out=Sb.ap[:, h, :, :], in_=Sf.ap[:, h, :, :]), reads=[Sf], writes=[Sb])
                p.op("dve", lambda e: e.memset(s_.ap[:], 0.0), writes=[s_])
                p.op("act", lambda e: e.activation(out=junk.ap[:], in_=O.ap[:], func=AF.Identity,
                                                   accum_out=s_.ap[:, 0:1]), reads=[O], writes=[junk, s_])
                p.op("act", lambda e: e.activation(out=junk.ap[:], in_=O.ap[:], func=AF.Square,
                                                   accum_out=s_.ap[:, 1:2]), reads=[O], writes=[junk, s_])
                p.op("dve", lambda e: e.tensor_scalar(out=s_.ap[:, 0:1], in0=s_.ap[:, 0:1], scalar1=1.0 / 512,
                                                      scalar2=None, op0=ALU.mult), reads=[s_], writes=[s_])
                p.op("dve", lambda e: e.tensor_tensor(out=s_.ap[:, 2:3], in0=s_.ap[:, 0:1], in1=s_.ap[:, 0:1],
                                                      op=ALU.mult), reads=[s_], writes=[s_])
                p.op("dve", lambda e: e.scalar_tensor_tensor(out=s_.ap[:, 3:4], in0=s_.ap[:, 1:2], scalar=1.0 / 512,
                                                             in1=s_.ap[:, 2:3], op0=ALU.mult, op1=ALU.subtract),
                     reads=[s_], writes=[s_])
                p.op("act", lambda e: e.activation(out=s_.ap[:, 3:4], in_=s_.ap[:, 3:4], func=AF.Sqrt,
                                                   bias=epst.ap[:], scale=1.0), reads=[s_, epst], writes=[s_])
                p.op("dve", lambda e: e.reciprocal(out=s_.ap[:, 3:4], in_=s_.ap[:, 3:4]), reads=[s_], writes=[s_])
                p.op("dve", lambda e: e.tensor_scalar(out=tmpn.ap[:], in0=O.ap[:], scalar1=s_.ap[:, 0:1],
                                                      scalar2=s_.ap[:, 3:4], op0=ALU.subtract, op1=ALU.mult),
                     reads=[O, s_], writes=[tmpn])
                p.op("pool", lambda e: e.tensor_tensor(out=OG.ap[:, n, h * 512:(h + 1) * 512], in0=tmpn.ap[:],
                                                       in1=z_.ap[:, h * 512:(h + 1) * 512], op=ALU.mult),
                     reads=[tmpn, z_], writes=[OG])
    SA = [B[0], B[1]]
    OA = [B[2], B[3]]
    MISC = [B[4], B[5]]
    PT = [p.sbuf(f"PT{i}", [128, 512], BF16) for i in range(3)]
    ztm = [p.sbuf(f"zt{i}", [128, 512], F32) for i in range(2)]
    rin = [p.sbuf(f"rin{i}", [128, 4], F32) for i in range(4)]
    identf = p.sbuf("identf", [128, 128], F32)
    p.op("dve", lambda e: e.tensor_copy(out=identf.ap[:], in_=ident.ap[:]), reads=[ident], writes=[identf])
    otsb = [p.sbuf(f"otsb{i}", [65, 512], F32) for i in range(2)]
    cnt = {}

    def nxt(key, lst):
        t = lst[cnt.get(key, 0) % len(lst)]
        cnt[key] = cnt.get(key, 0) + 1
        return t
    tools = dict(identf=identf, otsb=otsb, MISC=MISC, rin=rin, nxt=nxt)
    kmT = p.sbuf("kmT", [128, 2, 256], BF16)
    vm_aug = p.sbuf("vm_aug", [128, 2, 4, 65], BF16)
    fg = p.sbuf("fg", [128, DM], F32)
    p.dma("sp", fg.ap[:], d["fg"].partition_broadcast(128), writes=[fg])
    emit_mem_kv(p, d, ident, MISC, kmT, vm_aug)
    emit_mem_attn(p, d, kmT, vm_aug, QT, SA, OA, PT, ztm, tools, nxt, OG, 2048, 2048)
    emit_outproj(p, d, ident, OG, 2304, MISC, SA, WB, fg)


P4_INPUTS = dict(
    ident=([128, 128], BF16), dt=([128, 4, 128], F32), kdec=([128, 4], F32), qdec=([128, 4, 128], F32),
    scp=([128, NPRE, 4], F32), kp=([NPRE, 128, 1024], BF16), vp=([NPRE, 128, 2048], BF16),
    qT=([NQT, 128, 4, 2, 128], BF16), kT=([NQT, 128, 4, 2, 128], BF16), kt=([NQT, 128, 1024], BF16),
    v=([NQT, 128, 2048], BF16), zs=([NQT, 128, 2304], F32), eqT=([2, 128, 2048], BF16),
    memx=([256, DM], F32), memg=([DM], F32), wkv=([DM, 512], F32), wout=([2304, DM], F32),
    x=([TPC, DM], F32), fg=([DM], F32),
)


def build_ret():
    nc = bass.Bass("TRN2", target_bir_lowering=False)
    d = {}
    for k, (shp, dt_) in P4_INPUTS.items():
        d[k] = nc.dram_tensor(k, shp, dt_, kind="ExternalInput").ap()
    d["xo"] = nc.dram_tensor("xo", [TPC, DM], F32, kind="ExternalOutput").ap()
    with ExitStack() as es:
        p = Prog(nc, es)
        p.const_tile(EPS)
        emit_ret_phase(p, nc, d)
        p.finish()
    return nc


def ret_tables(c):
    g = np.array(RET_G, np.float64)
    i = np.arange(128, dtype=np.float64)
    tb = {}
    diff = i[None, :] - i[:, None]
    dtab = np.where(diff[:, None, :] >= 0, g[None, :, None] ** np.maximum(diff[:, None, :], 0.0), 0.0) / 16.0
    tb["dt"] = dtab.astype(np.float32)
    kd = g[None, :] ** (127.0 - i[:, None])
    tb["kdec"] = (kd / 16.0).astype(np.float32)
    qd = g[:, None] ** (i[None, :] + 1.0)
    tb["qdec"] = np.ascontiguousarray(np.broadcast_to(qd[None], (128, 4, 128))).astype(np.float32)
    scp = np.zeros((128, NPRE, 4), np.float64)
    J = 16 * c
    for j in range(min(J, NPRE)):
        scp[:, j, :] = kd / 16.0 * (g[None, :] ** (128.0 * (J - 1 - j)))
    tb["scp"] = scp.astype(np.float32)
    return tb


def prep_ret_shared(PB, inputs):
    sh = {}
    sh["kp"] = np.ascontiguousarray(PB[:NPRE * 128, PB1["rk"]:PB1["rk"] + 1024].reshape(NPRE, 128, 1024))
    sh["vp"] = np.ascontiguousarray(PB[:NPRE * 128, PB1["rv"]:PB1["rv"] + 2048].reshape(NPRE, 128, 2048))
    sh["memx"] = np.ascontiguousarray(inputs["mem"][0])
    sh["memg"] = inputs["mem_norm_g"]
    sh["wkv"] = inputs["l1_w_mem_kv"]
    sh["wout"] = inputs["l1_w_out"]
    sh["fg"] = inputs["final_norm_g"]
    sh["ident"] = np.eye(128, dtype=NPBF)
    return sh


def prep_ret_core(c, PB, PF, x1, sh):
    m = dict(sh)
    m.update(ret_tables(c))
    t0 = c * TPC
    pb = PB[t0:t0 + TPC]
    pf = PF[t0:t0 + TPC]
    for nm, off in (("qT", PB1["rq"]), ("kT", PB1["rk"])):
        a = pb[:, off:off + 1024].reshape(NQT, 128, 4, 2, 128).transpose(0, 4, 2, 3, 1)
        m[nm] = np.ascontiguousarray(a)
    m["kt"] = np.ascontiguousarray(pb[:, PB1["rk"]:PB1["rk"] + 1024].reshape(NQT, 128, 1024))
    m["v"] = np.ascontiguousarray(pb[:, PB1["rv"]:PB1["rv"] + 2048].reshape(NQT, 128, 2048))
    m["zs"] = np.ascontiguousarray(pf.reshape(NQT, 128, 2304))
    m["eqT"] = np.ascontiguousarray(pb[:, PB1["eq"]:PB1["eq"] + 256].reshape(TPC, 2, 128).transpose(1, 2, 0))
    m["x"] = np.ascontiguousarray(x1[t0:t0 + TPC])
    return m


def _proj_maps(cfg, x, pos, g, w, invf, gb=None):
    maps = []
    for c in range(NCORE):
        m = {"x": np.ascontiguousarray(x[c * TPC:(c + 1) * TPC]), "pos": np.ascontiguousarray(pos[c * TPC:(c + 1) * TPC]),
             "g": g, "w": w, "invf": np.ascontiguousarray(np.broadcast_to(invf[None], (128, cfg["nfreq"]))),
             "ident": np.eye(128, dtype=NPBF)}
        if gb is not None:
            m["gb"] = gb
        maps.append(m)
    return maps


_NC_CACHE = {}


def _get(name, fn):
    if name not in _NC_CACHE:
        _NC_CACHE[name] = fn()
    return _NC_CACHE[name]


def kernel(**inputs):
    inputs = {k: np.asarray(v) for k, v in inputs.items()}
    x = np.ascontiguousarray(inputs["x"][0], dtype=np.float32)
    pos = np.ascontiguousarray(inputs["positions"][0]).astype(np.int32)
    cores = list(range(NCORE))
    invf0 = (1.0 / (10000.0 ** (np.arange(0, 64, 2, dtype=np.float32) / np.float32(64)))).astype(np.float32)
    invf1 = (1.0 / (10000.0 ** np.linspace(0.0, 1.0, 128, dtype=np.float32))).astype(np.float32)
    nc1 = _get("p1", lambda: build_proj(CFG0))
    r1 = run_bass_kernel_spmd(nc1, _proj_maps(CFG0, x, pos, inputs["l0_norm_g"], inputs["l0_w_in"], invf0,
                                              inputs["l0_nsa_gate_b"]), core_ids=cores).results
    PB = np.concatenate([r["pb"] for r in r1], 0)
    PF = np.concatenate([r["pf"] for r in r1], 0)
    nc2 = _get("p2", lambda: build_attn())
    sh = prep_attn_shared(PB, inputs)
    st = static_tables()
    r2 = run_bass_kernel_spmd(nc2, [prep_attn_core(c, PB, PF, x, sh, st) for c in cores], core_ids=cores).results
    x1 = np.empty((T, DM), np.float32)
    for c in cores:
        x1[core_tokens(c)] = r2[c]["xo"]
    nc3 = _get("p3", lambda: build_proj(CFG1))
    r3 = run_bass_kernel_spmd(nc3, _proj_maps(CFG1, x1, pos, inputs["l1_norm_g"], inputs["l1_w_in"], invf1),
                              core_ids=cores).results
    PB_1 = np.concatenate([r["pb"] for r in r3], 0)
    PF_1 = np.concatenate([r["pf"] for r in r3], 0)
    nc4 = _get("p4", lambda: build_ret())
    sh4 = prep_ret_shared(PB_1, inputs)
    r4 = run_bass_kernel_spmd(nc4, [prep_ret_core(c, PB_1, PF_1, x1, sh4) for c in cores], core_ids=cores).results
    out = np.concatenate([r["xo"] for r in r4], 0)
    return out[None].astype(np.float32)
```

```python
import numpy as np
import ml_dtypes
from contextlib import ExitStack
import concourse.bass as bass
import concourse.mybir as mybir
from concourse.bass_utils import run_bass_kernel_spmd

F32 = mybir.dt.float32
BF16 = mybir.dt.bfloat16
I32 = mybir.dt.int32
AF = mybir.ActivationFunctionType
ALU = mybir.AluOpType
AX = mybir.AxisListType
NPBF = ml_dtypes.bfloat16

SEM_ROT = 20000


class Trk:
    __slots__ = ("w", "r", "ap", "name")

    def __init__(self, ap=None, name=""):
        self.w = None
        self.r = {}
        self.ap = ap
        self.name = name


class Prog:
    def __init__(self, nc, es, n_dma_sems=24):
        self.nc = nc
        self.es = es
        self.es_perm = es
        self.eng = {"pe": nc.tensor, "dve": nc.vector, "act": nc.scalar,
                    "pool": nc.gpsimd, "sp": nc.sync}
        self.sems = {}
        self.cur = {}
        self.seen = {e: {} for e in self.eng}
        self.nsem = 0
        for e in self.eng:
            self._new_eng_sem(e)
        self.dma_pool = {}
        self.dma_idx = {}
        for q in ("sp", "act", "pool"):
            self.dma_pool[q] = []
            for i in range(n_dma_sems if q == "sp" else 8):
                k = ("dma", q, i)
                self.sems[k] = es.enter_context(nc.semaphore(f"d_{q}_{i}"))
                self.dma_pool[q].append([k, 0])
            self.dma_idx[q] = 0
        self.out_events = []
        self.ninst = 0

    def _new_eng_sem(self, e):
        k = ("eng", e, self.nsem)
        self.nsem += 1
        self.sems[k] = self.es_perm.enter_context(self.nc.semaphore(f"s_{e}_{self.nsem}"))
        self.cur[e] = [k, 0]

    def _uniq(self, name):
        used = self.__dict__.setdefault("_names", {})
        n = used.get(name, 0)
        used[name] = n + 1
        return name if n == 0 else f"{name}_v{n}"

    def sbuf(self, name, shape, dtype, perm=False):
        name = self._uniq(name)
        t = (self.es_perm if perm else self.es).enter_context(self.nc.sbuf_tensor("sb_" + name, list(shape), dtype))
        return Trk(t, name)

    def psum(self, name, shape, dtype=F32):
        name = self._uniq("ps_" + name)[3:]
        t = self.es.enter_context(self.nc.psum_tensor("ps_" + name, list(shape), dtype))
        return Trk(t, name)

    def trk(self, ap=None, name=""):
        return Trk(ap, name)

    def const_tile(self, val):
        if not hasattr(self, "_cb"):
            self._cb = {}
        if val not in self._cb:
            t = self.sbuf(f"cb{len(self._cb)}", [128, 1], F32, perm=True)
            self.op("pool", lambda e: e.memset(t.ap[:], val), writes=[t])
            self._cb[val] = t
        return self._cb[val]

    def _wait(self, e, ev):
        if ev is None:
            return
        k, v = ev
        if self.seen[e].get(k, 0) >= v:
            return
        self.eng[e].wait_ge(self.sems[k], v)
        self.seen[e][k] = v

    def _deps(self, e, reads, writes, same_eng_key):
        for t in reads:
            if t.w is not None:
                self._wait(e, t.w)
        for t in writes:
            if t.w is not None and t.w[0] != same_eng_key:
                self._wait(e, t.w)
            for k, v in t.r.items():
                if k != same_eng_key:
                    self._wait(e, (k, v))

    def op(self, e, fn, reads=(), writes=()):
        if DBG.get("stopped"):
            return None
        ck = self.cur[e]
        if ck[1] >= SEM_ROT:
            self._new_eng_sem(e)
            ck = self.cur[e]
        self._deps(e, reads, writes, ck[0])
        ins = fn(self.eng[e])
        ck[1] += 1
        ins.then_inc(self.sems[ck[0]], 1)
        ev = (ck[0], ck[1])
        for t in reads:
            t.r[ck[0]] = ck[1]
        for t in writes:
            t.w = ev
            t.r = {}
        self.ninst += 1
        return ev

    def dma(self, q, out, in_, reads=(), writes=(), is_output=False, **kw):
        if DBG.get("stopped"):
            return None
        pool = self.dma_pool[q]
        slot = pool[self.dma_idx[q] % len(pool)]
        self.dma_idx[q] += 1
        k = slot[0]
        if slot[1] > 0:
            self._wait(q, (k, slot[1]))
        self._deps(q, reads, writes, None)
        ins = self.eng[q].dma_start(out=out, in_=in_, **kw)
        slot[1] += 16
        ins.then_inc(self.sems[k], 16)
        ev = (k, slot[1])
        for t in reads:
            t.r[k] = slot[1]
        for t in writes:
            t.w = ev
            t.r = {}
        if is_output:
            self.out_events.append(ev)
        self.ninst += 1
        return ev

    def barrier(self):
        if DBG.get("stopped"):
            return
        for e in self.eng:
            for e2 in self.eng:
                if e2 != e and e2 != "sp":
                    k, v = self.cur[e2]
                    if v > 0:
                        self._wait(e, (k, v))
            for q in self.dma_pool:
                for k, v in self.dma_pool[q]:
                    if v > 0:
                        self._wait(e, (k, v))

    def scope(self):
        prog = self

        class _Scope:
            def __enter__(self_):
                self_.old = prog.es
                self_.st = ExitStack()
                self_.st.__enter__()
                prog.es = self_.st
                return self_

            def __exit__(self_, *a):
                prog.barrier()
                prog.es = self_.old
                return self_.st.__exit__(*a)
        return _Scope()

    def finish(self):
        for q in self.dma_pool:
            for k, v in self.dma_pool[q]:
                if v > 0:
                    self._wait("sp", (k, v))
        for e in self.eng:
            k, v = self.cur[e]
            if v > 0 and e != "sp":
                self._wait("sp", (k, v))


T = 16384
DM = 1024
NCORE = 8
TPC = T // NCORE
NTT = TPC // 128
EPS = 1e-6
TWO_PI = float(2.0 * np.pi)
PI = float(np.pi)


def bcast_mid(ap2d, h):
    p, n = ap2d.shape
    return ap2d.unsqueeze(1).to_broadcast([p, h, n])


def emit_sincos(p, pos_i32, invf_bc, nfreq, ntt, cos_t, sin_t, tmp_t, posf_t):
    C1 = 6.28125
    C2 = float(np.float32(2.0 * np.pi - 6.28125))
    C3 = float(2.0 * np.pi - 6.28125 - np.float64(np.float32(2.0 * np.pi - 6.28125)))
    ki = p.sbuf("sc_ki", [128, ntt, nfreq], I32)
    kf = p.sbuf("sc_kf", [128, ntt, nfreq], F32)
    ang = p.sbuf("sc_ang", [128, ntt, nfreq], F32)
    m = p.sbuf("sc_m", [128, ntt, nfreq], F32)
    p.op("dve", lambda e: e.tensor_copy(out=posf_t.ap[:], in_=pos_i32.ap[:]),
         reads=[pos_i32], writes=[posf_t])
    for i in range(ntt):
        p.op("dve", lambda e: e.tensor_scalar(
            out=ang.ap[:, i, :], in0=invf_bc.ap[:], scalar1=posf_t.ap[:, i:i + 1],
            scalar2=None, op0=ALU.mult), reads=[invf_bc, posf_t], writes=[ang])
    p.op("dve", lambda e: e.tensor_scalar(out=ki.ap[:], in0=ang.ap[:], scalar1=1.0 / TWO_PI,
                                          scalar2=None, op0=ALU.mult), reads=[ang], writes=[ki])
    p.op("dve", lambda e: e.tensor_copy(out=kf.ap[:], in_=ki.ap[:]), reads=[ki], writes=[kf])
    r = tmp_t
    p.op("dve", lambda e: e.scalar_tensor_tensor(out=r.ap[:], in0=kf.ap[:], scalar=-C1, in1=ang.ap[:],
                                                 op0=ALU.mult, op1=ALU.add), reads=[kf, ang], writes=[r])
    for cc in (C2, C3):
        p.op("dve", lambda e: e.scalar_tensor_tensor(out=r.ap[:], in0=kf.ap[:], scalar=-cc, in1=r.ap[:],
                                                     op0=ALU.mult, op1=ALU.add), reads=[kf, r], writes=[r])

    def wrap(t):
        p.op("dve", lambda e: e.tensor_scalar(out=m.ap[:], in0=t.ap[:], scalar1=PI, scalar2=TWO_PI,
                                              op0=ALU.is_gt, op1=ALU.mult), reads=[t], writes=[m])
        p.op("dve", lambda e: e.tensor_tensor(out=t.ap[:], in0=t.ap[:], in1=m.ap[:], op=ALU.subtract),
             reads=[t, m], writes=[t])
        p.op("dve", lambda e: e.tensor_scalar(out=m.ap[:], in0=t.ap[:], scalar1=-PI, scalar2=TWO_PI,
                                              op0=ALU.is_lt, op1=ALU.mult), reads=[t], writes=[m])
        p.op("dve", lambda e: e.tensor_tensor(out=t.ap[:], in0=t.ap[:], in1=m.ap[:], op=ALU.add),
             reads=[t, m], writes=[t])
        p.op("dve", lambda e: e.tensor_scalar(out=t.ap[:], in0=t.ap[:], scalar1=PI, scalar2=-PI,
                                              op0=ALU.min, op1=ALU.max), reads=[t], writes=[t])
    wrap(r)
    p.op("act", lambda e: e.activation(out=sin_t.ap[:], in_=r.ap[:], func=AF.Sin),
         reads=[r], writes=[sin_t])
    p.op("dve", lambda e: e.tensor_scalar(out=r.ap[:], in0=r.ap[:], scalar1=PI / 2, scalar2=None,
                                          op0=ALU.add), reads=[r], writes=[r])
    wrap(r)
    p.op("act", lambda e: e.activation(out=cos_t.ap[:], in_=r.ap[:], func=AF.Sin),
         reads=[r], writes=[cos_t])


def emit_proj_phase(p, nc, d, cfg):
    nin, nfreq = cfg["nin"], cfg["nfreq"]
    hd = 2 * nfreq
    NB, NF = cfg["nb"], cfg["nf"]
    ntt = NTT
    W = p.sbuf("W", [128, 8, nin], BF16)
    g_bc = p.sbuf("g_bc", [128, DM], F32)
    ident = p.sbuf("ident", [128, 128], BF16)
    cos_t = p.sbuf("cos_t", [128, ntt, nfreq], F32)
    sin_t = p.sbuf("sin_t", [128, ntt, nfreq], F32)
    p.dma("sp", g_bc.ap[:], d["g"].partition_broadcast(128), writes=[g_bc])
    p.dma("sp", ident.ap[:], d["ident"][:, :], writes=[ident])
    if cfg.get("sig") is not None:
        gb_bc = p.sbuf("gb_bc", [128, 24], F32)
        p.dma("sp", gb_bc.ap[:], d["gb"].partition_broadcast(128), writes=[gb_bc])
    with p.scope():
        invf = p.sbuf("invf", [128, nfreq], F32)
        pos_i = p.sbuf("pos_i", [128, ntt], I32)
        pos_f = p.sbuf("pos_f", [128, ntt], F32)
        ang_t = p.sbuf("ang_t", [128, ntt, nfreq], F32)
        p.dma("sp", invf.ap[:], d["invf"][:, :], writes=[invf])
        p.dma("sp", pos_i.ap[:], d["pos"].rearrange("(n p) -> p n", p=128), writes=[pos_i],
              allow_slow_non_contiguous=True)
        emit_sincos(p, pos_i, invf, nfreq, ntt, cos_t, sin_t, ang_t, pos_f)

    CW = 512
    nchunk = (nin + CW - 1) // CW
    Wt = [p.trk(name=f"Wt{j}") for j in range(nchunk)]
    with p.scope():
        stg = [p.sbuf(f"wstg{i}", [128, 8, CW], F32) for i in range(2)]
        wv = d["w"].rearrange("(c p) n -> p c n", p=128)
        cvt_eng = ["dve", "pool", "act"]
        for j in range(nchunk):
            c0 = j * CW
            cw = min(CW, nin - c0)
            s = stg[j % 2]
            p.dma("sp" if j % 2 == 0 else "pool", s.ap[:, :, 0:cw], wv[:, :, c0:c0 + cw], writes=[s])
            for c in range(8):
                en = cvt_eng[(j * 8 + c) % 3]
                if en == "act":
                    p.op("act", lambda e: e.copy(out=W.ap[:, c, c0:c0 + cw], in_=s.ap[:, c, 0:cw]),
                         reads=[s], writes=[Wt[j]])
                else:
                    p.op(en, lambda e: e.tensor_copy(out=W.ap[:, c, c0:c0 + cw], in_=s.ap[:, c, 0:cw]),
                         reads=[s], writes=[Wt[j]])

    NBUF = cfg.get("nbuf", 2)
    xt = [p.sbuf(f"xt{i}", [128, DM], F32) for i in range(2)]
    junk = p.sbuf("junk", [128, DM], F32)
    ssq = [p.sbuf(f"ssq{i}", [128, 1], F32) for i in range(2)]
    rstd = [p.sbuf(f"rstd{i}", [128, 1], F32) for i in range(2)]
    hb = [p.sbuf(f"hb{i}", [128, DM], BF16) for i in range(2)]
    hT = [p.sbuf(f"hT{i}", [128, 8, 128], BF16) for i in range(2)]
    pT = [p.psum(f"pT{i}", [128, 8, 128], BF16) for i in range(2)]
    pp = [p.psum(f"pp{i}", [128, CW], F32) for i in range(4)]
    proj = [p.sbuf(f"proj{i}", [128, nin], F32) for i in range(NBUF)]
    ob = [p.sbuf(f"ob{i}", [128, NB], BF16) for i in range(NBUF)]
    of = [p.sbuf(f"of{i}", [128, NF], F32) for i in range(NBUF)]
    rt = [p.sbuf(f"rt{i}", [128, 8, nfreq], F32) for i in range(4)]
    epst = p.const_tile(EPS)
    xv = d["x"].rearrange("(n p) f -> n p f", p=128)
    pbv = d["pb"].rearrange("(n p) f -> n p f", p=128)
    pfv = d["pf"].rearrange("(n p) f -> n p f", p=128)
    ppi = 0
    for i in range(ntt):
        b = i % 2
        p.dma("sp", xt[b].ap[:], xv[i], reads=([d["_x_trk"]] if "_x_trk" in d else []), writes=[xt[b]])
        p.op("dve", lambda e: e.memset(ssq[b].ap[:], 0.0), writes=[ssq[b]])
        p.op("act", lambda e: e.activation(out=junk.ap[:], in_=xt[b].ap[:], func=AF.Square,
                                           accum_out=ssq[b].ap[:]),
             reads=[xt[b]], writes=[junk, ssq[b]])
        p.op("act", lambda e: e.activation(out=rstd[b].ap[:], in_=ssq[b].ap[:], func=AF.Sqrt,
                                           bias=epst.ap[:], scale=1.0 / DM),
             reads=[ssq[b], epst], writes=[rstd[b]])
        p.op("dve", lambda e: e.reciprocal(out=rstd[b].ap[:], in_=rstd[b].ap[:]),
             reads=[rstd[b]], writes=[rstd[b]])
        p.op("dve", lambda e: e.scalar_tensor_tensor(
            out=hb[b].ap[:], in0=xt[b].ap[:], scalar=rstd[b].ap[:, 0:1], in1=g_bc.ap[:],
            op0=ALU.mult, op1=ALU.mult), reads=[xt[b], rstd[b], g_bc], writes=[hb[b]])
        for c in range(8):
            p.op("pe", lambda e: e.transpose(out=pT[b].ap[:, c, :], in_=hb[b].ap[:, c * 128:(c + 1) * 128],
                                             identity=ident.ap[:]),
                 reads=[hb[b], ident], writes=[pT[b]])
        p.op("act", lambda e: e.copy(out=hT[b].ap[:], in_=pT[b].ap[:]), reads=[pT[b]], writes=[hT[b]])
        pj = proj[i % NBUF]
        for j in range(nchunk):
            c0 = j * CW
            cw = min(CW, nin - c0)
            ps = pp[ppi % 4]
            ppi += 1
            for c in range(8):
                p.op("pe", lambda e: e.matmul(ps.ap[:, 0:cw], lhsT=hT[b].ap[:, c, :],
                                              rhs=W.ap[:, c, c0:c0 + cw],
                                              start=(c == 0), stop=(c == 7)),
                     reads=[hT[b], Wt[j]], writes=[ps])
            p.op("act", lambda e: e.copy(out=pj.ap[:, c0:c0 + cw], in_=ps.ap[:, 0:cw]),
                 reads=[ps], writes=[pj])
        o_b, o_f = ob[i % NBUF], of[i % NBUF]
        ri = 0
        for (so, nh, do, scl) in cfg["rot"]:
            for h0 in range(0, nh, 8):
                hh = min(8, nh - h0)
                src = pj.ap[:, so + h0 * hd: so + (h0 + hh) * hd].rearrange("p (h t f) -> p h t f", h=hh, t=2)
                dst = o_b.ap[:, do + h0 * hd: do + (h0 + hh) * hd].rearrange("p (h t f) -> p h t f", h=hh, t=2)
                x1, x2 = src[:, :, 0, :], src[:, :, 1, :]
                cb = bcast_mid(cos_t.ap[:, i, :], hh)
                sb = bcast_mid(sin_t.ap[:, i, :], hh)
                t1, t2 = rt[ri % 4], rt[(ri + 1) % 4]
                ri += 2
                e1 = "dve"
                e2 = "pool"
                p.op(e1, lambda e: e.tensor_tensor(out=t1.ap[:, 0:hh, :], in0=x1, in1=cb, op=ALU.mult),
                     reads=[pj, cos_t], writes=[t1])
                p.op(e2, lambda e: e.tensor_tensor(out=t2.ap[:, 0:hh, :], in0=x2, in1=sb, op=ALU.mult),
                     reads=[pj, sin_t], writes=[t2])
                p.op(e1, lambda e: e.tensor_tensor(out=dst[:, :, 0, :], in0=t1.ap[:, 0:hh, :],
                                                   in1=t2.ap[:, 0:hh, :], op=ALU.subtract),
                     reads=[t1, t2], writes=[o_b])
                t3, t4 = rt[ri % 4], rt[(ri + 1) % 4]
                ri += 2
                p.op(e2, lambda e: e.tensor_tensor(out=t3.ap[:, 0:hh, :], in0=x1, in1=sb, op=ALU.mult),
                     reads=[pj, sin_t], writes=[t3])
                p.op(e1, lambda e: e.tensor_tensor(out=t4.ap[:, 0:hh, :], in0=x2, in1=cb, op=ALU.mult),
                     reads=[pj, cos_t], writes=[t4])
                p.op(e2, lambda e: e.tensor_tensor(out=dst[:, :, 1, :], in0=t3.ap[:, 0:hh, :],
                                                   in1=t4.ap[:, 0:hh, :], op=ALU.add),
                     reads=[t3, t4], writes=[o_b])
        for (so, w, do) in cfg["cpb"]:
            p.op("act", lambda e: e.copy(out=o_b.ap[:, do:do + w], in_=pj.ap[:, so:so + w]),
                 reads=[pj], writes=[o_b])
        for (so, w, do) in cfg["silu"]:
            p.op("act", lambda e: e.activation(out=o_f.ap[:, do:do + w], in_=pj.ap[:, so:so + w], func=AF.Silu),
                 reads=[pj], writes=[o_f])
        if cfg.get("sig") is not None:
            so, w, do = cfg["sig"]
            p.op("dve", lambda e: e.tensor_tensor(out=pj.ap[:, so:so + w], in0=pj.ap[:, so:so + w],
                                                  in1=gb_bc.ap[:], op=ALU.add),
                 reads=[pj, gb_bc], writes=[pj])
            p.op("act", lambda e: e.activation(out=o_f.ap[:, do:do + w], in_=pj.ap[:, so:so + w], func=AF.Sigmoid),
                 reads=[pj], writes=[o_f])
        p.dma("sp", pbv[i], o_b.ap[:], reads=[o_b], is_output=True)
        p.dma("sp", pfv[i], o_f.ap[:], reads=[o_f], is_output=True)


L0 = dict(nq=0, nkc=512, nvc=640, nks=768, nvs=896, nkw=1024, nvw=1152, ng=1280, nz=1304,
          mq=1816, mk=2328, mv=2840, mz=3352, eq=3864, ez=4120)
PB0 = dict(nq=0, nkc=512, nvc=640, nks=768, nvs=896, nkw=1024, nvw=1152, mq=1280, mk=1792, mv=2304, eq=2816)
PF0 = dict(gates=0, nz=24, mz=536, ez=1048)
CFG0 = dict(
    nin=4376, nfreq=32, nb=3072, nf=1304,
    rot=[(L0["nq"], 8, PB0["nq"], 1.0), (L0["nkc"], 2, PB0["nkc"], 1.0), (L0["nks"], 2, PB0["nks"], 1.0),
         (L0["nkw"], 2, PB0["nkw"], 1.0), (L0["mq"], 8, PB0["mq"], 1.0), (L0["mk"], 8, PB0["mk"], 1.0)],
    cpb=[(L0["nvc"], 128, PB0["nvc"]), (L0["nvs"], 128, PB0["nvs"]), (L0["nvw"], 128, PB0["nvw"]),
         (L0["mv"], 512, PB0["mv"]), (L0["eq"], 256, PB0["eq"])],
    silu=[(L0["nz"], 512, PF0["nz"]), (L0["mz"], 512, PF0["mz"]), (L0["ez"], 256, PF0["ez"])],
    sig=(L0["ng"], 24, PF0["gates"]),
)
L1 = dict(rq=0, rk=1024, rv=2048, rz=4096, eq=6144, ez=6400)
PB1 = dict(rq=0, rk=1024, rv=2048, eq=4096)
PF1 = dict(rz=0, ez=2048)
CFG1 = dict(
    nin=6656, nfreq=128, nb=4352, nf=2304, nbuf=1,
    rot=[(L1["rq"], 4, PB1["rq"], 1.0), (L1["rk"], 4, PB1["rk"], 1.0)],
    cpb=[(L1["rv"], 2048, PB1["rv"]), (L1["eq"], 256, PB1["eq"])],
    silu=[(L1["rz"], 2048, PF1["rz"]), (L1["ez"], 256, PF1["ez"])],
    sig=None,
)


def build_proj(cfg):
    nc = bass.Bass("TRN2", target_bir_lowering=False)
    d = {}
    d["x"] = nc.dram_tensor("x", [TPC, DM], F32, kind="ExternalInput").ap()
    d["pos"] = nc.dram_tensor("pos", [TPC], I32, kind="ExternalInput").ap()
    d["g"] = nc.dram_tensor("g", [DM], F32, kind="ExternalInput").ap()
    d["w"] = nc.dram_tensor("w", [DM, cfg["nin"]], F32, kind="ExternalInput").ap()
    d["invf"] = nc.dram_tensor("invf", [128, cfg["nfreq"]], F32, kind="ExternalInput").ap()
    d["ident"] = nc.dram_tensor("ident", [128, 128], BF16, kind="ExternalInput").ap()
    if cfg.get("sig") is not None:
        d["gb"] = nc.dram_tensor("gb", [24], F32, kind="ExternalInput").ap()
    d["pb"] = nc.dram_tensor("pb", [TPC, cfg["nb"]], BF16, kind="ExternalOutput").ap()
    d["pf"] = nc.dram_tensor("pf", [TPC, cfg["nf"]], F32, kind="ExternalOutput").ap()
    with ExitStack() as es:
        p = Prog(nc, es)
        p.const_tile(EPS)
        emit_proj_phase(p, nc, d, cfg)
        p.finish()
    return nc


BIG = 30000.0
DBG = {}


class StopEmit(Exception):
    pass


def ck(n):
    c = DBG.setdefault("_cnt", {})
    c[n] = c.get(n, 0) + 1
    if DBG.get("stop") == n or DBG.get("stop") == (n, c[n]):
        DBG["stopped"] = True
NQT = 16
SCALE = 0.125


def emit_mem_kv(p, d, ident, misc, kmT, vm_aug):
    with p.scope():
        wst = p.sbuf("mk_wst", [128, 8, 512], F32)
        wkv = p.sbuf("mk_wkv", [128, 8, 512], BF16)
        mx = p.sbuf("mk_mx", [128, 2, DM], F32)
        mg = p.sbuf("mk_g", [128, DM], F32)
        mh = p.sbuf("mk_h", [128, DM], BF16)
        memT = p.sbuf("mk_T", [128, 8, 256], BF16)
        junk = p.sbuf("mk_junk", [128, DM], F32)
        ssq = p.sbuf("mk_ssq", [128, 1], F32)
        epst = p.const_tile(EPS)
        p.dma("sp", wst.ap[:], d["wkv"].rearrange("(c p) n -> p c n", p=128), writes=[wst])
        p.dma("sp", mx.ap[:], d["memx"].rearrange("(t p) f -> p t f", p=128), writes=[mx])
        p.dma("sp", mg.ap[:], d["memg"].partition_broadcast(128), writes=[mg])
        for c in range(8):
            p.op("pool" if c % 2 else "dve", lambda e: e.tensor_copy(out=wkv.ap[:, c, :], in_=wst.ap[:, c, :]),
                 reads=[wst], writes=[wkv])
        for t in range(2):
            p.op("dve", lambda e: e.memset(ssq.ap[:], 0.0), writes=[ssq])
            p.op("act", lambda e: e.activation(out=junk.ap[:], in_=mx.ap[:, t, :], func=AF.Square,
                                               accum_out=ssq.ap[:]), reads=[mx], writes=[junk, ssq])
            p.op("act", lambda e: e.activation(out=ssq.ap[:], in_=ssq.ap[:], func=AF.Sqrt,
                                               bias=epst.ap[:], scale=1.0 / DM), reads=[ssq, epst], writes=[ssq])
            p.op("dve", lambda e: e.reciprocal(out=ssq.ap[:], in_=ssq.ap[:]), reads=[ssq], writes=[ssq])
            p.op("dve", lambda e: e.scalar_tensor_tensor(out=mh.ap[:], in0=mx.ap[:, t, :], scalar=ssq.ap[:, 0:1],
                                                         in1=mg.ap[:], op0=ALU.mult, op1=ALU.mult),
                 reads=[mx, ssq, mg], writes=[mh])
            for half in range(2):
                mt_ = misc[half]
                pv = mt_.ap[:].bitcast(BF16)
                for cc in range(4):
                    c = half * 4 + cc
                    p.op("pe", lambda e: e.transpose(out=pv[:, cc * 128:(cc + 1) * 128],
                                                     in_=mh.ap[:, c * 128:(c + 1) * 128], identity=ident.ap[:]),
                         reads=[mh, ident], writes=[mt_])
                p.op("act", lambda e: e.copy(
                    out=memT.ap[:, half * 4:half * 4 + 4, t * 128:(t + 1) * 128],
                    in_=pv[:, 0:512].rearrange("p (c m) -> p c m", c=4)), reads=[mt_], writes=[memT])
        for hp in range(2):
            ps = misc[hp]
            for c in range(8):
                p.op("pe", lambda e: e.matmul(ps.ap[:, 0:256], lhsT=wkv.ap[:, c, hp * 128:(hp + 1) * 128],
                                              rhs=memT.ap[:, c, :], start=(c == 0), stop=(c == 7)),
                     reads=[wkv, memT], writes=[ps])
            p.op("act", lambda e: e.copy(out=kmT.ap[:, hp, :], in_=ps.ap[:, 0:256]), reads=[ps], writes=[kmT])
        p.op("pool", lambda e: e.memset(vm_aug.ap[:], 1.0), writes=[vm_aug])
        for mt in range(2):
            ps = misc[mt]
            for c in range(8):
                p.op("pe", lambda e: e.matmul(ps.ap[:, 0:256], lhsT=memT.ap[:, c, mt * 128:(mt + 1) * 128],
                                              rhs=wkv.ap[:, c, 256:512], start=(c == 0), stop=(c == 7)),
                     reads=[wkv, memT], writes=[ps])
            p.op("act", lambda e: e.copy(out=vm_aug.ap[:, mt, :, 0:64],
                                         in_=ps.ap[:, 0:256].rearrange("p (h f) -> p h f", h=4)),
                 reads=[ps], writes=[vm_aug])


def emit_attn_phase(p, nc, d, stages=("nsa", "moba", "mem", "out")):
    ident = p.sbuf("ident", [128, 128], BF16)
    p.dma("sp", ident.ap[:], d["ident"][:, :], writes=[ident])
    causal = p.sbuf("causal", [128, 128], BF16)
    dmask = p.sbuf("dmask", [128, 8, 128], BF16)
    for nm, t in (("causal", causal), ("dmask", dmask)):
        p.dma("sp", t.ap[:], d[nm], writes=[t])
    KA = p.sbuf("KA", [128, T], BF16)
    VA = p.sbuf("VA", [128, 128, 2, 65], BF16)
    QT = p.sbuf("QT", [128, NQT * 512], BF16)
    OG = p.sbuf("OG", [128, NQT, 1280], BF16)
    SA = [p.psum(f"SA{i}", [128, 512]) for i in range(2)]
    OC = p.psum("OC", [128, 2, 512])
    OA = [p.psum(f"OA{i}", [128, 512]) for i in range(2)]
    MISC = [p.psum(f"MISC{i}", [128, 512]) for i in range(2)]
    PT = [p.sbuf(f"PT{i}", [128, 512], BF16) for i in range(4)]
    zt = [p.sbuf(f"zt{i}", [128, 512], F32) for i in range(2)]
    rin = [p.sbuf(f"rin{i}", [128, 4], F32) for i in range(4)]
    SA3 = SA + [MISC[1]]
    cnt = {"s3": 0, "s": 0, "pt": 0, "oa": 0, "z": 0, "rin": 0, "misc": 0, "otsb": 0}

    def nxt(key, lst):
        t = lst[cnt[key] % len(lst)]
        cnt[key] += 1
        return t

    def kv_load(kap, vap):
        for q4 in range(4):
            p.dma("sp" if q4 % 2 == 0 else "pool", KA.ap[:, q4 * 4096:(q4 + 1) * 4096],
                  kap[:, q4 * 4096:(q4 + 1) * 4096], writes=[KA])
        if vap is not None:
            for q4 in range(4):
                p.dma("sp" if q4 % 2 == 0 else "pool", VA.ap[:, q4 * 32:(q4 + 1) * 32],
                      vap[:, q4 * 32:(q4 + 1) * 32], writes=[VA])

    identf = p.sbuf("identf", [128, 128], F32)
    p.op("dve", lambda e: e.tensor_copy(out=identf.ap[:], in_=ident.ap[:]), reads=[ident], writes=[identf])
    otsb = [p.sbuf(f"otsb{i}", [65, 512], F32) for i in range(2)]
    tools = dict(identf=identf, otsb=otsb, MISC=MISC, rin=rin, nxt=nxt)

    def finish_heads(oa, clamp=True):
        return finish_T(p, tools, oa, clamp)

    def stage_nsa():
        band = p.sbuf("band", [128, 128], BF16)
        wm0 = p.sbuf("wm0", [128, 4, 128], BF16)
        cmask = p.sbuf("cmask", [128, NQT, 2, 128], BF16)
        gates = p.sbuf("gates", [128, NQT, 24], F32)
        for nm, t in (("band", band), ("wm0", wm0), ("cmask", cmask), ("gates", gates)):
            p.dma("sp", t.ap[:], d[nm], writes=[t])
        kcT = p.sbuf("kcT", [64, 2, 1024], BF16)
        VC = p.sbuf("VC", [128, 8, 2, 321], BF16)
        p.op("pool", lambda e: e.memset(VC.ap[:], 1.0), writes=[VC])
        for g in range(2):
            p.dma("sp", VC.ap[:, :, g, 65:321], d["ovl"], writes=[VC])
        p.op("pool", lambda e: e.memset(kcT.ap[:], 0.0), writes=[kcT])
        w1s = p.sbuf("w1s", [128, 32, 64], F32)
        w1b = p.sbuf("w1b", [128, 32, 64], BF16)
        w2s = p.sbuf("w2s", [64, 128], F32)
        w2b = p.sbuf("w2b", [64, 128], BF16)
        pes = p.sbuf("pes", [128, 32], F32)
        peb = p.sbuf("peb", [128, 32], BF16)
        cbias = p.sbuf("cbias", [64, 1], F32)
        hidT = p.sbuf("hidT", [64, 1024], BF16)
        for kind in ("k", "v"):
            p.dma("sp", w1s.ap[:], d["w1" + kind], writes=[w1s])
            p.dma("sp", w2s.ap[:], d["w2" + kind], writes=[w2s])
            p.dma("sp", pes.ap[:], d["pe" + kind], writes=[pes])
            p.op("dve", lambda e: e.tensor_copy(out=w1b.ap[:], in_=w1s.ap[:]), reads=[w1s], writes=[w1b])
            p.op("dve", lambda e: e.tensor_copy(out=w2b.ap[:], in_=w2s.ap[:]), reads=[w2s], writes=[w2b])
            p.op("dve", lambda e: e.tensor_copy(out=peb.ap[:], in_=pes.ap[:]), reads=[pes], writes=[peb])
            kv_load(d["nkcT" if kind == "k" else "nvcT"], None)
            ms = nxt("misc", MISC)
            for l in range(32):
                p.op("pe", lambda e: e.matmul(ms.ap[0:64, 0:1], lhsT=w1b.ap[0:64, l, :], rhs=peb.ap[0:64, l:l + 1],
                                              start=(l == 0), stop=(l == 31)), reads=[w1b, peb], writes=[ms])
            p.op("act", lambda e: e.copy(out=cbias.ap[:], in_=ms.ap[0:64, 0:1]), reads=[ms], writes=[cbias])
            for g in range(2):
                pb = 64 * g
                for nh in range(2):
                    n0 = nh * 512
                    nn = 512 if nh == 0 else 511
                    s = nxt("s", SA)
                    for l in range(32):
                        st = n0 * 16 + l
                        rhs = KA.ap[pb:pb + 64, st: st + (nn - 1) * 16 + 1: 16]
                        p.op("pe", lambda e: e.matmul(s.ap[0:64, 0:nn], lhsT=w1b.ap[pb:pb + 64, l, :], rhs=rhs,
                                                      start=(l == 0), stop=(l == 31)), reads=[w1b, KA], writes=[s])
                    p.op("act", lambda e: e.activation(out=hidT.ap[:, n0:n0 + nn], in_=s.ap[0:64, 0:nn], func=AF.Silu,
                                                       bias=cbias.ap[:], scale=1.0), reads=[s, cbias], writes=[hidT])
                if kind == "k":
                    for nh in range(2):
                        n0 = nh * 512
                        nn = 512 if nh == 0 else 511
                        s = nxt("s", SA)
                        p.op("pe", lambda e: e.matmul(s.ap[:, 0:nn], lhsT=w2b.ap[:, :], rhs=hidT.ap[:, n0:n0 + nn],
                                                      start=True, stop=True), reads=[w2b, hidT], writes=[s])
                        p.op("act", lambda e: e.copy(out=kcT.ap[0:64, g, n0:n0 + nn], in_=s.ap[0:64, 0:nn]),
                             reads=[s], writes=[kcT])
                else:
                    for nt in range(8):
                        nn = 128 if nt < 7 else 127
                        ms = nxt("misc", MISC)
                        p.op("pe", lambda e: e.matmul(ms.ap[0:nn, 0:64], lhsT=hidT.ap[:, nt * 128:nt * 128 + nn],
                                                      rhs=w2b.ap[:, 0:64], start=True, stop=True),
                             reads=[w2b, hidT], writes=[ms])
                        p.op("act", lambda e: e.copy(out=VC.ap[0:nn, nt, g, 0:64], in_=ms.ap[0:nn, 0:64]),
                             reads=[ms], writes=[VC])
        for q4 in range(4):
            p.dma("sp" if q4 % 2 == 0 else "pool", VA.ap[:, q4 * 32:(q4 + 1) * 32],
                  d["vs"][:, q4 * 32:(q4 + 1) * 32], writes=[VA])
        rhsW = [p.sbuf(f"rhsW{i}", [128, 4, 512], BF16) for i in range(2)]
        rWm = [p.trk(name=f"rWm{i}") for i in range(2)]
        abt = [p.sbuf(f"abt{i}", [128, 2, 256], F32) for i in range(2)]
        kwt = [p.sbuf(f"kwt{i}", [64, 640], BF16) for i in range(2)]
        vwt = [p.sbuf(f"vwt{i}", [128, 5, 2, 65], BF16) for i in range(2)]
        imp = p.sbuf("imp", [128, 256], F32)
        wk = p.sbuf("impw", [128, 256], F32)
        m8 = p.sbuf("m8", [128, 16], F32)
        selp = p.sbuf("selp", [128, 320], BF16)
        p.op("pool", lambda e: e.memset(selp.ap[:], 0.0), writes=[selp])
        oacc = p.sbuf("oacc", [128, 4, 64], F32)
        gw = p.sbuf("gw", [128, 4], F32)
        it = 0
        for g in range(DBG.get("g0", 0), DBG.get("ngrp", 2)):
            for q4 in range(4):
                eng_ = "sp" if q4 % 2 == 0 else "pool"
                p.dma(eng_, KA.ap[0:64, q4 * 4096:(q4 + 1) * 4096], d["ksT"][g][:, q4 * 4096:(q4 + 1) * 4096], writes=[KA])
                if g == DBG.get("g0", 0):
                    p.dma(eng_, KA.ap[64:128, q4 * 4096:(q4 + 1) * 4096], d["epn"][:, q4 * 4096:(q4 + 1) * 4096],
                          writes=[KA])
            for j in range(DBG.get('nj', NQT)):
                ab = abt[it % 2]
                kw_, vw_ = kwt[it % 2], vwt[it % 2]
                rw, rm = rhsW[it % 2], rWm[it % 2]
                it += 1
                nwin = (16 * j + 15) // 64 + 1
                p.dma("sp", ab.ap[:], d["ab"][j], writes=[ab])
                p.dma("sp", kw_.ap[:], d["kw"][j, g], writes=[kw_])
                p.dma("sp", vw_.ap[:], d["vw"][j], writes=[vw_])
                z = nxt("z", zt)
                p.dma("sp", z.ap[:, 0:256], d["zs"][j][:, g * 256:(g + 1) * 256], writes=[z])
                for w in range(nwin):
                    p.dma("sp", rw.ap[0:64, w, :], d["qn"][j, g], writes=[rw])
                ntl = j // 2
                for hp in range(2):
                    for nt in range(ntl + 1):
                        s = nxt("s", SA)
                        nmask = 1 if nt >= ntl - 1 else 0
                        p.op("pe", lambda e: e.matmul(s.ap[:, 0:256], lhsT=kcT.ap[0:64, g, nt * 128:(nt + 1) * 128],
                                                      rhs=rw.ap[0:64, 0, hp * 256:(hp + 1) * 256],
                                                      start=True, stop=(nmask == 0)), reads=[kcT, rw], writes=[s])
                        if nmask:
                            wsel = nt - (ntl - 1)
                            if ntl == 0:
                                wsel = 1
                            rb = cmask.ap[:, j, wsel, :].unsqueeze(1).to_broadcast([128, 2, 128])
                            p.op("pe", lambda e: e.matmul(s.ap[:, 0:256], lhsT=ident.ap[:, :], rhs=rb,
                                                          start=False, stop=True), reads=[ident, cmask], writes=[s])
                        pt = nxt("pt", PT)
                        p.op("act", lambda e: e.activation(out=pt.ap[:, 0:256], in_=s.ap[:, 0:256], func=AF.Exp,
                                                           scale=SCALE), reads=[s], writes=[pt])
                        for i2 in range(2):
                            p.op("pe", lambda e: e.matmul(OC.ap[:, i2, 0:321], lhsT=pt.ap[:, i2 * 128:(i2 + 1) * 128],
                                                          rhs=VC.ap[:, nt, g, :], start=(nt == 0), stop=(nt == ntl)),
                                 reads=[pt, VC], writes=[OC])
                    r = nxt("rin", rin)
                    p.op("dve", lambda e: e.tensor_scalar(out=r.ap[:, 0:2], in0=OC.ap[:, :, 64], scalar1=1e-30,
                                                          scalar2=None, op0=ALU.max), reads=[OC], writes=[r])
                    p.op("dve", lambda e: e.reciprocal(out=r.ap[:, 0:2], in_=r.ap[:, 0:2]), reads=[r], writes=[r])
                    for i2 in range(2):
                        i = hp * 2 + i2
                        if i == 0:
                            p.op("dve", lambda e: e.tensor_scalar(out=imp.ap[:], in0=OC.ap[:, i2, 65:321],
                                                                  scalar1=r.ap[:, i2:i2 + 1], scalar2=None, op0=ALU.mult),
                                 reads=[OC, r], writes=[imp])
                        else:
                            p.op("dve", lambda e: e.scalar_tensor_tensor(
                                out=imp.ap[:], in0=OC.ap[:, i2, 65:321], scalar=r.ap[:, i2:i2 + 1], in1=imp.ap[:],
                                op0=ALU.mult, op1=ALU.add), reads=[OC, r, imp], writes=[imp])
                    p.op("dve", lambda e: e.tensor_tensor(out=gw.ap[:, 0:2], in0=r.ap[:, 0:2],
                                                          in1=gates.ap[:, j, g * 4 + hp * 2: g * 4 + hp * 2 + 2],
                                                          op=ALU.mult), reads=[r, gates], writes=[gw])
                    for i2 in range(2):
                        i = hp * 2 + i2
                        p.op("dve", lambda e: e.tensor_scalar(out=oacc.ap[:, i, :], in0=OC.ap[:, i2, 0:64],
                                                              scalar1=gw.ap[:, i2:i2 + 1], scalar2=None, op0=ALU.mult),
                             reads=[OC, gw], writes=[oacc])
                p.op("dve", lambda e: e.tensor_tensor(out=imp.ap[:], in0=imp.ap[:], in1=ab.ap[:, 0, :], op=ALU.mult),
                     reads=[imp, ab], writes=[imp])
                p.op("dve", lambda e: e.tensor_tensor(out=imp.ap[:], in0=imp.ap[:], in1=ab.ap[:, 1, :], op=ALU.add),
                     reads=[imp, ab], writes=[imp])
                p.op("dve", lambda e: e.memset(imp.ap[:, 0:1], 3e9), writes=[imp])
                p.op("dve", lambda e: e.max(out=m8.ap[:, 0:8], in_=imp.ap[:]), reads=[imp], writes=[m8])
                p.op("dve", lambda e: e.match_replace(out=wk.ap[:], in_to_replace=m8.ap[:, 0:8], in_values=imp.ap[:],
                                                      imm_value=-3e30), reads=[imp, m8], writes=[wk])
                p.op("dve", lambda e: e.max(out=m8.ap[:, 8:16], in_=wk.ap[:]), reads=[wk], writes=[m8])
                p.op("dve", lambda e: e.tensor_scalar(out=selp.ap[:, 64:320], in0=imp.ap[:], scalar1=m8.ap[:, 15:16],
                                                      scalar2=None, op0=ALU.is_ge), reads=[imp, m8], writes=[selp])
                for w in range(nwin):
                    ms = nxt("misc", MISC)
                    p.op("pe", lambda e: e.matmul(ms.ap[:, 0:128], lhsT=selp.ap[:, 64 * w:64 * w + 128],
                                                  rhs=ident.ap[:, :], start=True, stop=True),
                         reads=[selp, ident], writes=[ms])
                    p.op("dve", lambda e: e.tensor_scalar(
                        out=rw.ap[64:128, w, :].rearrange("p (i q) -> p i q", i=4),
                        in0=ms.ap[64:128, 0:128].unsqueeze(1).to_broadcast([64, 4, 128]),
                        scalar1=-1.0, scalar2=BIG, op0=ALU.add, op1=ALU.mult), reads=[ms], writes=[rm])
                oa = nxt("oa", OA)
                nkt = 8 * j + 8
                pend = None

                def sel_pv(kt_, pt_):
                    p.op("pe", lambda e: e.matmul(oa.ap[0:65, :], lhsT=VA.ap[:, kt_, g, :], rhs=pt_.ap[:, :],
                                                  start=(kt_ == 0), stop=(kt_ == nkt - 1)), reads=[pt_, VA], writes=[oa])
                pendq = []
                for kt in range(nkt):
                    s = nxt("s3", SA3)
                    amb = kt >= 8 * j
                    p.op("pe", lambda e: e.matmul(s.ap[:, :], lhsT=KA.ap[:, kt * 128:(kt + 1) * 128],
                                                  rhs=rw.ap[:, kt // 32, :], start=True, stop=(not amb)),
                         reads=[KA, rw, rm], writes=[s])
                    if amb:
                        rb = dmask.ap[:, kt - 8 * j, :].unsqueeze(1).to_broadcast([128, 4, 128])
                        p.op("pe", lambda e: e.matmul(s.ap[:, :], lhsT=ident.ap[:, :], rhs=rb, start=False, stop=True),
                             reads=[ident, dmask], writes=[s])
                    pt = nxt("pt", PT)
                    p.op("act", lambda e: e.activation(out=pt.ap[:], in_=s.ap[:], func=AF.Exp, scale=SCALE),
                         reads=[s], writes=[pt])
                    pendq.append((kt, pt))
                    if len(pendq) > 2:
                        sel_pv(*pendq.pop(0))
                for x_ in pendq:
                    sel_pv(*x_)
                r, ov, oa = finish_heads(oa)
                p.op("dve", lambda e: e.tensor_tensor(out=gw.ap[:], in0=r.ap[:], in1=gates.ap[:, j, 8 + g * 4: 8 + g * 4 + 4],
                                                      op=ALU.mult), reads=[r, gates], writes=[gw])
                for i in range(4):
                    p.op("dve", lambda e: e.scalar_tensor_tensor(out=oacc.ap[:, i, :], in0=ov[:, i, 0:64],
                                                                 scalar=gw.ap[:, i:i + 1], in1=oacc.ap[:, i, :],
                                                                 op0=ALU.mult, op1=ALU.add),
                         reads=[oa, gw, oacc], writes=[oacc])
                oa = nxt("oa", OA)
                for w in range(5):
                    s = nxt("s", SA)
                    msk = []
                    if w == 0:
                        msk.append((band, band.ap[:, :]))
                    if w == 4:
                        msk.append((causal, causal.ap[:, :]))
                    if j == 0 and w < 4:
                        msk.append((wm0, wm0.ap[:, w, :]))
                    p.op("pe", lambda e: e.matmul(s.ap[:, :], lhsT=kw_.ap[0:64, w * 128:(w + 1) * 128],
                                                  rhs=rw.ap[0:64, 0, :], start=True, stop=(len(msk) == 0)),
                         reads=[kw_, rw], writes=[s])
                    for mi, (mt_, map_) in enumerate(msk):
                        rb = map_.unsqueeze(1).to_broadcast([128, 4, 128])
                        p.op("pe", lambda e: e.matmul(s.ap[:, :], lhsT=ident.ap[:, :], rhs=rb, start=False,
                                                      stop=(mi == len(msk) - 1)), reads=[ident, mt_], writes=[s])
                    pt = nxt("pt", PT)
                    p.op("act", lambda e: e.activation(out=pt.ap[:], in_=s.ap[:], func=AF.Exp, scale=SCALE),
                         reads=[s], writes=[pt])
                    p.op("pe", lambda e: e.matmul(oa.ap[0:65, :], lhsT=vw_.ap[:, w, g, :], rhs=pt.ap[:, :],
                                                  start=(w == 0), stop=(w == 4)), reads=[pt, vw_], writes=[oa])
                r, ov, oa = finish_heads(oa)
                p.op("dve", lambda e: e.tensor_tensor(out=gw.ap[:], in0=r.ap[:], in1=gates.ap[:, j, 16 + g * 4: 16 + g * 4 + 4],
                                                      op=ALU.mult), reads=[r, gates], writes=[gw])
                for i in range(4):
                    p.op("dve", lambda e: e.scalar_tensor_tensor(out=oacc.ap[:, i, :], in0=ov[:, i, 0:64],
                                                                 scalar=gw.ap[:, i:i + 1], in1=oacc.ap[:, i, :],
                                                                 op0=ALU.mult, op1=ALU.add),
                         reads=[oa, gw, oacc], writes=[oacc])
                p.op("dve", lambda e: e.tensor_tensor(out=OG.ap[:, j, g * 256:(g + 1) * 256],
                                                      in0=oacc.ap[:].rearrange("p h f -> p (h f)"),
                                                      in1=z.ap[:, 0:256], op=ALU.mult), reads=[oacc, z], writes=[OG])
    if "nsa" in stages:
        with p.scope():
            stage_nsa()
    else:
        p.op("pool", lambda e: e.memset(OG.ap[:, :, 0:512], 0.0), writes=[OG])

    def stage_moba():
        kmean = p.sbuf("kmean", [64, 64], F32)
        kmb = p.sbuf("kmb", [64, 64], BF16)
        mabt = [p.sbuf(f"mabt{i}", [128, 4, 3, 64], F32) for i in range(2)]
        gt = p.sbuf("gt", [128, 4, 64], F32)
        m8m = p.sbuf("m8m", [128, 4, 8], F32)
        selmm = p.sbuf("selmm", [128, 4, 64], F32)
        selmb = p.sbuf("selmb", [128, 4, 128], BF16)
        p.op("pool", lambda e: e.memset(selmb.ap[:], 0.0), writes=[selmb])
        QTm = [p.trk(name=f"QTm{G}") for G in range(4)]
        for q4 in range(4):
            p.dma("sp" if q4 % 2 == 0 else "pool", KA.ap[64:128, q4 * 4096:(q4 + 1) * 4096],
                  d["epm"][:, q4 * 4096:(q4 + 1) * 4096], writes=[KA])
        for h in range(2 * DBG.get('nhp', 4)):
            hp, hh = h // 2, h % 2
            if hh == 0:
                for q4 in range(4):
                    p.dma("sp" if q4 % 2 == 0 else "pool", VA.ap[:, q4 * 32:(q4 + 1) * 32],
                          d["mv"][hp][:, q4 * 32:(q4 + 1) * 32], writes=[VA])
            for q4 in range(4):
                p.dma("sp" if q4 % 2 == 0 else "pool", KA.ap[0:64, q4 * 4096:(q4 + 1) * 4096],
                      d["mkT"][h][:, q4 * 4096:(q4 + 1) * 4096], writes=[KA])
            p.dma("sp", QT.ap[0:64, 0:2048], d["mq"][h], writes=[QT])
            p.op("dve", lambda e: e.tensor_reduce(out=kmean.ap[:], in_=KA.ap[0:64, :].rearrange("p (n s) -> p n s", s=256),
                                                  axis=AX.X, op=ALU.add), reads=[KA], writes=[kmean])
            p.op("dve", lambda e: e.tensor_scalar(out=kmb.ap[:], in0=kmean.ap[:], scalar1=1.0 / 256, scalar2=None,
                                                  op0=ALU.mult), reads=[kmean], writes=[kmb])
            for G in range(DBG.get('ng', 4)):
                j0 = G * 4
                mab = mabt[(h * 4 + G) % 2]
                p.dma("sp", mab.ap[:], d["mab"][G], writes=[mab])
                z = nxt("z", zt)
                p.dma("sp", z.ap[:, 0:256].rearrange("p (r f) -> p r f", r=4),
                      d["zs"][j0:j0 + 4, :, 512 + h * 64: 512 + (h + 1) * 64].rearrange("r p f -> p r f"),
                      writes=[z], allow_slow_non_contiguous=True)
                ms = nxt("misc", MISC)
                for r_ in range(4):
                    p.op("pe", lambda e: e.matmul(ms.ap[:, r_ * 64:(r_ + 1) * 64],
                                                  lhsT=QT.ap[0:64, (j0 + r_) * 128:(j0 + r_ + 1) * 128],
                                                  rhs=kmb.ap[:, :], start=True, stop=True),
                         reads=[QT, kmb], writes=[ms])
                msv = ms.ap[:, 0:256].rearrange("p (r n) -> p r n", r=4)
                p.op("dve", lambda e: e.tensor_tensor(out=gt.ap[:], in0=msv, in1=mab.ap[:, :, 0, :], op=ALU.mult),
                     reads=[ms, mab], writes=[gt])
                p.op("dve", lambda e: e.tensor_tensor(out=gt.ap[:], in0=gt.ap[:], in1=mab.ap[:, :, 1, :], op=ALU.add),
                     reads=[gt, mab], writes=[gt])
                for r_ in range(4):
                    p.op("dve", lambda e: e.max(out=m8m.ap[:, r_, :], in_=gt.ap[:, r_, :]), reads=[gt], writes=[m8m])
                for r_ in range(4):
                    p.op("dve", lambda e: e.tensor_scalar(out=selmm.ap[:, r_, :], in0=gt.ap[:, r_, :],
                                                          scalar1=m8m.ap[:, r_, 2:3], scalar2=None, op0=ALU.is_ge),
                         reads=[gt, m8m], writes=[selmm])
                p.op("dve", lambda e: e.tensor_tensor(out=selmm.ap[:], in0=selmm.ap[:], in1=mab.ap[:, :, 0, :],
                                                      op=ALU.mult), reads=[selmm, mab], writes=[selmm])
                p.op("dve", lambda e: e.tensor_tensor(out=selmb.ap[:, :, 64:128], in0=selmm.ap[:], in1=mab.ap[:, :, 2, :],
                                                      op=ALU.add), reads=[selmm, mab], writes=[selmb])
                ms2 = nxt("misc", MISC)
                for r_ in range(4):
                    p.op("pe", lambda e: e.matmul(ms2.ap[:, r_ * 128:(r_ + 1) * 128], lhsT=selmb.ap[:, r_, :],
                                                  rhs=ident.ap[:, :], start=True, stop=True),
                         reads=[selmb, ident], writes=[ms2])
                p.op("dve", lambda e: e.tensor_scalar(out=QT.ap[64:128, j0 * 128:(j0 + 4) * 128], in0=ms2.ap[64:128, :],
                                                      scalar1=-1.0, scalar2=BIG, op0=ALU.add, op1=ALU.mult),
                     reads=[ms2], writes=[QTm[G]])
                oa = nxt("oa", OA)
                nkt = 8 * (j0 + 3) + 8
                pend = None

                def mo_pv(kt_, pt_, c0_):
                    p.op("pe", lambda e: e.matmul(oa.ap[0:65, c0_:512], lhsT=VA.ap[:, kt_, hh, :],
                                                  rhs=pt_.ap[:, c0_:512], start=(kt_ == 0), stop=(kt_ == nkt - 1)),
                         reads=[pt_, VA], writes=[oa])
                pendq = []
                for kt in range(nkt):
                    jlo = max(j0, kt // 8)
                    c0 = (jlo - j0) * 128
                    amb = kt // 8 >= j0
                    s = nxt("s3", SA3)
                    p.op("pe", lambda e: e.matmul(s.ap[:, c0:512], lhsT=KA.ap[:, kt * 128:(kt + 1) * 128],
                                                  rhs=QT.ap[:, j0 * 128 + c0:(j0 + 4) * 128],
                                                  start=True, stop=(not amb)), reads=[KA, QT, QTm[G]], writes=[s])
                    if amb:
                        p.op("pe", lambda e: e.matmul(s.ap[:, c0:c0 + 128], lhsT=ident.ap[:, :],
                                                      rhs=dmask.ap[:, kt % 8, :], start=False, stop=True),
                             reads=[ident, dmask], writes=[s])
                    pt = nxt("pt", PT)
                    p.op("act", lambda e: e.activation(out=pt.ap[:, c0:512], in_=s.ap[:, c0:512], func=AF.Exp,
                                                       scale=SCALE), reads=[s], writes=[pt])
                    pendq.append((kt, pt, c0))
                    if len(pendq) > 2:
                        mo_pv(*pendq.pop(0))
                for x_ in pendq:
                    mo_pv(*x_)
                r, ov, oa = finish_heads(oa)
                for r_ in range(4):
                    p.op("dve", lambda e: e.scalar_tensor_tensor(
                        out=OG.ap[:, j0 + r_, 512 + h * 64: 512 + (h + 1) * 64], in0=ov[:, r_, 0:64],
                        scalar=r.ap[:, r_:r_ + 1], in1=z.ap[:, r_ * 64:(r_ + 1) * 64], op0=ALU.mult, op1=ALU.mult),
                        reads=[oa, r, z], writes=[OG])
    if "moba" in stages:
        with p.scope():
            stage_moba()
    else:
        p.op("pool", lambda e: e.memset(OG.ap[:, :, 512:1024], 0.0), writes=[OG])

    if "mem" in stages:
        kmT = p.sbuf("kmT", [128, 2, 256], BF16)
        vm_aug = p.sbuf("vm_aug", [128, 2, 4, 65], BF16)
        emit_mem_kv(p, d, ident, MISC, kmT, vm_aug)
        emit_mem_attn(p, d, kmT, vm_aug, QT, SA, OA, PT, zt, tools, nxt, OG, 1024, 1024)
    else:
        p.op("pool", lambda e: e.memset(OG.ap[:, :, 1024:1280], 0.0), writes=[OG])

    if "out" in stages:
        emit_outproj(p, d, ident, OG, 1280, MISC, SA, KA, None)
    return OG


def finish_T(p, tools, oa, clamp=True):
    nxt = tools["nxt"]
    ot = nxt("otsb", tools["otsb"])
    p.op("act", lambda e: e.copy(out=ot.ap[:], in_=oa.ap[0:65, :]), reads=[oa], writes=[ot])
    ms = nxt("misc", tools["MISC"])
    for r_ in range(4):
        p.op("pe", lambda e: e.transpose(out=ms.ap[:, r_ * 65:(r_ + 1) * 65], in_=ot.ap[:, r_ * 128:(r_ + 1) * 128],
                                         identity=tools["identf"].ap[0:65, 0:65]),
             reads=[ot, tools["identf"]], writes=[ms])
    ov = ms.ap[:, 0:260].rearrange("p (h f) -> p h f", h=4)
    r = nxt("rin", tools["rin"])
    if clamp:
        p.op("dve", lambda e: e.tensor_scalar(out=r.ap[:, 0:4], in0=ov[:, :, 64], scalar1=1e-30, scalar2=None,
                                              op0=ALU.max), reads=[ms], writes=[r])
        p.op("dve", lambda e: e.reciprocal(out=r.ap[:, 0:4], in_=r.ap[:, 0:4]), reads=[r], writes=[r])
    else:
        p.op("dve", lambda e: e.reciprocal(out=r.ap[:, 0:4], in_=ov[:, :, 64]), reads=[ms], writes=[r])
    return r, ov, ms


def emit_mem_attn(p, d, kmT, vm_aug, QT, SA, OA, PT, zt, tools, nxt, OG, col0, zcol0):
    for hp in range(2):
        p.dma("sp", QT.ap[:, 0:2048], d["eqT"][hp], writes=[QT])
        for hh in range(2):
            pb = 64 * hh
            h = hp * 2 + hh
            for G in range(4):
                j0 = G * 4
                z = nxt("z", zt)
                p.dma("sp", z.ap[:, 0:256].rearrange("p (r f) -> p r f", r=4),
                      d["zs"][j0:j0 + 4, :, zcol0 + h * 64: zcol0 + (h + 1) * 64].rearrange("r p f -> p r f"),
                      writes=[z], allow_slow_non_contiguous=True)
                oa = nxt("oa", OA)
                for mt in range(2):
                    s = nxt("s", SA)
                    p.op("pe", lambda e: e.matmul(s.ap[:, :], lhsT=kmT.ap[pb:pb + 64, hp, mt * 128:(mt + 1) * 128],
                                                  rhs=QT.ap[pb:pb + 64, j0 * 128:(j0 + 4) * 128], start=True, stop=True),
                         reads=[kmT, QT], writes=[s])
                    pt = nxt("pt", PT)
                    p.op("act", lambda e: e.activation(out=pt.ap[:], in_=s.ap[:], func=AF.Exp, scale=SCALE),
                         reads=[s], writes=[pt])
                    p.op("pe", lambda e: e.matmul(oa.ap[0:65, :], lhsT=vm_aug.ap[:, mt, h, :], rhs=pt.ap[:, :],
                                                  start=(mt == 0), stop=(mt == 1)), reads=[pt, vm_aug], writes=[oa])
                r, ov, oa = finish_T(p, tools, oa, clamp=False)
                for r_ in range(4):
                    p.op("dve", lambda e: e.scalar_tensor_tensor(
                        out=OG.ap[:, j0 + r_, col0 + h * 64: col0 + (h + 1) * 64], in0=ov[:, r_, 0:64],
                        scalar=r.ap[:, r_:r_ + 1], in1=z.ap[:, r_ * 64:(r_ + 1) * 64], op0=ALU.mult, op1=ALU.mult),
                        reads=[oa, r, z], writes=[OG])


def emit_outproj(p, d, ident, OG, nfeat, MISC, SA, WB, final_g):
    nch = nfeat // 128
    wst = [p.sbuf(f"op_wst{i}", [128, DM], F32) for i in range(2)]
    Wv = WB.ap[:, 0:nch * DM].rearrange("p (c n) -> p c n", c=nch)
    wv = d["wout"].rearrange("(c p) n -> p c n", p=128)
    for c in range(nch):
        s = wst[c % 2]
        p.dma("sp", s.ap[:], wv[:, c, :], writes=[s])
        p.op("dve" if c % 2 == 0 else "pool", lambda e: e.tensor_copy(out=Wv[:, c, :], in_=s.ap[:]),
             reads=[s], writes=[WB])
    OGT = [p.sbuf(f"OGT{i}", [128, nch, 128], BF16) for i in range(2)]
    xt = [p.sbuf(f"op_x{i}", [128, DM], F32) for i in range(2)]
    if final_g is not None:
        junk = p.sbuf("op_junk", [128, DM], F32)
        ssq = [p.sbuf(f"op_ssq{i}", [128, 1], F32) for i in range(2)]
        epst = p.const_tile(EPS)
    xv = d["x"].rearrange("(n p) f -> n p f", p=128)
    ov = d["xo"].rearrange("(n p) f -> n p f", p=128)
    k = 0
    for j in range(NQT):
        ogt = OGT[j % 2]
        x_ = xt[j % 2]
        p.dma("sp", x_.ap[:], xv[j], writes=[x_])
        for c0 in range(0, nch, 8):
            cn = min(8, nch - c0)
            ms = MISC[k % 2]
            k += 1
            pv = ms.ap[:].bitcast(BF16)
            for cc in range(cn):
                p.op("pe", lambda e: e.transpose(out=pv[:, cc * 128:(cc + 1) * 128],
                                                 in_=OG.ap[:, j, (c0 + cc) * 128:(c0 + cc + 1) * 128], identity=ident.ap[:]),
                     reads=[OG, ident], writes=[ms])
            p.op("act", lambda e: e.copy(out=ogt.ap[:, c0:c0 + cn, :],
                                         in_=pv[:, 0:cn * 128].rearrange("p (c m) -> p c m", c=cn)),
                 reads=[ms], writes=[ogt])
        for nh in range(2):
            s = SA[nh]
            for c in range(nch):
                p.op("pe", lambda e: e.matmul(s.ap[:, :], lhsT=ogt.ap[:, c, :], rhs=Wv[:, c, nh * 512:(nh + 1) * 512],
                                              start=(c == 0), stop=(c == nch - 1)), reads=[ogt, WB], writes=[s])
            p.op("dve", lambda e: e.tensor_tensor(out=x_.ap[:, nh * 512:(nh + 1) * 512], in0=s.ap[:, :],
                                                  in1=x_.ap[:, nh * 512:(nh + 1) * 512], op=ALU.add),
                 reads=[s, x_], writes=[x_])
        if final_g is not None:
            sq = ssq[j % 2]
            p.op("dve", lambda e: e.memset(sq.ap[:], 0.0), writes=[sq])
            p.op("act", lambda e: e.activation(out=junk.ap[:], in_=x_.ap[:], func=AF.Square, accum_out=sq.ap[:]),
                 reads=[x_], writes=[junk, sq])
            p.op("act", lambda e: e.activation(out=sq.ap[:], in_=sq.ap[:], func=AF.Sqrt, bias=epst.ap[:], scale=1.0 / DM),
                 reads=[sq, epst], writes=[sq])
            p.op("dve", lambda e: e.reciprocal(out=sq.ap[:], in_=sq.ap[:]), reads=[sq], writes=[sq])
            p.op("dve", lambda e: e.scalar_tensor_tensor(out=x_.ap[:], in0=x_.ap[:], scalar=sq.ap[:, 0:1],
                                                         in1=final_g.ap[:], op0=ALU.mult, op1=ALU.mult),
                 reads=[x_, sq, final_g], writes=[x_])
        p.dma("sp", ov[j], x_.ap[:], reads=[x_], writes=([d["_xo_trk"]] if "_xo_trk" in d else []), is_output=True)


P2_INPUTS = dict(
    ident=([128, 128], BF16), causal=([128, 128], BF16), band=([128, 128], BF16), dmask=([128, 8, 128], BF16),
    wm0=([128, 4, 128], BF16), cmask=([128, NQT, 2, 128], BF16), epn=([64, T], BF16),
    gates=([128, NQT, 24], F32), ovl=([128, 8, 256], BF16),
    w1k=([128, 32, 64], F32), w1v=([128, 32, 64], F32), w2k=([64, 128], F32), w2v=([64, 128], F32),
    pek=([128, 32], F32), pev=([128, 32], F32),
    nkcT=([128, T], BF16), nvcT=([128, T], BF16), ksT=([2, 64, T], BF16), vs=([128, 128, 2, 65], BF16),
    qn=([NQT, 2, 64, 512], BF16), ab=([NQT, 128, 2, 256], F32), kw=([NQT, 2, 64, 640], BF16),
    vw=([NQT, 128, 5, 2, 65], BF16), zs=([NQT, 128, 1280], F32),
    mkT=([8, 64, T], BF16), epm=([64, T], BF16), mv=([4, 128, 128, 2, 65], BF16), mq=([8, 64, 2048], BF16),
    mab=([4, 128, 4, 3, 64], F32),
    eqT=([2, 128, 2048], BF16), memx=([256, DM], F32), memg=([DM], F32), wkv=([DM, 512], F32),
    wout=([1280, DM], F32), x=([TPC, DM], F32),
)


def build_attn(stages=("nsa", "moba", "mem", "out"), debug_og=False):
    nc = bass.Bass("TRN2", target_bir_lowering=False)
    d = {}
    for k, (shp, dt) in P2_INPUTS.items():
        d[k] = nc.dram_tensor(k, shp, dt, kind="ExternalInput").ap()
    d["xo"] = nc.dram_tensor("xo", [TPC, DM], F32, kind="ExternalOutput").ap()
    if debug_og:
        d["og"] = nc.dram_tensor("og", [128, NQT, 1280], BF16, kind="ExternalOutput").ap()
    with ExitStack() as es:
        p = Prog(nc, es)
        p.const_tile(EPS)
        try:
            OG = emit_attn_phase(p, nc, d, stages)
            if debug_og:
                p.dma("sp", d["og"], OG.ap[:], reads=[OG], is_output=True)
        except StopEmit:
            p.es = p.es_perm
        p.finish()
    return nc


def _bf(a):
    return np.ascontiguousarray(a).astype(NPBF)


def core_tokens(c):
    return np.concatenate([np.arange((8 * j + c) * 128, (8 * j + c + 1) * 128) for j in range(NQT)])


def static_tables():
    k = np.arange(128)[:, None]
    q = np.arange(128)[None, :]
    tb = {}
    tb["ident"] = np.eye(128, dtype=NPBF)
    tb["causal"] = _bf(np.where(k <= q, 0.0, -BIG))
    tb["band"] = _bf(np.where(k > q, 0.0, -BIG))
    tb["epn"] = _bf((np.arange(64)[:, None] == ((np.arange(T)[None, :] // 64) % 64)).astype(np.float32))
    n = np.arange(1024)
    s = np.arange(256)
    cs = n * 16
    ov = ((cs[:, None] < s[None, :] * 64 + 64) & (cs[:, None] + 32 > s[None, :] * 64)).astype(np.float32)
    ov[1023] = 0
    tb["ovl"] = _bf(ov.reshape(8, 128, 256).transpose(1, 0, 2))
    tb["epm"] = _bf((np.arange(64)[:, None] == (np.arange(T)[None, :] // 256)).astype(np.float32))
    return tb


def core_tables(c):
    k = np.arange(128)[:, None]
    q = np.arange(128)[None, :]
    tb = {}
    dm = np.zeros((128, 8, 128), np.float32)
    for a in range(8):
        if a == c:
            dm[:, a, :] = np.where(k <= q, 0.0, -BIG)
        elif a > c:
            dm[:, a, :] = -BIG
    tb["dmask"] = _bf(dm)
    wm = np.zeros((128, 4, 128), np.float32)
    for w in range(4):
        if c - 4 + w < 0:
            wm[:, w, :] = -BIG
    tb["wm0"] = _bf(wm)
    cm = np.zeros((128, NQT, 2, 128), np.float32)
    ab = np.zeros((NQT, 128, 2, 256), np.float32)
    mab = np.zeros((4, 128, 4, 3, 64), np.float32)
    s = np.arange(256)[None, :]
    nb = np.arange(64)
    for j in range(NQT):
        qt = 8 * j + c
        ntl = j // 2
        for wsel in range(2):
            nt = ntl - 1 + wsel
            if nt < 0:
                continue
            n = nt * 128 + k
            t = qt * 128 + q
            cm[:, j, wsel, :] = np.where((16 * n + 31 <= t) & (n < 1023), 0.0, -BIG)
        t = (qt * 128 + np.arange(128))[:, None]
        cur = t // 64
        forced = (s == cur) | (s == cur - 1)
        valid = s <= cur
        ab[j, :, 0, :] = np.where(valid & ~forced, 1.0, 0.0)
        b = np.zeros((128, 256), np.float32)
        b = np.where(s == cur, 1e9, b)
        b = np.where(s == cur - 1, 2e9, b)
        b = np.where(~valid, -1e30, b)
        ab[j, :, 1, :] = b
        curm = qt // 2
        mab[j // 4, :, j % 4, 0, :] = (nb < curm).astype(np.float32)[None]
        mab[j // 4, :, j % 4, 1, :] = np.where(nb < curm, 0.0, -1e30)[None]
        mab[j // 4, :, j % 4, 2, :] = (nb == curm).astype(np.float32)[None]
    tb["cmask"] = _bf(cm)
    tb["ab"] = ab
    tb["mab"] = mab
    return tb


def with_ones(a):
    return np.concatenate([a, np.ones(a.shape[:-1] + (1,), a.dtype)], -1)


def prep_attn_shared(PB, inputs):
    sh = {}
    sh["nkcT"] = np.ascontiguousarray(PB[:, PB0["nkc"]:PB0["nkc"] + 128].T)
    sh["nvcT"] = np.ascontiguousarray(PB[:, PB0["nvc"]:PB0["nvc"] + 128].T)
    sh["ksT"] = np.ascontiguousarray(PB[:, PB0["nks"]:PB0["nks"] + 128].T).reshape(2, 64, T)
    v = PB[:, PB0["nvs"]:PB0["nvs"] + 128].reshape(128, 128, 2, 64).transpose(1, 0, 2, 3)
    sh["vs"] = np.ascontiguousarray(with_ones(v))
    sh["mkT"] = np.ascontiguousarray(PB[:, PB0["mk"]:PB0["mk"] + 512].reshape(T, 8, 64).transpose(1, 2, 0))
    v = PB[:, PB0["mv"]:PB0["mv"] + 512].reshape(128, 128, 4, 2, 64).transpose(2, 1, 0, 3, 4)
    sh["mv"] = np.ascontiguousarray(with_ones(v))
    for kind in ("k", "v"):
        w1 = inputs[f"l0_cmp_w1_{kind}"].transpose(1, 0, 2)
        sh["w1" + kind] = np.ascontiguousarray(np.concatenate([w1, w1], 0))
        w2 = inputs[f"l0_cmp_w2_{kind}"]
        sh["w2" + kind] = np.ascontiguousarray(np.concatenate([w2, w2], 1))
        pe = inputs[f"l0_cmp_pe_{kind}"].T
        sh["pe" + kind] = np.ascontiguousarray(np.concatenate([pe, pe], 0))
    sh["memx"] = np.ascontiguousarray(inputs["mem"][0])
    sh["memg"] = inputs["mem_norm_g"]
    sh["wkv"] = inputs["l0_w_mem_kv"]
    sh["wout"] = inputs["l0_w_out"]
    sh["kw_full"] = PB[:, PB0["nkw"]:PB0["nkw"] + 128]
    sh["vw_full"] = PB[:, PB0["nvw"]:PB0["nvw"] + 128]
    return sh


def prep_attn_core(c, PB, PF, x, sh, st):
    m = {}
    for k_ in ("nkcT", "nvcT", "ksT", "vs", "mkT", "mv", "w1k", "w1v", "w2k", "w2v", "pek", "pev",
               "memx", "memg", "wkv", "wout"):
        m[k_] = sh[k_]
    for k_ in ("ident", "causal", "band", "epn", "ovl", "epm"):
        m[k_] = st[k_]
    m.update(core_tables(c))
    tq = core_tokens(c)
    pbq = PB[tq]
    pfq = PF[tq]
    a = pbq[:, PB0["nq"]:PB0["nq"] + 512].reshape(NQT, 128, 2, 4, 64).transpose(0, 2, 4, 3, 1)
    m["qn"] = np.ascontiguousarray(a).reshape(NQT, 2, 64, 512)
    kw = np.zeros((NQT, 2, 64, 640), NPBF)
    vw = np.zeros((NQT, 128, 5, 2, 64), NPBF)
    for j in range(NQT):
        qt = 8 * j + c
        for w in range(5):
            kt = qt - 4 + w
            if kt < 0:
                continue
            kw[j, :, :, w * 128:(w + 1) * 128] = sh["kw_full"][kt * 128:(kt + 1) * 128].T.reshape(2, 64, 128)
            vw[j, :, w] = sh["vw_full"][kt * 128:(kt + 1) * 128].reshape(128, 2, 64)
    m["kw"] = kw
    m["vw"] = np.ascontiguousarray(with_ones(vw))
    m["gates"] = np.ascontiguousarray(pfq[:, 0:24].reshape(NQT, 128, 24).transpose(1, 0, 2))
    m["zs"] = np.ascontiguousarray(pfq[:, 24:1304].reshape(NQT, 128, 1280))
    m["mq"] = np.ascontiguousarray(pbq[:, PB0["mq"]:PB0["mq"] + 512].reshape(2048, 8, 64).transpose(1, 2, 0))
    m["eqT"] = np.ascontiguousarray(pbq[:, PB0["eq"]:PB0["eq"] + 256].reshape(2048, 2, 128).transpose(1, 2, 0))
    m["x"] = np.ascontiguousarray(x[tq])
    return m


NPRE = 112
RET_G = [1.0 - 2.0 ** (-5.0 - h) for h in range(4)]


def emit_ret_phase(p, nc, d):
    ident = p.sbuf("ident", [128, 128], BF16)
    p.dma("sp", ident.ap[:], d["ident"][:, :], writes=[ident])
    B = [p.psum(f"B{i}", [128, 512]) for i in range(8)]
    OG = p.sbuf("OG", [128, NQT, 2304], BF16)
    WB = p.sbuf("WB", [128, 18 * DM], BF16)
    QT = p.sbuf("QT", [128, 2048], BF16)
    Sf = p.sbuf("Sf", [128, 4, 2, 512], F32)
    Sb = p.sbuf("Sb", [128, 4, 2, 512], BF16)
    dt = p.sbuf("dt", [128, 4, 128], F32)
    kdec = p.sbuf("kdec", [128, 4], F32)
    qdec = p.sbuf("qdec", [128, 4, 128], F32)
    for nm, t in (("dt", dt), ("kdec", kdec), ("qdec", qdec)):
        p.dma("sp", t.ap[:], d[nm], writes=[t])
    epst = p.const_tile(EPS)
    with p.scope():
        scp = p.sbuf("scp", [128, NPRE, 4], F32)
        p.dma("sp", scp.ap[:], d["scp"], writes=[scp])
        kpt = [p.sbuf(f"kpt{i}", [128, 4, 256], BF16) for i in range(4)]
        vpt = [p.sbuf(f"vpt{i}", [128, 2048], BF16) for i in range(4)]
        kps = [p.sbuf(f"kps{i}", [128, 4, 256], BF16) for i in range(4)]
        for j in range(NPRE):
            k_, v_, ks_ = kpt[j % 4], vpt[j % 4], kps[j % 4]
            p.dma("sp", k_.ap[:], d["kp"][j].rearrange("p (h f) -> p h f", h=4), writes=[k_])
            p.dma("act", v_.ap[:], d["vp"][j], writes=[v_])
            p.op("dve", lambda e: e.tensor_tensor(out=ks_.ap[:], in0=k_.ap[:],
                                                  in1=scp.ap[:, j, :].unsqueeze(2).to_broadcast([128, 4, 256]),
                                                  op=ALU.mult), reads=[k_, scp], writes=[ks_])
            for h in range(4):
                for dc in range(2):
                    bk = B[h * 2 + dc]
                    p.op("pe", lambda e: e.matmul(bk.ap[:, :], lhsT=ks_.ap[:, h, dc * 128:(dc + 1) * 128],
                                                  rhs=v_.ap[:, h * 512:(h + 1) * 512], start=(j == 0),
                                                  stop=(j == NPRE - 1)), reads=[ks_, v_], writes=[bk])
        for h in range(4):
            for dc in range(2):
                bk = B[h * 2 + dc]
                p.op("act", lambda e: e.copy(out=Sf.ap[:, h, dc, :], in_=bk.ap[:, :]), reads=[bk], writes=[Sf])
        p.op("dve", lambda e: e.tensor_copy(out=Sb.ap[:], in_=Sf.ap[:]), reads=[Sf], writes=[Sb])
    with p.scope():
        qTt = [p.sbuf(f"qTt{i}", [128, 4, 2, 128], BF16) for i in range(2)]
        kTt = [p.sbuf(f"kTt{i}", [128, 4, 2, 128], BF16) for i in range(2)]
        ktt = [p.sbuf(f"ktt{i}", [128, 4, 256], BF16) for i in range(2)]
        vtt = [p.sbuf(f"vtt{i}", [128, 2048], BF16) for i in range(2)]
        zt = [p.sbuf(f"rzt{i}", [128, 2048], F32) for i in range(2)]
        Ab = [p.sbuf(f"Ab{i}", [128, 128], BF16) for i in range(2)]
        qs = [p.sbuf(f"qs{i}", [128, 2, 128], BF16) for i in range(2)]
        ks2 = [p.sbuf(f"ks2{i}", [128, 256], BF16) for i in range(2)]
        junk = p.sbuf("rjunk", [128, 512], F32)
        tmpn = p.sbuf("tmpn", [128, 512], F32)
        st = [p.sbuf(f"rst{i}", [128, 4], F32) for i in range(2)]
        AT = B[2]
        Ot = [B[3], B[4]]
        SU = [B[5], B[6]]
        k = 0
        for n in range(NQT):
            q_, kT_, kt_, v_, z_ = qTt[n % 2], kTt[n % 2], ktt[n % 2], vtt[n % 2], zt[n % 2]
            p.dma("sp", q_.ap[:], d["qT"][n], writes=[q_])
            p.dma("sp", kT_.ap[:], d["kT"][n], writes=[kT_])
            p.dma("pool", kt_.ap[:], d["kt"][n].rearrange("p (h f) -> p h f", h=4), writes=[kt_])
            p.dma("pool", v_.ap[:], d["v"][n], writes=[v_])
            p.dma("sp", z_.ap[:], d["zs"][n][:, 0:2048], writes=[z_])
            for h in range(4):
                ab_, qs_, ks_, s_ = Ab[k % 2], qs[k % 2], ks2[k % 2], st[k % 2]
                O = Ot[k % 2]
                k += 1
                for dc in range(2):
                    p.op("pe", lambda e: e.matmul(AT.ap[:, 0:128], lhsT=kT_.ap[:, h, dc, :], rhs=q_.ap[:, h, dc, :],
                                                  start=(dc == 0), stop=(dc == 1)), reads=[kT_, q_], writes=[AT])
                p.op("dve", lambda e: e.tensor_tensor(out=ab_.ap[:], in0=AT.ap[:, 0:128], in1=dt.ap[:, h, :],
                                                      op=ALU.mult), reads=[AT, dt], writes=[ab_])
                p.op("pool", lambda e: e.tensor_tensor(out=qs_.ap[:], in0=q_.ap[:, h, :, :],
                                                       in1=qdec.ap[:, h, :].unsqueeze(1).to_broadcast([128, 2, 128]),
                                                       op=ALU.mult), reads=[q_, qdec], writes=[qs_])
                p.op("pe", lambda e: e.matmul(O.ap[:, :], lhsT=ab_.ap[:, :], rhs=v_.ap[:, h * 512:(h + 1) * 512],
                                              start=True, stop=False), reads=[ab_, v_], writes=[O])
                for dc in range(2):
                    p.op("pe", lambda e: e.matmul(O.ap[:, :], lhsT=qs_.ap[:, dc, :], rhs=Sb.ap[:, h, dc, :],
                                                  start=False, stop=(dc == 1)), reads=[qs_, Sb], writes=[O])
                p.op("dve", lambda e: e.tensor_scalar(out=ks_.ap[:], in0=kt_.ap[:, h, :], scalar1=kdec.ap[:, h:h + 1],
                                                      scalar2=None, op0=ALU.mult), reads=[kt_, kdec], writes=[ks_])
                cd = float(RET_G[h] ** 128)
                for dc in range(2):
                    su = SU[dc]
                    p.op("pe", lambda e: e.matmul(su.ap[:, :], lhsT=ks_.ap[:, dc * 128:(dc + 1) * 128],
                                                  rhs=v_.ap[:, h * 512:(h + 1) * 512], start=True, stop=True),
                         reads=[ks_, v_], writes=[su])
                    p.op("dve", lambda e: e.scalar_tensor_tensor(out=Sf.ap[:, h, dc, :], in0=Sf.ap[:, h, dc, :],
                                                                 scalar=cd, in1=su.ap[:, :], op0=ALU.mult, op1=ALU.add),
                         reads=[Sf, su], writes=[Sf])
                p.op("act", lambda e: e.copy(out=Sb.ap[:, h, :, :], in_=Sf.ap[:, h, :, :]), reads=[Sf], writes=[Sb])
                p.op("dve", lambda e: e.memset(s_.ap[:], 0.0), writes=[s_])
                p.op("act", lambda e: e.activation(out=junk.ap[:], in_=O.ap[:], func=AF.Identity,
                                                   accum_out=s_.ap[:, 0:1]), reads=[O], writes=[junk, s_])
                p.op("act", lambda e: e.activation(out=junk.ap[:], in_=O.ap[:], func=AF.Square,
                                                   accum_out=s_.ap[:, 1:2]), reads=[O], writes=[junk, s_])
                p.op("dve", lambda e: e.tensor_scalar(out=s_.ap[:, 0:1], in0=s_.ap[:, 0:1], scalar1=1.0 / 512,
                                                      scalar2=None, op0=ALU.mult), reads=[s_], writes=[s_])
                p.op("dve", lambda e: e.tensor_tensor(out=s_.ap[:, 2:3], in0=s_.ap[:, 0:1], in1=s_.ap[:, 0:1],
                                                      op=ALU.mult), reads=[s_], writes=[s_])
                p.op("dve", lambda e: e.scalar_tensor_tensor(out=s_.ap[:, 3:4], in0=s_.ap[:, 1:2], scalar=1.0 / 512,
                                                             in1=s_.ap[:, 2:3], op0=ALU.mult, op1=ALU.subtract),
                     reads=[s_], writes=[s_])
                p.op("act", lambda e: e.activation(out=s_.ap[:, 3:4], in_=s_.ap[:, 3:4], func=AF.Sqrt,
                                                   bias=epst.ap[:], scale=1.0), reads=[s_, epst], writes=[s_])
                p.op("dve", lambda e: e.reciprocal(out=s_.ap[:, 3:4], in_=s_.ap[:, 3:4]), reads=[s_], writes=[s_])
                p.op("dve", lambda e: e.tensor_scalar(out=tmpn.ap[:], in0=O.ap[:], scalar1=s_.ap[:, 0:1],
                                                      scalar2=s_.ap[:, 3:4], op0=ALU.subtract, op1=ALU.mult),
                     reads=[O, s_], writes=[tmpn])
                p.op("pool", lambda e: e.tensor_tensor(out=OG.ap[:, n, h * 512:(h + 1) * 512], in0=tmpn.ap[:],
                                                       in1=z_.ap[:, h * 512:(h + 1) * 512], op=ALU.mult),
                     reads=[tmpn, z_], writes=[OG])
    SA = [B[0], B[1]]
    OA = [B[2], B[3]]
    MISC = [B[4], B[5]]
    PT = [p.sbuf(f"PT{i}", [128, 512], BF16) for i in range(3)]
    ztm = [p.sbuf(f"zt{i}", [128, 512], F32) for i in range(2)]
    rin = [p.sbuf(f"rin{i}", [128, 4], F32) for i in range(4)]
    identf = p.sbuf("identf", [128, 128], F32)
    p.op("dve", lambda e: e.tensor_copy(out=identf.ap[:], in_=ident.ap[:]), reads=[ident], writes=[identf])
    otsb = [p.sbuf(f"otsb{i}", [65, 512], F32) for i in range(2)]
    cnt = {}

    def nxt(key, lst):
        t = lst[cnt.get(key, 0) % len(lst)]
        cnt[key] = cnt.get(key, 0) + 1
        return t
    tools = dict(identf=identf, otsb=otsb, MISC=MISC, rin=rin, nxt=nxt)
    kmT = p.sbuf("kmT", [128, 2, 256], BF16)
    vm_aug = p.sbuf("vm_aug", [128, 2, 4, 65], BF16)
    fg = p.sbuf("fg", [128, DM], F32)
    p.dma("sp", fg.ap[:], d["fg"].partition_broadcast(128), writes=[fg])
    emit_mem_kv(p, d, ident, MISC, kmT, vm_aug)
    emit_mem_attn(p, d, kmT, vm_aug, QT, SA, OA, PT, ztm, tools, nxt, OG, 2048, 2048)
    emit_outproj(p, d, ident, OG, 2304, MISC, SA, WB, fg)


P4_INPUTS = dict(
    ident=([128, 128], BF16), dt=([128, 4, 128], F32), kdec=([128, 4], F32), qdec=([128, 4, 128], F32),
    scp=([128, NPRE, 4], F32), kp=([NPRE, 128, 1024], BF16), vp=([NPRE, 128, 2048], BF16),
    qT=([NQT, 128, 4, 2, 128], BF16), kT=([NQT, 128, 4, 2, 128], BF16), kt=([NQT, 128, 1024], BF16),
    v=([NQT, 128, 2048], BF16), zs=([NQT, 128, 2304], F32), eqT=([2, 128, 2048], BF16),
    memx=([256, DM], F32), memg=([DM], F32), wkv=([DM, 512], F32), wout=([2304, DM], F32),
    x=([TPC, DM], F32), fg=([DM], F32),
)


def build_ret():
    nc = bass.Bass("TRN2", target_bir_lowering=False)
    d = {}
    for k, (shp, dt_) in P4_INPUTS.items():
        d[k] = nc.dram_tensor(k, shp, dt_, kind="ExternalInput").ap()
    d["xo"] = nc.dram_tensor("xo", [TPC, DM], F32, kind="ExternalOutput").ap()
    with ExitStack() as es:
        p = Prog(nc, es)
        p.const_tile(EPS)
        emit_ret_phase(p, nc, d)
        p.finish()
    return nc


def build_attn_proj():
    nc = bass.Bass("TRN2", target_bir_lowering=False)
    d = {}
    for k, (shp, dt_) in P2_INPUTS.items():
        d[k] = nc.dram_tensor(k, shp, dt_, kind="ExternalInput").ap()
    d["xo"] = nc.dram_tensor("xo", [TPC, DM], F32, kind="ExternalOutput").ap()
    d["pos1"] = nc.dram_tensor("pos1", [TPC], I32, kind="ExternalInput").ap()
    d["g1"] = nc.dram_tensor("g1", [DM], F32, kind="ExternalInput").ap()
    d["w1"] = nc.dram_tensor("w1", [DM, CFG1["nin"]], F32, kind="ExternalInput").ap()
    d["invf1"] = nc.dram_tensor("invf1", [128, CFG1["nfreq"]], F32, kind="ExternalInput").ap()
    d["pb1"] = nc.dram_tensor("pb1", [TPC, CFG1["nb"]], BF16, kind="ExternalOutput").ap()
    d["pf1"] = nc.dram_tensor("pf1", [TPC, CFG1["nf"]], F32, kind="ExternalOutput").ap()
    with ExitStack() as es:
        p = Prog(nc, es)
        p.const_tile(EPS)
        xo_trk = p.trk(name="xo_dram")
        d["_xo_trk"] = xo_trk
        with p.scope():
            emit_attn_phase(p, nc, d)
        d2 = {"x": d["xo"], "pos": d["pos1"], "g": d["g1"], "w": d["w1"], "invf": d["invf1"], "ident": d["ident"],
              "pb": d["pb1"], "pf": d["pf1"], "_x_trk": xo_trk}
        emit_proj_phase(p, nc, d2, CFG1)
        p.finish()
    return nc


def ret_tables(c):
    g = np.array(RET_G, np.float64)
    i = np.arange(128, dtype=np.float64)
    tb = {}
    diff = i[None, :] - i[:, None]
    dtab = np.where(diff[:, None, :] >= 0, g[None, :, None] ** np.maximum(diff[:, None, :], 0.0), 0.0) / 16.0
    tb["dt"] = dtab.astype(np.float32)
    kd = g[None, :] ** (127.0 - i[:, None])
    tb["kdec"] = (kd / 16.0).astype(np.float32)
    qd = g[:, None] ** (i[None, :] + 1.0)
    tb["qdec"] = np.ascontiguousarray(np.broadcast_to(qd[None], (128, 4, 128))).astype(np.float32)
    scp = np.zeros((128, NPRE, 4), np.float64)
    J = 16 * c
    for j in range(min(J, NPRE)):
        scp[:, j, :] = kd / 16.0 * (g[None, :] ** (128.0 * (J - 1 - j)))
    tb["scp"] = scp.astype(np.float32)
    return tb


def prep_ret_shared(PB, inputs):
    sh = {}
    sh["kp"] = np.ascontiguousarray(PB[:NPRE * 128, PB1["rk"]:PB1["rk"] + 1024].reshape(NPRE, 128, 1024))
    sh["vp"] = np.ascontiguousarray(PB[:NPRE * 128, PB1["rv"]:PB1["rv"] + 2048].reshape(NPRE, 128, 2048))
    sh["memx"] = np.ascontiguousarray(inputs["mem"][0])
    sh["memg"] = inputs["mem_norm_g"]
    sh["wkv"] = inputs["l1_w_mem_kv"]
    sh["wout"] = inputs["l1_w_out"]
    sh["fg"] = inputs["final_norm_g"]
    sh["ident"] = np.eye(128, dtype=NPBF)
    return sh


def prep_ret_core(c, PB, PF, x1, sh):
    m = dict(sh)
    m.update(ret_tables(c))
    t0 = c * TPC
    pb = PB[t0:t0 + TPC]
    pf = PF[t0:t0 + TPC]
    for nm, off in (("qT", PB1["rq"]), ("kT", PB1["rk"])):
        a = pb[:, off:off + 1024].reshape(NQT, 128, 4, 2, 128).transpose(0, 4, 2, 3, 1)
        m[nm] = np.ascontiguousarray(a)
    m["kt"] = np.ascontiguousarray(pb[:, PB1["rk"]:PB1["rk"] + 1024].reshape(NQT, 128, 1024))
    m["v"] = np.ascontiguousarray(pb[:, PB1["rv"]:PB1["rv"] + 2048].reshape(NQT, 128, 2048))
    m["zs"] = np.ascontiguousarray(pf.reshape(NQT, 128, 2304))
    m["eqT"] = np.ascontiguousarray(pb[:, PB1["eq"]:PB1["eq"] + 256].reshape(TPC, 2, 128).transpose(1, 2, 0))
    m["x"] = np.ascontiguousarray(x1[t0:t0 + TPC])
    return m


def _proj_maps(cfg, x, pos, g, w, invf, gb=None):
    maps = []
    for c in range(NCORE):
        m = {"x": np.ascontiguousarray(x[c * TPC:(c + 1) * TPC]), "pos": np.ascontiguousarray(pos[c * TPC:(c + 1) * TPC]),
             "g": g, "w": w, "invf": np.ascontiguousarray(np.broadcast_to(invf[None], (128, cfg["nfreq"]))),
             "ident": np.eye(128, dtype=NPBF)}
        if gb is not None:
            m["gb"] = gb
        maps.append(m)
    return maps


_NC_CACHE = {}


def _get(name, fn):
    if name not in _NC_CACHE:
        _NC_CACHE[name] = fn()
    return _NC_CACHE[name]


def kernel(**inputs):
    inputs = {k: np.asarray(v) for k, v in inputs.items()}
    x = np.ascontiguousarray(inputs["x"][0], dtype=np.float32)
    pos = np.ascontiguousarray(inputs["positions"][0]).astype(np.int32)
    cores = list(range(NCORE))
    invf0 = (1.0 / (10000.0 ** (np.arange(0, 64, 2, dtype=np.float32) / np.float32(64)))).astype(np.float32)
    invf1 = (1.0 / (10000.0 ** np.linspace(0.0, 1.0, 128, dtype=np.float32))).astype(np.float32)
    nc1 = _get("p1", lambda: build_proj(CFG0))
    r1 = run_bass_kernel_spmd(nc1, _proj_maps(CFG0, x, pos, inputs["l0_norm_g"], inputs["l0_w_in"], invf0,
                                              inputs["l0_nsa_gate_b"]), core_ids=cores).results
    PB = np.concatenate([r["pb"] for r in r1], 0)
    PF = np.concatenate([r["pf"] for r in r1], 0)
    nc2 = _get("p23", lambda: build_attn_proj())
    sh = prep_attn_shared(PB, inputs)
    st = static_tables()
    maps2 = []
    for c in cores:
        m = prep_attn_core(c, PB, PF, x, sh, st)
        m["pos1"] = np.ascontiguousarray(pos[core_tokens(c)])
        m["g1"] = inputs["l1_norm_g"]
        m["w1"] = inputs["l1_w_in"]
        m["invf1"] = np.ascontiguousarray(np.broadcast_to(invf1[None], (128, 128)))
        maps2.append(m)
    r2 = run_bass_kernel_spmd(nc2, maps2, core_ids=cores).results
    x1 = np.empty((T, DM), np.float32)
    PB_1 = np.empty((T, CFG1["nb"]), NPBF)
    PF_1 = np.empty((T, CFG1["nf"]), np.float32)
    for c in cores:
        tq = core_tokens(c)
        x1[tq] = r2[c]["xo"]
        PB_1[tq] = r2[c]["pb1"]
        PF_1[tq] = r2[c]["pf1"]
    nc4 = _get("p4", lambda: build_ret())
    sh4 = prep_ret_shared(PB_1, inputs)
    r4 = run_bass_kernel_spmd(nc4, [prep_ret_core(c, PB_1, PF_1, x1, sh4) for c in cores], core_ids=cores).results
    out = np.concatenate([r["xo"] for r in r4], 0)
    return out[None].astype(np.float32)
```
